# Optimizing a Trainium2 kernel written in Bass

```python
import math
import jax
import jax.numpy as jnp
from jax import lax
import numpy as np

D_MODEL = 2048
BATCH = 2
SEQ = 8192
DEPTH = 2

DEEPNORM_ALPHA = (2.0 * DEPTH) ** 0.25
DEEPNORM_BETA = (8.0 * DEPTH) ** -0.25
LN_EPS = 1e-5

SSD_D_INNER = D_MODEL
SSD_HEAD_DIM = 64
SSD_N_HEADS = SSD_D_INNER // SSD_HEAD_DIM
SSD_N_GROUPS = 4
SSD_D_STATE = 128
SSD_CONV = 4
SSD_CHUNK = 64
SSD_GN = SSD_N_GROUPS * SSD_D_STATE
SSD_XBC = SSD_D_INNER + 2 * SSD_GN
SSD_EPS = 1e-5

RWKV_D = D_MODEL
RWKV_HEAD_DIM = 64
RWKV_N_HEADS = RWKV_D // RWKV_HEAD_DIM
RWKV_DECAY_LORA = max(32, int(round(1.8 * RWKV_D ** 0.5 / 32)) * 32)
RWKV_A_LORA = max(32, int(round(1.8 * RWKV_D ** 0.5 / 32)) * 32)
RWKV_GATE_LORA = max(32, int(round(0.6 * RWKV_D ** 0.8 / 32)) * 32)
RWKV_COLS = 3 * RWKV_D + RWKV_DECAY_LORA + RWKV_A_LORA + RWKV_GATE_LORA
RWKV_SPLITS = (RWKV_D, 2 * RWKV_D, 3 * RWKV_D, 3 * RWKV_D + RWKV_DECAY_LORA,
               3 * RWKV_D + RWKV_DECAY_LORA + RWKV_A_LORA)
RWKV_LN_EPS = 64e-5

MLSTM_N_HEADS = 8
MLSTM_DV = D_MODEL
MLSTM_DQK = D_MODEL // 2
MLSTM_HEAD_DV = MLSTM_DV // MLSTM_N_HEADS
MLSTM_HEAD_DQK = MLSTM_DQK // MLSTM_N_HEADS
MLSTM_CONV = 4
MLSTM_CHUNK = 64
MLSTM_EPS = 1e-6

N_BRANCHES = 3
FFN_D = int(round(8 * D_MODEL / 3 / 128)) * 128
FFN_CONV = 3

IN_SIZES = (SSD_D_INNER, SSD_XBC, SSD_N_HEADS,
            RWKV_COLS,
            2 * MLSTM_DQK, MLSTM_DV, MLSTM_DV,
            MLSTM_N_HEADS, MLSTM_N_HEADS,
            N_BRANCHES * D_MODEL)
IN_SPLITS = tuple(sum(IN_SIZES[:i + 1]) for i in range(len(IN_SIZES) - 1))
D_IN = sum(IN_SIZES)

kernel_name = "hybrid_ssd_rwkv7_mlstm_deepnorm"


def layer_norm(x, g, b, eps=LN_EPS):
    xf = x.astype(jnp.float32)
    mu = jnp.mean(xf, axis=-1, keepdims=True)
    var = jnp.mean(jnp.square(xf - mu), axis=-1, keepdims=True)
    return ((xf - mu) * lax.rsqrt(var + eps) * g + b).astype(x.dtype)


def group_rms_norm(y, groups, eps):
    shp = y.shape
    yg = y.reshape(shp[:-1] + (groups, shp[-1] // groups))
    yg = yg * lax.rsqrt(jnp.mean(jnp.square(yg), axis=-1, keepdims=True) + eps)
    return yg.reshape(shp)


def head_layer_norm(y, eps):
    mu = jnp.mean(y, axis=-1, keepdims=True)
    var = jnp.mean(jnp.square(y - mu), axis=-1, keepdims=True)
    return (y - mu) * lax.rsqrt(var + eps)


def causal_dwconv(x, w, b):
    width, ch = w.shape
    y = lax.conv_general_dilated(x, w[:, None, :], window_strides=(1,),
                                 padding=((width - 1, 0),),
                                 dimension_numbers=("NWC", "WIO", "NWC"),
                                 feature_group_count=ch)
    return y + b


def token_shift(p):
    return jnp.pad(p, ((0, 0), (1, 0), (0, 0)))[:, :-1]


def ssd_chunked_scan(xdt, adt, bmat, cmat):
    bsz, seq = xdt.shape[:2]
    L = SSD_CHUNK
    nc = seq // L
    hg = SSD_N_HEADS // SSD_N_GROUPS
    x = xdt.reshape(bsz, nc, L, SSD_N_GROUPS, hg, SSD_HEAD_DIM)
    a_cs = jnp.cumsum(adt.reshape(bsz, nc, L, SSD_N_GROUPS, hg), axis=2)
    bm = bmat.reshape(bsz, nc, L, SSD_N_GROUPS, SSD_D_STATE)
    cm = cmat.reshape(bsz, nc, L, SSD_N_GROUPS, SSD_D_STATE)
    causal = jnp.tril(jnp.ones((L, L), dtype=bool))
    seg = a_cs[:, :, :, None] - a_cs[:, :, None, :]
    decay = jnp.exp(jnp.where(causal[:, :, None, None], seg, -jnp.inf))
    scores = jnp.einsum("bclgn,bcsgn->bclsg", cm, bm)
    y_diag = jnp.einsum("bclsgh,bcsghp->bclghp", scores[..., None] * decay, x)
    decay_to_end = jnp.exp(a_cs[:, :, -1:] - a_cs)
    chunk_states = jnp.einsum("bclgn,bclghp->bcghpn", bm, decay_to_end[..., None] * x)
    chunk_decay = jnp.exp(a_cs[:, :, -1])

    def step(state, inp):
        dec, new = inp
        return dec[..., None, None] * state + new, state

    init = jnp.zeros((bsz, SSD_N_GROUPS, hg, SSD_HEAD_DIM, SSD_D_STATE), xdt.dtype)
    _, prev = lax.scan(step, init, (jnp.moveaxis(chunk_decay, 1, 0), jnp.moveaxis(chunk_states, 1, 0)))
    prev = jnp.moveaxis(prev, 0, 1)
    y_off = jnp.einsum("bclgn,bcghpn->bclghp", cm, prev) * jnp.exp(a_cs)[..., None]
    return (y_diag + y_off).reshape(bsz, seq, SSD_N_HEADS, SSD_HEAD_DIM)


def ssd_mixer(z, xbc, dt_raw, conv_w, conv_b, dt_bias, a_log, d_skip, norm_w):
    bsz, seq, _ = z.shape
    xbc = jax.nn.silu(causal_dwconv(xbc, conv_w, conv_b)).astype(jnp.float32)
    xs, bs, cs = jnp.split(xbc, (SSD_D_INNER, SSD_D_INNER + SSD_GN), axis=-1)
    xh = xs.reshape(bsz, seq, SSD_N_HEADS, SSD_HEAD_DIM)
    dt = jax.nn.softplus(dt_raw.astype(jnp.float32) + dt_bias)
    a = -jnp.exp(a_log.astype(jnp.float32))
    y = ssd_chunked_scan(xh * dt[..., None], dt * a,
                         bs.reshape(bsz, seq, SSD_N_GROUPS, SSD_D_STATE),
                         cs.reshape(bsz, seq, SSD_N_GROUPS, SSD_D_STATE))
    y = (y + xh * d_skip[:, None]).reshape(bsz, seq, SSD_D_INNER)
    y = group_rms_norm(y * jax.nn.silu(z.astype(jnp.float32)), SSD_N_GROUPS, SSD_EPS) * norm_w
    return y.astype(z.dtype)


def rwkv7_recurrence(r, w, k, v, a, b):
    bsz, _, nh, dk = r.shape

    def step(s, inp):
        r_t, w_t, k_t, v_t, a_t, b_t = inp
        sa = jnp.einsum("bhij,bhj->bhi", s, a_t)
        s = s * w_t[:, :, None, :] + v_t[..., :, None] * k_t[..., None, :] + sa[..., :, None] * b_t[..., None, :]
        return s, jnp.einsum("bhij,bhj->bhi", s, r_t)

    xs = tuple(jnp.moveaxis(t, 1, 0) for t in (r, w, k, v, a, b))
    _, y = lax.scan(step, jnp.zeros((bsz, nh, dk, dk), r.dtype), xs)
    return jnp.moveaxis(y, 0, 1)


def rwkv7_mixer(p, mu, w0, w2, a0, a2, g2, k_k, k_a, r_k, ln_w, ln_b):
    out_dtype = p.dtype
    bsz, seq, _ = p.shape
    p = p.astype(jnp.float32)
    p = p + (token_shift(p) - p) * mu
    r, k, v, w_lo, a_lo, g_lo = jnp.split(p, RWKV_SPLITS, axis=-1)
    log_w = -jax.nn.softplus(-(w0 + jnp.tanh(w_lo) @ w2)) - 0.5
    decay = jnp.exp(-jnp.exp(log_w))
    a = jax.nn.sigmoid(a0 + a_lo @ a2)
    g = jax.nn.sigmoid(g_lo) @ g2

    def heads(t):
        return t.reshape(bsz, seq, RWKV_N_HEADS, RWKV_HEAD_DIM)

    kk = heads(k * k_k)
    kk = kk * lax.rsqrt(jnp.maximum(jnp.sum(kk * kk, axis=-1, keepdims=True), 1e-24))
    kh = heads(k * (1.0 + (a - 1.0) * k_a))
    rh, vh, ah = heads(r), heads(v), heads(a)
    y = rwkv7_recurrence(rh, heads(decay), kh, vh, -kk, kk * ah)
    y = head_layer_norm(y, RWKV_LN_EPS).reshape(bsz, seq, RWKV_D) * ln_w + ln_b
    y = y + (jnp.sum(rh * kh * r_k, axis=-1, keepdims=True) * vh).reshape(bsz, seq, RWKV_D)
    return (y * g).astype(out_dtype)


def mlstm_chunked(q, k, v, log_i, log_f):
    bsz, seq, nh, dk = q.shape
    dv = v.shape[-1]
    L = MLSTM_CHUNK
    nc = seq // L

    def chunks(t):
        return jnp.moveaxis(t.reshape((bsz, nc, L, nh) + t.shape[3:]), 1, 0).swapaxes(2, 3)

    causal = jnp.tril(jnp.ones((L, L), dtype=bool))

    def step(carry, inp):
        c_st, n_st, m_st = carry
        qc, kc, vc, lic, lfc = inp
        b_cum = jnp.cumsum(lfc, axis=-1)
        log_d = jnp.where(causal, b_cum[..., :, None] - b_cum[..., None, :] + lic[..., None, :], -jnp.inf)
        m_inter = b_cum + m_st[..., None]
        m_t = jnp.maximum(jnp.max(log_d, axis=-1), m_inter)
        w_intra = jnp.exp(log_d - m_t[..., None]) * jnp.einsum("bhld,bhsd->bhls", qc, kc)
        s_inter = jnp.exp(m_inter - m_t)
        num = jnp.einsum("bhls,bhsv->bhlv", w_intra, vc) + s_inter[..., None] * jnp.einsum("bhvd,bhld->bhlv", c_st, qc)
        den = jnp.sum(w_intra, axis=-1) + s_inter * jnp.einsum("bhd,bhld->bhl", n_st, qc)
        h = num / jnp.maximum(jnp.abs(den), jnp.exp(-m_t))[..., None]
        b_tot = b_cum[..., -1]
        log_w = b_tot[..., None] - b_cum + lic
        m_new = jnp.maximum(b_tot + m_st, jnp.max(log_w, axis=-1))
        w_end = jnp.exp(log_w - m_new[..., None])
        s_old = jnp.exp(b_tot + m_st - m_new)
        c_new = s_old[..., None, None] * c_st + jnp.einsum("bhsv,bhsd->bhvd", w_end[..., None] * vc, kc)
        n_new = s_old[..., None] * n_st + jnp.einsum("bhs,bhsd->bhd", w_end, kc)
        return (c_new, n_new, m_new), h

    init = (jnp.zeros((bsz, nh, dv, dk), q.dtype), jnp.zeros((bsz, nh, dk), q.dtype),
            jnp.zeros((bsz, nh), q.dtype))
    _, hs = lax.scan(step, init, (chunks(q), chunks(k), chunks(v), chunks(log_i), chunks(log_f)))
    return jnp.moveaxis(hs.swapaxes(2, 3), 0, 1).reshape(bsz, seq, nh, dv)


def mlstm_mixer(qk, v, o, i_pre, f_pre, conv_w, conv_b, i_bias, f_bias, norm_w):
    bsz, seq, _ = v.shape
    qk = jax.nn.silu(causal_dwconv(qk, conv_w, conv_b)).astype(jnp.float32)
    q, k = jnp.split(qk, 2, axis=-1)
    q = q.reshape(bsz, seq, MLSTM_N_HEADS, MLSTM_HEAD_DQK) * (MLSTM_HEAD_DQK ** -0.5)
    k = k.reshape(bsz, seq, MLSTM_N_HEADS, MLSTM_HEAD_DQK)
    vh = v.astype(jnp.float32).reshape(bsz, seq, MLSTM_N_HEADS, MLSTM_HEAD_DV)
    log_i = i_pre.astype(jnp.float32) + i_bias
    log_f = jax.nn.log_sigmoid(f_pre.astype(jnp.float32) + f_bias)
    h = mlstm_chunked(q, k, vh, log_i, log_f).reshape(bsz, seq, MLSTM_DV)
    h = group_rms_norm(h, MLSTM_N_HEADS, MLSTM_EPS) * norm_w
    return (jax.nn.sigmoid(o.astype(jnp.float32)) * h).astype(v.dtype)


def hybrid_mixer(h, w_in, ssd_conv_w, ssd_conv_b, ssd_dt_bias, ssd_a_log, ssd_d, ssd_norm_w,
                 rwkv_mu, rwkv_w0, rwkv_w2, rwkv_a0, rwkv_a2, rwkv_g2, rwkv_k_k, rwkv_k_a,
                 rwkv_r_k, rwkv_ln_w, rwkv_ln_b, mlstm_conv_w, mlstm_conv_b, mlstm_i_bias,
                 mlstm_f_bias, mlstm_norm_w, proj_ssd, proj_rwkv, proj_mlstm, w_out):
    proj = h @ w_in
    (ssd_z, ssd_xbc, ssd_dt, rwkv_p, ml_qk, ml_v, ml_o, ml_i, ml_f,
     gate_pre) = jnp.split(proj, IN_SPLITS, axis=-1)
    y_ssd = ssd_mixer(ssd_z, ssd_xbc, ssd_dt, ssd_conv_w, ssd_conv_b, ssd_dt_bias, ssd_a_log,
                      ssd_d, ssd_norm_w)
    y_rwkv = rwkv7_mixer(rwkv_p, rwkv_mu, rwkv_w0, rwkv_w2, rwkv_a0, rwkv_a2, rwkv_g2,
                         rwkv_k_k, rwkv_k_a, rwkv_r_k, rwkv_ln_w, rwkv_ln_b)
    y_mlstm = mlstm_mixer(ml_qk, ml_v, ml_o, ml_i, ml_f, mlstm_conv_w, mlstm_conv_b,
                          mlstm_i_bias, mlstm_f_bias, mlstm_norm_w)
    g_ssd, g_rwkv, g_mlstm = jnp.split(jax.nn.sigmoid(gate_pre), N_BRANCHES, axis=-1)
    merged = (g_ssd * (y_ssd @ proj_ssd) + g_rwkv * (y_rwkv @ proj_rwkv)
              + g_mlstm * (y_mlstm @ proj_mlstm))
    return merged @ w_out


def conv_ffn(h, w_up, conv_w, conv_b, w_down):
    hcat = causal_dwconv(h @ w_up, conv_w, conv_b)
    gate, up = jnp.split(hcat, 2, axis=-1)
    return (jax.nn.gelu(gate, approximate=False) * up) @ w_down


def setup_inputs(seed: int = 0) -> dict:
    key = jax.random.key(seed)
    ks = iter(jax.random.split(key, 48))

    def nrm(shape, scale):
        return jax.random.normal(next(ks), shape, jnp.float32) * scale

    def unif(shape, lo, hi):
        return jax.random.uniform(next(ks), shape, jnp.float32, minval=lo, maxval=hi)

    L = DEPTH
    dt = jnp.exp(unif((L, SSD_N_HEADS), math.log(1e-3), math.log(1e-1)))
    return {
        "x": nrm((BATCH, SEQ, D_MODEL), 1.0),
        "ln_in_g": 1.0 + nrm((D_MODEL,), 0.02),
        "ln_in_b": nrm((D_MODEL,), 0.02),
        "w_in": nrm((L, D_MODEL, D_IN), D_MODEL ** -0.5),
        "ssd_conv_w": nrm((L, SSD_CONV, SSD_XBC), SSD_CONV ** -0.5),
        "ssd_conv_b": nrm((L, SSD_XBC), 0.02),
        "ssd_dt_bias": dt + jnp.log(-jnp.expm1(-dt)),
        "ssd_a_log": jnp.log(unif((L, SSD_N_HEADS), 1.0, 16.0)),
        "ssd_d": 1.0 + nrm((L, SSD_N_HEADS), 0.1),
        "ssd_norm_w": 1.0 + nrm((L, SSD_D_INNER), 0.02),
        "rwkv_mu": unif((L, RWKV_COLS), 0.0, 1.0),
        "rwkv_w0": unif((L, RWKV_D), -6.0, 1.0),
        "rwkv_w2": nrm((L, RWKV_DECAY_LORA, RWKV_D), 0.5 * RWKV_DECAY_LORA ** -0.5),
        "rwkv_a0": nrm((L, RWKV_D), 0.1),
        "rwkv_a2": nrm((L, RWKV_A_LORA, RWKV_D), RWKV_A_LORA ** -0.5),
        "rwkv_g2": nrm((L, RWKV_GATE_LORA, RWKV_D), RWKV_GATE_LORA ** -0.5),
        "rwkv_k_k": 0.85 + nrm((L, RWKV_D), 0.05),
        "rwkv_k_a": 1.0 + nrm((L, RWKV_D), 0.05),
        "rwkv_r_k": nrm((L, RWKV_N_HEADS, RWKV_HEAD_DIM), 0.1),
        "rwkv_ln_w": 1.0 + nrm((L, RWKV_D), 0.02),
        "rwkv_ln_b": nrm((L, RWKV_D), 0.02),
        "mlstm_conv_w": nrm((L, MLSTM_CONV, 2 * MLSTM_DQK), MLSTM_CONV ** -0.5),
        "mlstm_conv_b": nrm((L, 2 * MLSTM_DQK), 0.02),
        "mlstm_i_bias": nrm((L, MLSTM_N_HEADS), 0.1),
        "mlstm_f_bias": unif((L, MLSTM_N_HEADS), 3.0, 6.0),
        "mlstm_norm_w": 1.0 + nrm((L, MLSTM_DV), 0.02),
        "proj_ssd": nrm((L, SSD_D_INNER, D_MODEL), SSD_D_INNER ** -0.5),
        "proj_rwkv": nrm((L, RWKV_D, D_MODEL), RWKV_D ** -0.5),
        "proj_mlstm": nrm((L, MLSTM_DV, D_MODEL), MLSTM_DV ** -0.5),
        "w_out": nrm((L, D_MODEL, D_MODEL), DEEPNORM_BETA * D_MODEL ** -0.5),
        "ln1_g": 1.0 + nrm((L, D_MODEL), 0.02),
        "ln1_b": nrm((L, D_MODEL), 0.02),
        "ffn_w_up": nrm((L, D_MODEL, 2 * FFN_D), D_MODEL ** -0.5),
        "ffn_conv_w": nrm((L, FFN_CONV, 2 * FFN_D), FFN_CONV ** -0.5),
        "ffn_conv_b": nrm((L, 2 * FFN_D), 0.02),
        "ffn_w_down": nrm((L, FFN_D, D_MODEL), DEEPNORM_BETA * FFN_D ** -0.5),
        "ln2_g": 1.0 + nrm((L, D_MODEL), 0.02),
        "ln2_b": nrm((L, D_MODEL), 0.02),
    }


def reference(x, ln_in_g, ln_in_b, w_in, ssd_conv_w, ssd_conv_b, ssd_dt_bias, ssd_a_log, ssd_d,
              ssd_norm_w, rwkv_mu, rwkv_w0, rwkv_w2, rwkv_a0, rwkv_a2, rwkv_g2, rwkv_k_k,
              rwkv_k_a, rwkv_r_k, rwkv_ln_w, rwkv_ln_b, mlstm_conv_w, mlstm_conv_b, mlstm_i_bias,
              mlstm_f_bias, mlstm_norm_w, proj_ssd, proj_rwkv, proj_mlstm, w_out, ln1_g, ln1_b,
              ffn_w_up, ffn_conv_w, ffn_conv_b, ffn_w_down, ln2_g, ln2_b):
    h = layer_norm(x, ln_in_g, ln_in_b)
    for l in range(DEPTH):
        mix = hybrid_mixer(h, w_in[l], ssd_conv_w[l], ssd_conv_b[l], ssd_dt_bias[l], ssd_a_log[l],
                           ssd_d[l], ssd_norm_w[l], rwkv_mu[l], rwkv_w0[l], rwkv_w2[l], rwkv_a0[l],
                           rwkv_a2[l], rwkv_g2[l], rwkv_k_k[l], rwkv_k_a[l], rwkv_r_k[l],
                           rwkv_ln_w[l], rwkv_ln_b[l], mlstm_conv_w[l], mlstm_conv_b[l],
                           mlstm_i_bias[l], mlstm_f_bias[l], mlstm_norm_w[l], proj_ssd[l],
                           proj_rwkv[l], proj_mlstm[l], w_out[l])
        h = layer_norm(DEEPNORM_ALPHA * h + mix, ln1_g[l], ln1_b[l])
        ffn = conv_ffn(h, ffn_w_up[l], ffn_conv_w[l], ffn_conv_b[l], ffn_w_down[l])
        h = layer_norm(DEEPNORM_ALPHA * h + ffn, ln2_g[l], ln2_b[l])
    return h
```

```python
import os
import numpy as np
import ml_dtypes
from concourse.bass_utils import run_bass_kernel_spmd
import concourse.bass as bass
import concourse.mybir as mybir

F32 = mybir.dt.float32
BF16 = mybir.dt.bfloat16
AF = mybir.ActivationFunctionType
ALU = mybir.AluOpType
AX = mybir.AxisListType

SAME_ENGINE_SYNC = True


class Prog:
    def __init__(self, nc):
        self.nc = nc
        self.eng = {"pe": nc.tensor, "dve": nc.vector, "act": nc.scalar,
                    "pool": nc.gpsimd, "sp": nc.sync}
        self.q = {e: [] for e in self.eng}
        self.cnt = {e: 0 for e in self.eng}
        self.waited = {e: {} for e in self.eng}
        self.acc = {}
        self.dma_cnt = {}
        self.sems = {}
        self.stack = []
        self.nops = 0
        self.prefix = ""

    def _enter(self, cm):
        v = cm.__enter__()
        self.stack.append(cm)
        return v

    def sb(self, name, shape, dt=F32):
        return self._enter(self.nc.sbuf_tensor("sb_" + self.prefix + name, list(shape), dt))

    def ps(self, name, shape, dt=F32):
        return self._enter(self.nc.psum_tensor("pp_" + name, list(shape), dt))

    def sem(self, key):
        if key not in self.sems:
            self.sems[key] = self._enter(self.nc.semaphore("s_" + str(key).replace(" ", "")))
        return self.sems[key]

    @staticmethod
    def region(ap):
        t = ap.tensor
        name = t.name
        space = str(ap.space)
        pairs = list(ap.ap)
        off = ap.offset
        if "DRAM" in space.upper() or "HBM" in space.upper():
            lo = off
            hi = off
            for st, n in pairs:
                if st >= 0:
                    hi += st * (n - 1)
                else:
                    lo += st * (n - 1)
            return (name, 0, 0, lo, hi)
        if "PSUM" in space.upper():
            return (name, 0, 127, 0, 1 << 30)
        pst, pn = pairs[0]
        p0 = ap.start_partition()
        p1 = p0 + (pn - 1 if pst != 0 else 0)
        if pst != 0:
            fo = off - p0 * pst
        else:
            fo = off
        lo = fo
        hi = fo
        for st, n in pairs[1:]:
            if st >= 0:
                hi += st * (n - 1)
            else:
                lo += st * (n - 1)
        return (name, p0, p1, lo, hi)

    def _deps(self, reads, writes):
        deps = {}

        def add(tok):
            s, v = tok
            if deps.get(s, 0) < v:
                deps[s] = v

        for ap in reads:
            name, p0, p1, lo, hi = self.region(ap)
            for r in self.acc.get(name, ()):
                if r[5] and not (r[2] < p0 or r[1] > p1 or r[4] < lo or r[3] > hi):
                    add(r[0])
        for ap in writes:
            name, p0, p1, lo, hi = self.region(ap)
            for r in self.acc.get(name, ()):
                if not (r[2] < p0 or r[1] > p1 or r[4] < lo or r[3] > hi):
                    add(r[0])
        return deps

    def _record(self, tok, reads, writes):
        for is_w, aps in ((False, reads), (True, writes)):
            for ap in aps:
                name, p0, p1, lo, hi = self.region(ap)
                lst = self.acc.setdefault(name, [])
                new = []
                for r in lst:
                    contained = (r[1] >= p0 and r[2] <= p1 and r[3] >= lo and r[4] <= hi)
                    if contained and (is_w or (r[0][0] == tok[0] and not r[5])):
                        continue
                    new.append(r)
                new.append((tok, p0, p1, lo, hi, is_w))
                self.acc[name] = new

    def _waits(self, e, deps):
        w = []
        for s, v in deps.items():
            if s == e and (e == "pe" or not SAME_ENGINE_SYNC):
                continue
            if self.waited[e].get(s, 0) < v:
                self.waited[e][s] = v
                w.append((s, v))
        return w

    def op(self, e, fn, reads, writes):
        pr_ = [a for a in reads if "PSUM" in str(a.space).upper()]
        if pr_:
            reads = [a for a in reads if "PSUM" not in str(a.space).upper()]
            writes = list(writes) + pr_
        deps = self._deps(reads, writes)
        w = self._waits(e, deps)
        self.cnt[e] += 1
        tok = (e, self.cnt[e])
        self.q[e].append((fn, w, (e, 1)))
        self._record(tok, reads, writes)
        self.nops += 1

    def dma(self, e, out, in_, key, **kw):
        deps = self._deps([in_], [out])
        w = self._waits(e, deps)
        k = ("dma", key)
        self.dma_cnt[k] = self.dma_cnt.get(k, 0) + 16
        tok = (k, self.dma_cnt[k])
        self.q[e].append((lambda en: en.dma_start(out=out, in_=in_, **kw), w, (k, 16)))
        self._record(tok, [in_], [out])
        self.nops += 1

    def barrier(self):
        for e in self.eng:
            w = []
            for f in self.eng:
                if f != e and self.cnt[f] > self.waited[e].get(f, 0):
                    self.waited[e][f] = self.cnt[f]
                    w.append((f, self.cnt[f]))
            for k, v in self.dma_cnt.items():
                if self.waited[e].get(k, 0) < v:
                    self.waited[e][k] = v
                    w.append((k, v))
            self.q[e].append((None, w, None))

    def mark(self):
        return len(self.stack)

    def release(self, m):
        self.barrier()
        while len(self.stack) > m:
            self.stack.pop().__exit__(None, None, None)

    def dma_dyn(self, e, out, in_fn, in_static, key):
        deps = self._deps([in_static], [out])
        w = self._waits(e, deps)
        k = ("dma", key)
        self.dma_cnt[k] = self.dma_cnt.get(k, 0) + 16
        tok = (k, self.dma_cnt[k])
        self.q[e].append((lambda en: en.dma_start(out=out, in_=in_fn(en)), w, (k, 16)))
        self._record(tok, [in_static], [out])
        self.nops += 1

    def cc(self, kind, src, dst, groups):
        deps = self._deps([src], [dst])
        w = self._waits("pool", deps)
        k = ("dma", "cc")
        self.dma_cnt[k] = self.dma_cnt.get(k, 0) + 1
        tok = (k, self.dma_cnt[k])
        self.q["pool"].append((lambda en: en.collective_compute(kind, ALU.bypass, replica_groups=groups,
                                                                ins=[src], outs=[dst]), w, (k, 1)))
        self._record(tok, [src], [dst])
        self.nops += 1

    def wait_all(self, e):
        w = []
        for k, v in self.dma_cnt.items():
            if self.waited[e].get(k, 0) < v:
                self.waited[e][k] = v
                w.append((k, v))
        self.q[e].append((None, w, None))

    def emit(self):
        nc = self.nc
        for e in self.eng:
            self.sem(e)
        for k in self.dma_cnt:
            self.sem(k)
        with nc.Block() as block:
            def mk(e):
                def body(en):
                    for fn, w, inc in self.q[e]:
                        for s, v in w:
                            en.wait_ge(self.sems[s], v)
                        if fn is not None:
                            ins = fn(en)
                            ins.then_inc(self.sems[inc[0]], inc[1])
                return body
            block.tensor(mk("pe"))
            block.vector(mk("dve"))
            block.scalar(mk("act"))
            block.gpsimd(mk("pool"))
            block.sync(mk("sp"))
        while self.stack:
            self.stack.pop().__exit__(None, None, None)

    def mm(self, out, lhsT, rhs, start=True, stop=True, **kw):
        self.op("pe", lambda en: en.matmul(out, lhsT, rhs, start=start, stop=stop, **kw),
                [lhsT, rhs], [out])

    def tr(self, out, in_, ident):
        self.op("pe", lambda en: en.transpose(out, in_, ident), [in_, ident], [out])

    def act(self, out, in_, func, bias=None, scale=None, accum_out=None, eng="act"):
        kw = {}
        rd = [in_]
        if bias is not None:
            kw["bias"] = bias
            if not isinstance(bias, (int, float)):
                rd.append(bias)
        if scale is not None:
            kw["scale"] = scale
            if not isinstance(scale, (int, float)):
                rd.append(scale)
        wr = [out]
        if accum_out is not None:
            kw["accum_out"] = accum_out
            wr.append(accum_out)
        self.op("act", lambda en: en.activation(out=out, in_=in_, func=func, **kw), rd, wr)

    def tt(self, out, in0, in1, op, eng="dve"):
        self.op(eng, lambda en: en.tensor_tensor(out=out, in0=in0, in1=in1, op=op), [in0, in1], [out])

    def ts(self, out, in0, s1, s2=None, op0=ALU.mult, op1=None, eng="dve", accum_out=None):
        rd = [in0]
        if not isinstance(s1, (int, float)):
            rd.append(s1)
        if s2 is not None and not isinstance(s2, (int, float)):
            rd.append(s2)
        kw = {}
        wr = [out]
        if op1 is not None:
            kw["op1"] = op1
        if accum_out is not None:
            kw["accum_out"] = accum_out
            wr.append(accum_out)
        self.op(eng, lambda en: en.tensor_scalar(out=out, in0=in0, scalar1=s1, scalar2=s2, op0=op0, **kw), rd, wr)

    def stt(self, out, in0, scalar, in1, op0, op1, accum_out=None):
        rd = [in0, in1]
        if not isinstance(scalar, (int, float)):
            rd.append(scalar)
        kw = {}
        wr = [out]
        if accum_out is not None:
            kw["accum_out"] = accum_out
            wr.append(accum_out)
        self.op("dve", lambda en: en.scalar_tensor_tensor(out=out, in0=in0, scalar=scalar, in1=in1, op0=op0, op1=op1, **kw), rd, wr)

    def copy(self, out, in_, eng="dve"):
        if eng == "act":
            self.op("act", lambda en: en.copy(out=out, in_=in_), [in_], [out])
        else:
            self.op(eng, lambda en: en.tensor_copy(out=out, in_=in_), [in_], [out])

    def memset(self, ap, val, eng="dve"):
        self.op(eng, lambda en: en.memset(ap, val), [], [ap])

    def recip(self, out, in_):
        self.op("dve", lambda en: en.reciprocal(out=out, in_=in_), [in_], [out])

    def reduce(self, out, in_, op=ALU.add, axis=AX.X):
        self.op("dve", lambda en: en.tensor_reduce(out=out, in_=in_, axis=axis, op=op), [in_], [out])


D = 2048
KC = 16
FFN = 5504
FC = 43
ALPHA = (2.0 * 2) ** 0.25
LN_EPS = 1e-5


def make_ident(p):
    ident = p.sb("ident", [128, 128])
    p.memset(ident[:], 1.0, eng="pool")
    p.op("pool", lambda en: en.affine_select(out=ident[:], in_=ident[:], pattern=[[-1, 128]],
                                              compare_op=ALU.is_equal, fill=0.0, base=0, channel_multiplier=1),
         [ident[:]], [ident[:]])
    return ident


def layer_norm_tile(p, out, xin, nb, gt, bt, st, mv, rstd, eps=LN_EPS):
    for c in range(4):
        p.op("dve", lambda en, c=c: en.bn_stats(out=st[0:nb, c, :], in_=xin[0:nb, c * 512:(c + 1) * 512]),
             [xin[0:nb, c * 512:(c + 1) * 512]], [st[0:nb, c, :]])
    p.op("dve", lambda en: en.bn_aggr(out=mv[0:nb, :], in_=st[0:nb].rearrange("p a b -> p (a b)")),
         [st[0:nb]], [mv[0:nb, :]])
    p.act(rstd[0:nb, :], mv[0:nb, 1:2], AF.Sqrt, bias=eps, scale=1.0)
    p.recip(rstd[0:nb, :], rstd[0:nb, :])
    p.ts(out[0:nb, :], xin[0:nb, :], mv[0:nb, 0:1], rstd[0:nb, 0:1], op0=ALU.subtract, op1=ALU.mult)
    p.tt(out[0:nb, :], out[0:nb, :], gt[0:nb, :], ALU.mult)
    p.tt(out[0:nb, :], out[0:nb, :], bt[0:nb, :], ALU.add)


def transpose_to_bf16(p, dstT, src, nb, ident, pst, tok0):
    for c4 in range(4):
        ps = pst[c4 % 2]
        for j in range(4):
            kc = c4 * 4 + j
            p.tr(ps[:, j * 128:j * 128 + nb], src[0:nb, kc * 128:(kc + 1) * 128], ident[0:nb, 0:nb])
        p.copy(dstT[:, c4 * 4:(c4 + 1) * 4, tok0:tok0 + nb],
               ps[:].rearrange("p (a b) -> p a b", a=4)[:, :, 0:nb], eng="act")


def emit_B(p, nc, ident, PS, TC, sfx, io, last, TB=512):
    T = TC

    def din(name, shape, dt=F32):
        return nc.dram_tensor(name + sfx, list(shape), dt, kind="ExternalInput").ap()

    hmask_d = io["hmask"]
    wg_d = din("wg", [D, 3 * D])
    pj_d = [din(n, [D, D]) for n in ("p_ssd", "p_rwkv", "p_ml")]
    wo_d = din("w_out", [D, D])
    ln_d = [din(n, [1, D]) for n in ("ln1_g", "ln1_b", "ln2_g", "ln2_b")]
    wup_d = din("w_up", [D, 2 * FFN])
    cw_d = din("cw", [128, 2 * FC * 3])
    cb_d = din("cb", [128, 2 * FC])
    wdn_d = din("w_down", [FFN, D])
    h_loc, hT_loc, htail_loc = io["h_loc"], io["hT_loc"], io["htail_loc"]
    hT_all, htail_all, y_all = io["hT_all"], io["htail_all"], io["y_all"]
    ho_d = io["out"] if last else h_loc
    NG = 4

    rank_cache = io.setdefault("rank_cache", {})

    def q_of(en):
        k = ("q", id(en))
        if k not in rank_cache:
            rank_cache[k] = en.partition_id() % NG
        return rank_cache[k]

    def rr_of(en):
        k = ("rr", id(en))
        if k not in rank_cache:
            rank_cache[k] = (en.partition_id() + (NG - 1)) % NG
        return rank_cache[k]

    lnt = [p.sb(f"lnt{i}", [128, D]) for i in range(2)]
    cw = p.sb("cw", [128, 2 * FC, 3])
    cb = p.sb("cb", [128, 2 * FC])
    hmask = p.sb("hmask", [128, 1])
    p.dma("sp", cw[:].rearrange("p c j -> p (c j)"), cw_d[:, :], "c4")
    p.dma("sp", cb[:], cb_d[:, :], "c5")
    p.dma("sp", hmask[:], hmask_d[:, :], "c6")
    tails = p.sb("tails", [128, 2 * FC, 2])

    st = p.sb("st", [128, 4, 6])
    mv = p.sb("mv", [128, 2])
    rstd = p.sb("rstd", [128, 1])

    WS = [p.sb(f"ws{i}", [128, KC, 256]) for i in range(2)]
    WB = [p.sb(f"wb{i}", [128, KC, 256], BF16) for i in range(2)]
    hT = p.sb("hTb", [128, KC, TB], BF16)
    h1T = hT
    ybig = p.sb("ybig", [128, 3 * KC * TB], BF16)
    yT = [ybig[:, i * KC * TB:(i + 1) * KC * TB].rearrange("p (k t) -> p k t", k=KC) for i in range(3)]
    actT = ybig[:, 0:FC * TB].rearrange("p (k t) -> p k t", k=FC)
    mT = p.sb("mT", [128, KC, TB], BF16)
    htm = [p.sb(f"htm{i}", [128, D]) for i in range(4)]
    G = p.sb("G", [128, 9 * TB])
    gst = [G[:, i * TB:(i + 1) * TB] for i in range(6)]
    gsb = [G[:, (6 + i) * TB:(7 + i) * TB] for i in range(3)]
    tmp = [G[:, 0:D], G[:, D:2 * D]]
    uext = [p.sb(f"uext{i}", [128, TB + 2]) for i in range(2)]
    wcnt = [0]

    def load_ln(i0):
        for i in range(2):
            p.dma("sp", lnt[i][:], ln_d[i0 + i][0:1, :].partition_broadcast(128), f"ln{i}")

    def load_panel(wd, col0, ncols, krows=None):
        s = wcnt[0] % 2
        wcnt[0] += 1
        wv = wd.rearrange("(k p) n -> p k n", p=128)
        p.dma("sp", WS[s][:, 0:8, 0:ncols], wv[:, 0:8, col0:col0 + ncols], f"ws{s}")
        p.dma("act", WS[s][:, 8:16, 0:ncols], wv[:, 8:16, col0:col0 + ncols], f"ws{s}b")
        eng = "act" if s == 0 else "dve"
        p.copy(WB[s][:, :, 0:ncols], WS[s][:, :, 0:ncols], eng=eng)
        return WB[s]

    blocks = [(0, 2)] + [(2 + b * TB, min(TB, T - b * TB)) for b in range((T + TB - 1) // TB)]
    for bi, (t0, nb) in enumerate(blocks):
        halo = bi == 0
        ntile = (nb + 127) // 128
        lt0 = t0 - 2
        HC = io["HC"]; NCHL = TC // HC; TT = io["TT"]; NPR = TC // TT
        if halo:
            p.dma_dyn("act", hT[:, :, 0:2],
                      lambda en: hT_all[bass.DynSlice(rr_of(en) * D + (NCHL - 1) * NG * D, D), HC - 2:HC].rearrange("(k p) t -> p k t", p=128),
                      hT_all, "hT")
            for i in range(3):
                p.dma_dyn("sp", yT[i][:, :, 0:2],
                          lambda en, i=i: y_all[i][bass.DynSlice((rr_of(en) * NPR + NPR - 1) * D, D), TT - 2:TT].rearrange("(k p) t -> p k t", p=128),
                          y_all[i], f"yl{i}")
        else:
            for o_ in range(0, nb, HC):
                n_ = min(HC, nb - o_)
                ch_, col_ = divmod(lt0 + o_, HC)
                p.dma("act", hT[:, :, o_:o_ + n_], hT_loc[ch_ * D:(ch_ + 1) * D, col_:col_ + n_].rearrange("(k p) t -> p k t", p=128), "hT")
            tcl, colb = divmod(lt0, TT)
            for i in range(3):
                p.dma_dyn("sp", yT[i][:, :, 0:nb],
                          lambda en, i=i, tcl=tcl, colb=colb, nb=nb: y_all[i][bass.DynSlice((q_of(en) * NPR + tcl) * D, D), colb:colb + nb].rearrange("(k p) t -> p k t", p=128),
                          y_all[i], f"yl{i}")
        for oc2 in range(D // 256):
            for i in range(3):
                wb = load_panel(wg_d, i * D + oc2 * 256, 256)
                for sub in range(2):
                    ps = PS[sub * 3 + i]
                    for kc in range(KC):
                        p.mm(ps[:, 0:nb], wb[:, kc, sub * 128:(sub + 1) * 128], hT[:, kc, 0:nb],
                             start=(kc == 0), stop=(kc == KC - 1))
            for sub in range(2):
                for i in range(3):
                    p.act(gst[sub * 3 + i][:, 0:nb], PS[sub * 3 + i][:, 0:nb], AF.Sigmoid)
            for i in range(3):
                wb = load_panel(pj_d[i], oc2 * 256, 256)
                for sub in range(2):
                    ps = PS[sub * 3 + i]
                    for kc in range(KC):
                        p.mm(ps[:, 0:nb], wb[:, kc, sub * 128:(sub + 1) * 128], yT[i][:, kc, 0:nb],
                             start=(kc == 0), stop=(kc == KC - 1))
            for sub in range(2):
                oc = oc2 * 2 + sub
                a = gsb[0]
                b_ = gsb[1]
                p.tt(a[:, 0:nb], gst[sub * 3 + 0][:, 0:nb], PS[sub * 3 + 0][:, 0:nb], ALU.mult)
                p.tt(b_[:, 0:nb], gst[sub * 3 + 1][:, 0:nb], PS[sub * 3 + 1][:, 0:nb], ALU.mult)
                p.tt(a[:, 0:nb], a[:, 0:nb], b_[:, 0:nb], ALU.add)
                p.tt(b_[:, 0:nb], gst[sub * 3 + 2][:, 0:nb], PS[sub * 3 + 2][:, 0:nb], ALU.mult)
                p.tt(mT[:, oc, 0:nb], a[:, 0:nb], b_[:, 0:nb], ALU.add)
        for ti in range(ntile):
            tn = min(128, nb - ti * 128)
            if halo:
                p.dma_dyn("act", htm[ti][0:tn, :], lambda en: htail_all[bass.DynSlice(rr_of(en) * 2, 2), :], htail_all, f"htm{ti}")
            else:
                p.dma("act", htm[ti][0:tn, :], h_loc[lt0 + ti * 128:lt0 + ti * 128 + tn, :], f"htm{ti}")
        for cg2 in range(D // 256):
            wb = load_panel(wo_d, cg2 * 256, 256)
            for ti in range(ntile):
                tn = min(128, nb - ti * 128)
                ps = PS[ti]
                for kc in range(KC):
                    p.mm(ps[0:tn, 0:256], mT[:, kc, ti * 128:ti * 128 + tn], wb[:, kc, 0:256],
                         start=(kc == 0), stop=(kc == KC - 1))
                sl = slice(cg2 * 256, (cg2 + 1) * 256)
                p.stt(htm[ti][0:tn, sl], htm[ti][0:tn, sl], ALPHA, ps[0:tn, 0:256], ALU.mult, ALU.add)
        load_ln(0)
        for ti in range(ntile):
            tn = min(128, nb - ti * 128)
            layer_norm_tile(p, htm[ti], htm[ti], tn, lnt[0], lnt[1], st, mv, rstd)
            transpose_to_bf16(p, h1T, htm[ti], tn, ident, PS[6:8], ti * 128)
        for c2 in range(0, FC, 2):
            ncol = min(2, FC - c2) * 128
            wbs = [load_panel(wup_d, half * FFN + c2 * 128, ncol) for half in range(2)]
            for sub in range(ncol // 128):
                c = c2 + sub
                accs = []
                for half in range(2):
                    ch = half * FC + c
                    ps = PS[half * 2 + sub]
                    for kc in range(KC):
                        p.mm(ps[:, 0:nb], wbs[half][:, kc, sub * 128:(sub + 1) * 128], h1T[:, kc, 0:nb],
                             start=(kc == 0), stop=(kc == KC - 1))
                    ue = uext[half]
                    if halo:
                        p.ts(tails[:, ch, :], ps[:, 0:2], hmask[:, 0:1], None, op0=ALU.mult)
                        continue
                    p.copy(ue[:, 2:2 + nb], ps[:, 0:nb], eng="act")
                    p.copy(ue[:, 0:2], tails[:, ch, :], eng="act")
                    acc = gsb[half]
                    p.ts(acc[:, 0:nb], ue[:, 2:2 + nb], cw[:, ch, 2:3], cb[:, ch:ch + 1], op0=ALU.mult, op1=ALU.add)
                    p.stt(acc[:, 0:nb], ue[:, 1:1 + nb], cw[:, ch, 1:2], acc[:, 0:nb], ALU.mult, ALU.add)
                    p.stt(acc[:, 0:nb], ue[:, 0:nb], cw[:, ch, 0:1], acc[:, 0:nb], ALU.mult, ALU.add)
                    p.copy(tails[:, ch, :], ue[:, nb:nb + 2], eng="act")
                    accs.append(acc)
                if halo:
                    continue
                p.act(gsb[2][:, 0:nb], accs[0][:, 0:nb], AF.Gelu)
                p.tt(actT[:, c, 0:nb], gsb[2][:, 0:nb], accs[1][:, 0:nb], ALU.mult)
        if halo:
            continue
        for cg2 in range(D // 256):
            sl = slice(cg2 * 256, (cg2 + 1) * 256)
            for k0 in range(0, FC, KC):
                kn = min(KC, FC - k0)
                s = wcnt[0] % 2
                wcnt[0] += 1
                wv = wdn_d.rearrange("(k p) n -> p k n", p=128)
                kh = (kn + 1) // 2
                p.dma("sp", WS[s][:, 0:kh, 0:256], wv[:, k0:k0 + kh, cg2 * 256:(cg2 + 1) * 256], f"ws{s}")
                p.dma("act", WS[s][:, kh:kn, 0:256], wv[:, k0 + kh:k0 + kn, cg2 * 256:(cg2 + 1) * 256], f"ws{s}b")
                p.copy(WB[s][:, 0:kn, 0:256], WS[s][:, 0:kn, 0:256], eng="act" if s == 0 else "dve")
                for ti in range(ntile):
                    tn = min(128, nb - ti * 128)
                    ps = PS[ti]
                    for k in range(kn):
                        p.mm(ps[0:tn, 0:256], actT[:, k0 + k, ti * 128:ti * 128 + tn], WB[s][:, k, 0:256],
                             start=(k0 + k == 0), stop=(k0 + k == FC - 1))
            for ti in range(ntile):
                tn = min(128, nb - ti * 128)
                p.stt(htm[ti][0:tn, sl], htm[ti][0:tn, sl], ALPHA, PS[ti][0:tn, 0:256], ALU.mult, ALU.add)
        load_ln(2)
        for ti in range(ntile):
            tn = min(128, nb - ti * 128)
            o = tmp[ti % 2]
            layer_norm_tile(p, o, htm[ti], tn, lnt[0], lnt[1], st, mv, rstd)
            tg = t0 - 2 + ti * 128
            p.dma("pool", ho_d[tg:tg + tn, :], o[0:tn, :], f"ho{ti % 2}")
            if (not last) and tg + tn == TC:
                p.dma("pool", htail_loc[0:2, :], o[tn - 2:tn, :], "htl")
            transpose_to_bf16(p, h1T, o, tn, ident, PS[6:8], ti * 128)
        if not last:
            for o_ in range(0, nb, HC):
                n_ = min(HC, nb - o_)
                ch_, col_ = divmod(lt0 + o_, HC)
                p.dma("pool", hT_loc[ch_ * D:(ch_ + 1) * D, col_:col_ + n_].rearrange("(k p) t -> p k t", p=128), h1T[:, :, o_:o_ + n_], "hTo")
                if io["overlap"] and col_ + n_ == HC:
                    io["xchg_h"](ch_)
            if io["overlap"] and lt0 + nb == TC:
                io["xchg_tail"]()


RTB = 256
CH = 32
NEG_E = -0.6065306597126334


def emit_rwkv(p, c, nc, T, ld_, PS, warena, hT_src, sfx, y_d):
    def din(name, shape, dt=F32):
        return nc.dram_tensor(name, list(shape), dt, kind="ExternalInput").ap()

    TB = RTB
    NW = 2048
    w_d = din("rw_w" + sfx, [D, NW])
    mu_d = din("rw_mu" + sfx, [128, 16])
    pc_d = din("rw_pc" + sfx, [128, 20])
    w2_d = din("rw_w2" + sfx, [96, 512]); a2_d = din("rw_a2" + sfx, [96, 512]); g2_d = din("rw_g2" + sfx, [256, 512])
    lnw_d = din("rw_lnw" + sfx, [128, 128]); lnb_d = din("rw_lnb" + sfx, [128, 128])

    w = warena[:, 0:KC * NW].rearrange("p (k n) -> p k n", k=KC)
    ld_.load(w_d, w, NW)
    mu = p.sb("r_mu", [128, 16]); pc = p.sb("r_pc", [128, 5, 4])
    w2 = p.sb("r_w2", [96, 512]); a2 = p.sb("r_a2", [96, 512])
    g2f = p.sb("r_g2f", [128, 2, 512]); g2 = p.sb("r_g2", [128, 2, 512], BF16)
    lnw = p.sb("r_lnw", [128, 2, 64]); lnb = p.sb("r_lnb", [128, 2, 64])
    p.dma("sp", mu[:], mu_d[:, :], "k0")
    p.dma("sp", pc[:].rearrange("p a b -> p (a b)"), pc_d[:, :], "k1")
    p.dma("sp", w2[:], w2_d[:, :], "k2")
    p.dma("sp", a2[:], a2_d[:, :], "k0")
    p.dma("sp", g2f[:], g2_d.rearrange("(k p) n -> p k n", p=128), "k1")
    p.dma("sp", lnw[:].rearrange("p a b -> p (a b)"), lnw_d[:, :], "k2")
    p.dma("sp", lnb[:].rearrange("p a b -> p (a b)"), lnb_d[:, :], "k0")
    p.copy(g2[:], g2f[:])

    def tri_mask(name, base):
        t = p.sb(name, [128, 4, 32])
        p.memset(t[:], 1.0, eng="pool")
        for rb in range(4):
            v = t[32 * rb:32 * rb + 32, :, :]
            p.op("pool", lambda en, v=v: en.affine_select(out=v, in_=v, pattern=[[0, 4], [1, 32]], compare_op=ALU.is_ge,
                                                          fill=0.0, base=base, channel_multiplier=-1), [v], [v])
        return t
    m_su = tri_mask("r_msu", -1)
    m_le = tri_mask("r_mle", 0)
    m_sl = p.sb("r_msl", [128, 4, 32])
    p.memset(m_sl[:], 1.0, eng="pool")
    for rb in range(4):
        v = m_sl[32 * rb:32 * rb + 32, :, :]
        p.op("pool", lambda en, v=v: en.affine_select(out=v, in_=v, pattern=[[0, 4], [-1, 32]], compare_op=ALU.is_ge,
                                                      fill=0.0, base=-1, channel_multiplier=1), [v], [v])
    m4 = p.sb("r_m4", [128, 4, 128])
    for i, m in enumerate((m_su, m_le, m_su, m_le)):
        p.copy(m4[:, i, :], m[:].rearrange("p a b -> p (a b)"), eng="pool")
    E2 = p.sb("r_E2", [128, 64])
    p.copy(E2[0:64, :], c["ident"][0:64, 0:64], eng="pool")
    p.copy(E2[64:128, :], c["ident"][64:128, 64:128], eng="pool")
    BO = p.sb("r_BO", [128, 128])
    p.memset(BO[:], 0.0, eng="pool")
    p.memset(BO[0:64, 0:64], 1.0, eng="pool")
    p.memset(BO[64:128, 64:128], 1.0, eng="pool")
    zb = p.sb("r_zb", [128, 512], BF16)
    p.memset(zb[:], 0.0, eng="pool")
    for i in range(8):
        p.mm(PS[i][:, 0:512], zb[:, 0:128], zb[:, 0:512])
    rmask = p.sb("r_rmask", [128, TB])
    p.memset(rmask[:], 1.0, eng="pool")
    p.memset(rmask[:].rearrange("p (a b) -> p a b", b=CH)[:, :, 0:1], 0.0, eng="pool")

    hT = [p.sb("r_hT0", [128, KC, TB], BF16)] * 2
    pe = [p.sb("r_pe0", [128, TB + 1])] * 2
    ptail = p.sb("r_ptail", [128, 16, 1])
    p.memset(ptail[:], 0.0)
    dtmp = p.sb("r_dtmp", [128, TB])
    tw = p.sb("r_tw", [128, TB]); alo = p.sb("r_alo", [128, TB]); sgb = p.sb("r_sgb", [128, 2, TB], BF16)
    rr = p.sb("r_rr", [128, TB]); kx = p.sb("r_kx", [128, TB]); vv = p.sb("r_vv", [128, TB])
    sg = p.sb("r_sg", [128, TB]); cs = p.sb("r_cs", [128, TB]); ecl = p.sb("r_ecl", [128, 4, TB])
    em = p.sb("r_em", [128, TB]); ecp = p.sb("r_ecp", [128, TB]); kk = p.sb("r_kk", [128, TB])
    t1 = p.sb("r_t1", [128, TB]); asg = p.sb("r_asg", [128, TB]); kh = p.sb("r_kh", [128, TB])
    NCK = TB // CH
    AR = [p.sb(f"r_AR{i}", [128, 2, NCK, 2, CH]) for i in range(4)]
    Bz = [p.sb(f"r_Bz{i}", [128, NCK, 2, CH]) for i in range(4)]
    Kz = [p.sb(f"r_Kz{i}", [128, NCK, 2, CH]) for i in range(4)]
    Vz = [p.sb(f"r_Vz{i}", [128, NCK, 2, CH]) for i in range(4)]
    RKz = [p.sb(f"r_RKz{i}", [128, NCK, 2, CH]) for i in range(4)]
    for lst in (AR, Bz, Kz, Vz, RKz):
        for t in lst:
            p.memset(t[:], 0.0, eng="pool")
    ST = p.sb("r_ST", [128, 4, 64])
    p.memset(ST[:], 0.0)
    gfm = p.sb("r_gfm", [64, 8, TB])
    ystage = [p.sb(f"r_ys{i}", [64, 8, TB], BF16) for i in range(2)]
    N4 = p.sb("r_N4", [128, 4, 128]); NT = p.sb("r_NT", [128, 128])
    Pk = [p.sb(f"r_P{i}", [128, 128]) for i in range(2)]; Qk = [p.sb(f"r_Q{i}", [128, 128]) for i in range(2)]
    Vst = p.sb("r_Vst", [128, 64]); Wk = [p.sb(f"r_W{i}", [128, 64]) for i in range(2)]
    KTx = [p.sb(f"r_KTs{i}", [128, 128]) for i in range(2)]; BTx = [p.sb(f"r_BTs{i}", [128, 128]) for i in range(2)]
    for t_ in KTx + BTx:
        p.memset(t_[:], 0.0, eng="pool")
    stmp = p.sb("r_stmp", [128, 2, 64])
    st6 = p.sb("r_st6", [128, 6]); mv = p.sb("r_mv", [128, 2]); rstd = p.sb("r_rstd", [128, 1])
    Pb16 = p.sb("r_Pb16", [128, 128], BF16); Qb16 = p.sb("r_Qb16", [128, 128], BF16)
    yn = p.sb("r_yn", [128, 64]); Ysb = p.sb("r_Ysb", [128, 64]); bsc = p.sb("r_bsc", [128, 2]); yfin = p.sb("r_yfin", [128, 64])

    w0 = lambda cc: pc[:, 0, cc:cc + 1]
    a0 = lambda cc: pc[:, 1, cc:cc + 1]
    k_k = lambda cc: pc[:, 2, cc:cc + 1]
    k_a = lambda cc: pc[:, 3, cc:cc + 1]
    r_k = lambda cc: pc[:, 4, cc:cc + 1]

    def proj_lerp(hb, ch, out):
        ps = PS[ch % 2]
        for kc in range(KC):
            p.mm(ps[:, 0:TB], w[:, kc, ch * 128:(ch + 1) * 128], hb[:, kc, :], start=(kc == 0), stop=(kc == KC - 1))
        x_ = pe[ch % 2]
        p.copy(x_[:, 1:1 + TB], ps[:, 0:TB], eng="act")
        p.copy(x_[:, 0:1], ptail[:, ch, :], eng="act")
        p.tt(dtmp[:], x_[:, 0:TB], x_[:, 1:1 + TB], ALU.subtract)
        p.stt(out, dtmp[:], mu[:, ch:ch + 1], x_[:, 1:1 + TB], ALU.mult, ALU.add)
        p.copy(ptail[:, ch, :], x_[:, TB:TB + 1], eng="act")

    nblk = T // TB
    H = slice(0, 64), slice(64, 128)
    for b in range(nblk):
        hb = hT[b % 2]
        for off_, ap_ in hT_src(b * TB, TB):
            p.dma("act", hb[:, :, off_:off_ + ap_.shape[2]], ap_, f"r_hT{b % 2}")
        proj_lerp(hb, 12, tw[:])
        p.act(tw[0:96, :], tw[0:96, :], AF.Tanh)
        proj_lerp(hb, 13, alo[:])
        for j in range(2):
            proj_lerp(hb, 14 + j, dtmp[:])
            p.act(sgb[:, j, :], dtmp[:], AF.Sigmoid)
        for hd in range(8):
            ps = PS[6 + hd % 2]
            for j in range(2):
                p.mm(ps[0:64, 0:TB], g2[:, j, hd * 64:(hd + 1) * 64], sgb[:, j, :], start=(j == 0), stop=(j == 1))
            p.copy(gfm[:, hd, :], ps[0:64, 0:TB], eng="act")
        for cc in range(4):
            proj_lerp(hb, cc, rr[:])
            proj_lerp(hb, 4 + cc, kx[:])
            proj_lerp(hb, 8 + cc, vv[:])
            p.mm(PS[6][:, 0:TB], w2[:, cc * 128:(cc + 1) * 128], tw[0:96, :])
            p.act(sg[:], PS[6][:, 0:TB], AF.Sigmoid, bias=w0(cc), scale=1.0)
            p.op("dve", lambda en: en.tensor_tensor_scan(out=cs[:], data0=rmask[:], data1=sg[:], initial=0.0,
                                                         op0=ALU.mult, op1=ALU.add), [rmask[:], sg[:]], [cs[:]])
            p.act(ecl[:, cc, :], cs[:], AF.Exp, scale=NEG_E)
            p.act(em[:], cs[:], AF.Exp, scale=-NEG_E)
            p.tt(t1[:], cs[:], sg[:], ALU.subtract)
            p.act(ecp[:], t1[:], AF.Exp, scale=NEG_E)
            p.mm(PS[7][:, 0:TB], a2[:, cc * 128:(cc + 1) * 128], alo[0:96, :])
            p.act(asg[:], PS[7][:, 0:TB], AF.Sigmoid, bias=a0(cc), scale=1.0)
            p.ts(kk[:], kx[:], k_k(cc), None, op0=ALU.mult)
            p.tt(t1[:], kk[:], kk[:], ALU.mult)
            p.mm(PS[6][:, 0:TB], BO[:], t1[:])
            p.ts(t1[:], PS[6][:, 0:TB], 1e-24, None, op0=ALU.max)
            p.act(t1[:], t1[:], AF.Sqrt)
            p.recip(t1[:], t1[:])
            p.tt(kk[:], kk[:], t1[:], ALU.mult)
            p.ts(t1[:], asg[:], -1.0, k_a(cc), op0=ALU.add, op1=ALU.mult)
            p.stt(kh[:], t1[:], 1.0, kx[:], ALU.add, ALU.mult)
            c3 = lambda ap: ap.rearrange("p (a b) -> p a b", b=CH)
            for h2 in range(2):
                hs = H[h2]
                p.stt(AR[cc][hs, 0, :, h2, :], c3(kk[hs, :]), -1.0, c3(ecp[hs, :]), ALU.mult, ALU.mult)
                p.tt(AR[cc][hs, 1, :, h2, :], c3(rr[hs, :]), c3(ecl[hs, cc, :]), ALU.mult)
                p.tt(t1[hs, :], kk[hs, :], asg[hs, :], ALU.mult)
                p.tt(Bz[cc][hs, :, h2, :], c3(t1[hs, :]), c3(em[hs, :]), ALU.mult)
                p.tt(Kz[cc][hs, :, h2, :], c3(kh[hs, :]), c3(em[hs, :]), ALU.mult)
                p.copy(Vz[cc][hs, :, h2, :], c3(vv[hs, :]), eng="act")
                p.stt(RKz[cc][hs, :, h2, :], c3(rr[hs, :]), r_k(cc)[hs, :], c3(kh[hs, :]), ALU.mult, ALU.mult)
        ysb = ystage[b % 2]
        import os
        STG = int(os.environ.get('RW_STAGE', '9'))
        for ci in range(TB // CH if STG > 0 else 0):
            tk = slice(ci * CH, (ci + 1) * CH)
            for g in range(2):
                cs2 = (2 * g, 2 * g + 1)
                for cl, cc in enumerate(cs2):
                    rhsAR = AR[cc][:, :, ci, :, :].rearrange("p m a b -> p m (a b)")
                    o1 = PS[2][64 * cl:64 * cl + 64, :].rearrange("p (m c x) -> p m c x", m=4, c=2)
                    p.mm(o1[:, 0:2, cl, :], Bz[cc][:, ci, :, :].rearrange("p a b -> p (a b)"), rhsAR)
                    p.mm(o1[:, 2:4, cl, :], Kz[cc][:, ci, :, :].rearrange("p a b -> p (a b)"), rhsAR)
                    p.mm(PS[3][64 * cl:64 * cl + 64, 64 * cl:64 * cl + 64], AR[cc][:, 0, ci, :, :].rearrange("p a b -> p (a b)"), Bz[cc][:, ci, :, :].rearrange("p a b -> p (a b)"))
                    p.mm(PS[4][64 * cl:64 * cl + 64, 0:64], Vz[cc][:, ci, :, :].rearrange("p a b -> p (a b)"), E2[:])
                p.tt(N4[:].rearrange("p a b -> p (a b)"), PS[2][:, 0:512], m4[:].rearrange("p a b -> p (a b)"), ALU.mult)
                p.tt(NT[:], PS[3][:, 0:128], m_sl[:].rearrange("p a b -> p (a b)"), ALU.mult)
                p.copy(Vst[:], PS[4][:, 0:64], eng="act")
                Nab, Nrb, Nak, Nrk = N4[:, 0, :], N4[:, 1, :], N4[:, 2, :], N4[:, 3, :]
                if STG < 2:
                    continue
                SUB = int(os.environ.get('RW_SUB', '9'))
                for cl, cc in enumerate(cs2):
                    hs = slice(64 * cl, 64 * cl + 64)
                    p.mm((PS[7][hs, 256:320] if SUB == 0 else PS[4][hs, 64:128]), AR[cc][:, 0, ci, :, :].rearrange("p a b -> p (a b)"), ST[:, cc, :])
                if SUB >= 2:
                    p.mm(PS[4][:, 320:384], Nak, Vst[:])
                if SUB >= 3:
                    p.copy(Wk[0][:], PS[4][:, 64:128], eng="act")
                if SUB >= 4:
                    p.tt(Wk[0][:], Wk[0][:], PS[4][:, 320:384], ALU.add)
                if STG < 3:
                    continue
                Pc, Qc = Nab, NT[:]
                wi = 0
                QPS = PS[7][:, 384:512] if os.environ.get('RW_QPS', '1') == '1' else PS[3][:, 256:384]
                RK = int(os.environ.get('RW_K', '5')); RQ = int(os.environ.get('RW_Q', '9'))
                for k in range(min(5, RK)):
                    p.mm(PS[5][:, 0:64], Pc, Wk[wi][:])
                    p.tt(Wk[1 - wi][:], Wk[wi][:], PS[5][:, 0:64], ALU.add)
                    wi = 1 - wi
                    if RQ < 1:
                        continue
                    if k < 4:
                        if os.environ.get('RW_BF', '0') == '1':
                            p.copy(Pb16[:], Pc); p.copy(Qb16[:], Qc, eng='pool')
                            Pm, Qm = Pb16[:], Qb16[:]
                        else:
                            Pm, Qm = Pc, Qc
                        p.mm(PS[3][:, 128:256], Qm, Pm)
                        if k < 3 and RQ != 7:
                            p.mm(QPS, Pm, Qm)
                        Pn = Pk[k % 2]
                        p.copy(Pn[:], PS[3][:, 128:256], eng="act")
                        if k < 3 and RQ != 7:
                            Qn = Qk[k % 2]
                            p.copy(Qn[:], QPS, eng=('act' if RQ == 8 else 'dve'))
                            Qc = Qn[:]
                        Pc = Pn[:]
                if STG < 4:
                    continue
                U = Wk[wi]
                for cl, cc in enumerate(cs2):
                    hs = slice(64 * cl, 64 * cl + 64)
                    p.mm(PS[4][hs, 128:192], AR[cc][:, 1, ci, :, :].rearrange("p a b -> p (a b)"), ST[:, cc, :])
                p.mm(PS[4][:, 384:448], Nrk, Vst[:])
                p.mm(PS[4][:, 448:512], Nrb, U[:])
                p.copy(Ysb[:], PS[4][:, 128:192], eng="act")
                p.tt(Ysb[:], Ysb[:], PS[4][:, 384:448], ALU.add)
                p.tt(Ysb[:], Ysb[:], PS[4][:, 448:512], ALU.add)
                for cl, cc in enumerate(cs2):
                    p.mm(PS[4][64 * cl:64 * cl + 64, 192:194], RKz[cc][:, ci, :, :].rearrange("p a b -> p (a b)"), c["ones"][:, 0:2])
                if STG < 5:
                    continue
                for cl, cc in enumerate(cs2):
                    p.mm(PS[5][64 * cl:64 * cl + 64, 128:256], Kz[cc][:, ci, :, :].rearrange("p a b -> p (a b)"), c["ident"][:])
                    p.mm(PS[7][64 * cl:64 * cl + 64, 256:384], Bz[cc][:, ci, :, :].rearrange("p a b -> p (a b)"), c["ident"][:])
                for cl in range(2):
                    hs = slice(64 * cl, 64 * cl + 64)
                    p.copy(KTx[cl][hs, :], PS[5][hs, 128:256], eng="act")
                    p.copy(BTx[cl][hs, :], PS[7][hs, 256:384])
                if STG < 6:
                    continue
                SUk = (PS[6][:, 256:320], PS[6][:, 320:384]); SUb = (PS[6][:, 384:448], PS[6][:, 448:512])
                for cl, cc in enumerate(cs2):
                    p.mm(SUk[cl], KTx[cl][:], Vst[:])
                    p.mm(SUb[cl], BTx[cl][:], U[:])
                if STG < 7:
                    continue
                p.op("dve", lambda en: en.bn_stats(out=st6[:], in_=Ysb[:]), [Ysb[:]], [st6[:]])
                p.op("dve", lambda en: en.bn_aggr(out=mv[:], in_=st6[:]), [st6[:]], [mv[:]])
                p.act(rstd[:], mv[:, 1:2], AF.Sqrt, bias=64e-5, scale=1.0)
                p.recip(rstd[:], rstd[:])
                p.ts(yn[:], Ysb[:], mv[:, 0:1], rstd[:, 0:1], op0=ALU.subtract, op1=ALU.mult)
                p.tt(yn[:], yn[:], lnw[:, g, :], ALU.mult)
                p.tt(yn[:], yn[:], lnb[:, g, :], ALU.add)
                p.copy(bsc[:], PS[4][:, 192:194], eng="act")
                p.stt(yfin[:], Vst[:], bsc[:, 0:1], yn[:], ALU.mult, ALU.add)
                p.tr(PS[5][0:64, 384:512], yfin[:], c["ident"][:])
                p.tt(ysb[:, 4 * g:4 * g + 4, tk], PS[5][0:64, 384:512].rearrange("p (h t) -> p h t", h=4),
                     gfm[:, 4 * g:4 * g + 4, tk], ALU.mult)
                for cl, cc in enumerate(cs2):
                    p.tt(stmp[:, cl, :], ST[:, cc, :], SUk[cl], ALU.add)
                    p.tt(stmp[:, cl, :], stmp[:, cl, :], SUb[cl], ALU.add)
                for cl, cc in enumerate(cs2):
                    te = ci * CH + CH - 1
                    p.ts(ST[:, cc, :], stmp[:, cl, :], ecl[:, cc, te:te + 1], None, op0=ALU.mult)
        p.dma("pool", y_d(b * TB, TB).rearrange("(h i) t -> i h t", i=64), ysb[:], f"r_yo{b % 2}")


TB = 512


def make_consts(p):
    c = {}
    c["ident"] = make_ident(p)
    for name, pat, cm, base in (("le", [[1, 128]], -1, 0), ("gt", [[-1, 128]], 1, -1)):
        t = p.sb("mask_" + name, [128, 128])
        p.memset(t[:], 1.0, eng="pool")
        p.op("pool", lambda en, t=t, pat=pat, cm=cm, base=base: en.affine_select(
            out=t[:], in_=t[:], pattern=pat, compare_op=ALU.is_ge, fill=0.0, base=base, channel_multiplier=cm),
            [t[:]], [t[:]])
        c[name] = t
    ones = p.sb("ones", [128, 128])
    p.memset(ones[:], 1.0, eng="pool")
    c["ones"] = ones
    return c


class Loader:
    def __init__(self, p):
        self.p = p
        self.WS = [p.sb(f"lws{i}", [128, KC, 128]) for i in range(2)]
        self.n = 0

    def load(self, wd, dst, N, kc=KC):
        p = self.p
        for n0 in range(0, N, 128):
            nn = min(128, N - n0)
            s = self.n % 2
            self.n += 1
            wv = wd.rearrange("(k p) n -> p k n", p=128)
            p.dma("sp", self.WS[s][:, 0:kc // 2, 0:nn], wv[:, 0:kc // 2, n0:n0 + nn], f"lws{s}")
            p.dma("act", self.WS[s][:, kc // 2:kc, 0:nn], wv[:, kc // 2:kc, n0:n0 + nn], f"lws{s}b")
            p.copy(dst[:, :, n0:n0 + nn], self.WS[s][:, 0:kc, 0:nn], eng="act" if s == 0 else "dve")


def gla_chunk(p, c, g, QT, KT, Kt, V, ld, nh, dvh, Yout):
    nhp = max(nh, 2)
    W = nh * dvh
    PS = g["PS"]
    p.mm(PS[0][:, 0:128], KT, QT)
    p.tt(g["sTm"][:], PS[0][:, 0:128], c["le"][:], ALU.mult)
    for h in range(nh):
        p.act(g["rhs_all"][:, h, :], c["le"][:], AF.Identity, scale=ld[:, h:h + 1])
    for h0 in range(0, nh, 4):
        hn = min(4, nh - h0)
        ps = PS[1 + h0 // 4]
        p.mm(ps[:, 0:hn * 128], c["gt"][:], g["rhs_all"][:, h0:h0 + hn, :].rearrange("p h l -> p (h l)"))
        p.act(g["dec"][:, h0:h0 + hn, :].rearrange("p h l -> p (h l)"), ps[:, 0:hn * 128], AF.Exp)
    sm = PS[3]
    p.mm(sm[:, 0:nhp], c["le"][:], ld[:, 0:nhp])
    p.mm(sm[:, 8:8 + nhp], c["gt"][:], ld[:, 0:nhp])
    p.mm(sm[:, 16:16 + nhp], c["ones"][:], ld[:, 0:nhp])
    p.act(g["sme"][:, 0:24], sm[:, 0:24], AF.Exp)
    ea = g["sme"][:, 0:nh]
    dte = g["sme"][:, 8:8 + nh]
    cdb = g["sme"][:, 16:16 + nh]
    p.tt(g["MT"][:, 0:nh, :], g["dec"][:, 0:nh, :], g["sTm"][:].unsqueeze(1).broadcast_to([128, nh, 128]), ALU.mult)
    for h in range(nh):
        p.mm(PS[4][:, h * dvh:(h + 1) * dvh] if W <= 512 else PS[4][:, 0:dvh], g["MT"][:, h, :], V[:, h * dvh:(h + 1) * dvh])
    p.mm(PS[5][:, 0:W], QT, g["Sb"][:, 0:W])
    v3 = lambda ap: ap.rearrange("p (h d) -> p h d", h=nh)
    p.tt(v3(g["tmpY"][:, 0:W]), v3(PS[5][:, 0:W]), ea.unsqueeze(2).broadcast_to([128, nh, dvh]), ALU.mult)
    p.tt(Yout[:, 0:W], g["tmpY"][:, 0:W], PS[4][:, 0:W], ALU.add)
    p.tt(v3(g["Vw"][:, 0:W]), v3(V[:, 0:W]), dte.unsqueeze(2).broadcast_to([128, nh, dvh]), ALU.mult)
    p.mm(PS[6][:, 0:W], Kt, g["Vw"][:, 0:W])
    p.tt(v3(g["S"][:, 0:W]), v3(g["S"][:, 0:W]), cdb.unsqueeze(2).broadcast_to([128, nh, dvh]), ALU.mult)
    p.tt(g["S"][:, 0:W], g["S"][:, 0:W], PS[6][:, 0:W], ALU.add)
    p.copy(g["Sb"][:, 0:W], g["S"][:, 0:W], eng="act")


def gla_alloc(p, tag, nh, dvh, PS):
    W = nh * dvh
    g = {"PS": PS}
    g["sTm"] = p.sb(tag + "sTm", [128, 128])
    g["rhs_all"] = p.sb(tag + "rhs", [128, nh, 128])
    g["dec"] = p.sb(tag + "dec", [128, nh, 128])
    g["sme"] = p.sb(tag + "sme", [128, 24])
    g["MT"] = p.sb(tag + "MT", [128, nh, 128], BF16)
    g["tmpY"] = p.sb(tag + "tmpY", [128, W])
    g["Vw"] = p.sb(tag + "Vw", [128, W], BF16)
    g["S"] = p.sb(tag + "S", [128, W])
    g["Sb"] = p.sb(tag + "Sb", [128, W], BF16)
    p.memset(g["S"][:], 0.0)
    p.memset(g["Sb"][:], 0.0)
    return g


def conv_fm(p, out_acc, xe, nb, cwt, cbt, ch, ntap):
    p.ts(out_acc[:, 0:nb], xe[:, ntap - 1:ntap - 1 + nb], cwt[:, ch, ntap - 1:ntap], cbt[:, ch:ch + 1], op0=ALU.mult, op1=ALU.add)
    for j in range(ntap - 2, -1, -1):
        p.stt(out_acc[:, 0:nb], xe[:, j:j + nb], cwt[:, ch, j:j + 1], out_acc[:, 0:nb], ALU.mult, ALU.add)


def emit_ssd(p, c, nc, T, ld_, PS, warena, hT_src, sfx, y_d):
    def din(name, shape, dt=F32):
        return nc.dram_tensor(name, list(shape), dt, kind="ExternalInput").ap()

    NW = 1288
    w_d = din("ssd_w" + sfx, [D, NW])
    cw_d = din("ssd_cw" + sfx, [128, 6 * 4])
    cb_d = din("ssd_cb" + sfx, [128, 6])
    vec_d = din("ssd_vec" + sfx, [1, 8 + 8 + 512 + 512])

    w = warena[:, 0:KC * NW].rearrange("p (k n) -> p k n", k=KC)
    ld_.load(w_d, w, NW)
    cw = p.sb("s_cw", [128, 6, 4]); cb = p.sb("s_cb", [128, 6])
    p.dma("sp", cw[:].rearrange("p c j -> p (c j)"), cw_d[:, :], "k0")
    p.dma("sp", cb[:], cb_d[:, :], "k1")
    vec = p.sb("s_vec", [128, 1040])
    p.dma("sp", vec[:], vec_d[0:1, :].partition_broadcast(128), "k2")
    dtb = vec[:, 0:8]; Dx = vec[:, 16:528]; nw = vec[:, 528:1040]
    aneg = p.sb("s_aneg", [128, 8])
    p.act(aneg[:], vec[:, 8:16], AF.Exp)
    p.ts(aneg[:], aneg[:], -1.0, None, op0=ALU.mult)

    hT = [p.sb(f"s_hT{i}", [128, KC, TB], BF16) for i in range(2)]
    xe = [p.sb(f"s_xe{i}", [128, TB + 3]) for i in range(2)]
    tails = p.sb("s_tails", [128, 6, 3])
    p.memset(tails[:], 0.0)
    acc = p.sb("s_acc", [128, TB])
    xs = p.sb("s_xs", [128, 4, TB])
    Bf = p.sb("s_Bf", [128, TB]); Bb = p.sb("s_Bb", [128, TB], BF16); Cb = p.sb("s_Cb", [128, TB], BF16)
    zs = p.sb("s_zs", [128, 512]); xtm = p.sb("s_xtm", [128, 512]); V = p.sb("s_V", [128, 512], BF16)
    Kt = p.sb("s_Kt", [128, 128], BF16)
    dt = p.sb("s_dt", [128, 8]); ld = p.sb("s_ld", [128, 8])
    Y = p.sb("s_Y", [128, 512]); yo = [p.sb(f"s_yo{i}", [128, 512]) for i in range(2)]
    ss = p.sb("s_ss", [128, 1]); junk = p.sb("s_junk", [128, 512])
    yob = [p.sb(f"s_yob{i}", [128, 4, 128], BF16) for i in range(2)]
    g = gla_alloc(p, "sg_", 8, 64, PS)

    nblk = T // TB
    for b in range(nblk):
        hb = hT[b % 2]
        for off_, ap_ in hT_src(b * TB, TB):
            p.dma("act", hb[:, :, off_:off_ + ap_.shape[2]], ap_, f"s_hT{b % 2}")
        for ch in range(6):
            ps = PS[ch % 2]
            for kc in range(KC):
                p.mm(ps[:, 0:TB], w[:, kc, ch * 128:(ch + 1) * 128], hb[:, kc, :], start=(kc == 0), stop=(kc == KC - 1))
            x_ = xe[ch % 2]
            p.copy(x_[:, 3:3 + TB], ps[:, 0:TB], eng="act")
            p.copy(x_[:, 0:3], tails[:, ch, :], eng="act")
            conv_fm(p, acc, x_, TB, cw, cb, ch, 4)
            p.copy(tails[:, ch, :], x_[:, TB:TB + 3], eng="act")
            if ch < 4:
                p.act(xs[:, ch, :], acc[:], AF.Silu)
            elif ch == 4:
                p.act(Bf[:], acc[:], AF.Silu)
                p.copy(Bb[:], Bf[:], eng="act")
            else:
                p.act(Cb[:], acc[:], AF.Silu)
        for ci in range(TB // 128):
            tk = slice(ci * 128, (ci + 1) * 128)
            for kc in range(KC):
                p.mm(PS[7][:, 0:512], hb[:, kc, tk], w[:, kc, 768:1280], start=(kc == 0), stop=(kc == KC - 1))
            p.act(zs[:], PS[7][:, 0:512], AF.Silu)
            for kc in range(KC):
                p.mm(PS[3][:, 32:40], hb[:, kc, tk], w[:, kc, 1280:1288], start=(kc == 0), stop=(kc == KC - 1))
            p.tt(dt[:], PS[3][:, 32:40], dtb, ALU.add)
            p.act(dt[:], dt[:], AF.Exp)
            p.act(dt[:], dt[:], AF.Ln, bias=1.0, scale=1.0)
            p.tt(ld[:], dt[:], aneg[:], ALU.mult)
            for j in range(4):
                p.tr(PS[7][:, j * 128:(j + 1) * 128], xs[:, j, tk], c["ident"][:])
            p.copy(xtm[:], PS[7][:, 0:512], eng="act")
            p.tt(V[:].rearrange("p (h d) -> p h d", h=8), xtm[:].rearrange("p (h d) -> p h d", h=8),
                 dt[:].unsqueeze(2).broadcast_to([128, 8, 64]), ALU.mult)
            p.tr(PS[0][:, 128:256], Bf[:, tk], c["ident"][:])
            p.copy(Kt[:], PS[0][:, 128:256], eng="act")
            gla_chunk(p, c, g, Cb[:, tk], Bb[:, tk], Kt[:], V, ld, 8, 64, Y)
            p.tt(xtm[:], xtm[:], Dx, ALU.mult)
            p.tt(Y[:], Y[:], xtm[:], ALU.add)
            p.tt(Y[:], Y[:], zs[:], ALU.mult)
            p.act(junk[:], Y[:], AF.Square, accum_out=ss[:])
            p.act(ss[:], ss[:], AF.Sqrt, bias=1e-5, scale=1.0 / 512)
            p.recip(ss[:], ss[:])
            o = yo[ci % 2]
            p.stt(o[:], Y[:], ss[:, 0:1], nw, ALU.mult, ALU.mult)
            t0 = b * TB + ci * 128
            ob = yob[ci % 2]
            for j in range(4):
                p.tr(PS[7][:, j * 128:(j + 1) * 128], o[:, j * 128:(j + 1) * 128], c["ident"][:])
            p.copy(ob[:], PS[7][:, 0:512].rearrange("p (a b) -> p a b", a=4), eng="act")
            p.dma("pool", y_d(t0, 128).rearrange("(c p) t -> p c t", p=128), ob[:], f"s_yo{ci % 2}")


def emit_mlstm(p, c, nc, T, ld_, PS, warena, hT_src, sfx, y_d):
    def din(name, shape, dt=F32):
        return nc.dram_tensor(name, list(shape), dt, kind="ExternalInput").ap()

    NW = 1540
    w_d = din("ml_w" + sfx, [D, NW])
    cw_d = din("ml_cw" + sfx, [128, 4 * 4])
    cb_d = din("ml_cb" + sfx, [128, 4])
    vec_d = din("ml_vec" + sfx, [1, 2 + 2 + 512])
    w = warena[:, 0:KC * NW].rearrange("p (k n) -> p k n", k=KC)
    ld_.load(w_d, w, NW)
    cw = p.sb("m_cw", [128, 4, 4]); cb = p.sb("m_cb", [128, 4])
    p.dma("sp", cw[:].rearrange("p c j -> p (c j)"), cw_d[:, :], "k0")
    p.dma("sp", cb[:], cb_d[:, :], "k1")
    vec = p.sb("m_vec", [128, 516])
    p.dma("sp", vec[:], vec_d[0:1, :].partition_broadcast(128), "k2")
    nw = vec[:, 4:516]

    hT = [p.sb(f"m_hT{i}", [128, KC, TB], BF16) for i in range(2)]
    xe = [p.sb(f"m_xe{i}", [128, TB + 3]) for i in range(2)]
    tails = p.sb("m_tails", [128, 4, 3])
    p.memset(tails[:], 0.0)
    acc = p.sb("m_acc", [128, TB])
    qb = p.sb("m_qb", [128, 2, TB], BF16); kf = p.sb("m_kf", [128, 2, TB]); kb = p.sb("m_kb", [128, 2, TB], BF16)
    vtm = p.sb("m_vtm", [128, 512]); osg = p.sb("m_osg", [128, 512])
    Vp = [p.sb(f"m_Vp{i}", [128, 257], BF16) for i in range(2)]
    Kt = p.sb("m_Kt", [128, 128], BF16)
    gi = p.sb("m_gi", [128, 4]); ei = p.sb("m_ei", [128, 2]); lf = p.sb("m_lf", [128, 2, 2])
    Y = p.sb("m_Y", [128, 257]); hh = p.sb("m_h", [128, 512]); yo = [p.sb(f"m_yo{i}", [128, 512]) for i in range(2)]
    yob = [p.sb(f"m_yob{i}", [128, 4, 128], BF16) for i in range(2)]
    den = p.sb("m_den", [128, 1]); ss = p.sb("m_ss", [128, 1]); junk = p.sb("m_junk", [128, 256])
    gs = [gla_alloc(p, f"mg{i}_", 1, 257, PS) for i in range(2)]

    nblk = T // TB
    for b in range(nblk):
        hb = hT[b % 2]
        for off_, ap_ in hT_src(b * TB, TB):
            p.dma("act", hb[:, :, off_:off_ + ap_.shape[2]], ap_, f"m_hT{b % 2}")
        for ch in range(4):
            ps = PS[ch % 2]
            for kc in range(KC):
                p.mm(ps[:, 0:TB], w[:, kc, ch * 128:(ch + 1) * 128], hb[:, kc, :], start=(kc == 0), stop=(kc == KC - 1))
            x_ = xe[ch % 2]
            p.copy(x_[:, 3:3 + TB], ps[:, 0:TB], eng="act")
            p.copy(x_[:, 0:3], tails[:, ch, :], eng="act")
            conv_fm(p, acc, x_, TB, cw, cb, ch, 4)
            p.copy(tails[:, ch, :], x_[:, TB:TB + 3], eng="act")
            if ch < 2:
                p.act(acc[:], acc[:], AF.Silu)
                p.ts(qb[:, ch, :], acc[:], 128.0 ** -0.5, None, op0=ALU.mult)
            else:
                p.act(kf[:, ch - 2, :], acc[:], AF.Silu)
                p.copy(kb[:, ch - 2, :], kf[:, ch - 2, :], eng="act")
        for ci in range(TB // 128):
            tk = slice(ci * 128, (ci + 1) * 128)
            for kc in range(KC):
                p.mm(PS[7][:, 0:512], hb[:, kc, tk], w[:, kc, 512:1024], start=(kc == 0), stop=(kc == KC - 1))
            p.copy(vtm[:], PS[7][:, 0:512], eng="act")
            for kc in range(KC):
                p.mm(PS[7][:, 0:512], hb[:, kc, tk], w[:, kc, 1024:1536], start=(kc == 0), stop=(kc == KC - 1))
            p.act(osg[:], PS[7][:, 0:512], AF.Sigmoid)
            for kc in range(KC):
                p.mm(PS[3][:, 32:36], hb[:, kc, tk], w[:, kc, 1536:1540], start=(kc == 0), stop=(kc == KC - 1))
            p.tt(gi[:], PS[3][:, 32:36], vec[:, 0:4], ALU.add)
            p.act(ei[:], gi[:, 0:2], AF.Exp)
            p.act(gi[:, 2:4], gi[:, 2:4], AF.Exp, scale=-1.0)
            p.act(gi[:, 2:4], gi[:, 2:4], AF.Ln, bias=1.0, scale=1.0)
            for hd in range(2):
                p.ts(lf[:, hd, :], gi[:, 2 + hd:3 + hd].to_broadcast([128, 2]), -1.0, None, op0=ALU.mult)
            t0 = b * TB + ci * 128
            for hd in range(2):
                vp = Vp[hd]
                p.ts(vp[:, 0:256], vtm[:, hd * 256:(hd + 1) * 256], ei[:, hd:hd + 1], None, op0=ALU.mult)
                p.copy(vp[:, 256:257], ei[:, hd:hd + 1], eng="act")
                p.tr(PS[0][:, 128:256], kf[:, hd, tk], c["ident"][:])
                p.copy(Kt[:], PS[0][:, 128:256], eng="act")
                gla_chunk(p, c, gs[hd], qb[:, hd, tk], kb[:, hd, tk], Kt[:], vp, lf[:, hd, :], 1, 257, Y)
                p.ts(den[:], Y[:, 256:257], -1.0, 1.0, op0=ALU.mult, op1=ALU.max)
                p.ts(junk[:, 0:1], Y[:, 256:257], 1.0, None, op0=ALU.max)
                p.tt(den[:], den[:], junk[:, 0:1], ALU.max)
                p.recip(den[:], den[:])
                hv = hh[:, hd * 256:(hd + 1) * 256]
                p.ts(hv, Y[:, 0:256], den[:, 0:1], None, op0=ALU.mult)
                p.act(junk[:], hv, AF.Square, accum_out=ss[:])
                p.act(ss[:], ss[:], AF.Sqrt, bias=1e-6, scale=1.0 / 256)
                p.recip(ss[:], ss[:])
                p.ts(hv, hv, ss[:, 0:1], None, op0=ALU.mult)
            o = yo[ci % 2]
            p.tt(o[:], hh[:], nw, ALU.mult)
            p.tt(o[:], o[:], osg[:], ALU.mult)
            ob = yob[ci % 2]
            for j in range(4):
                p.tr(PS[7][:, j * 128:(j + 1) * 128], o[:, j * 128:(j + 1) * 128], c["ident"][:])
            p.copy(ob[:], PS[7][:, 0:512].rearrange("p (a b) -> p a b", a=4), eng="act")
            p.dma("pool", y_d(t0, 128).rearrange("(c p) t -> p c t", p=128), ob[:], f"m_yo{ci % 2}")


D = 2048
FFN = 5504
FC = 43
O_Z = 0
O_XBC = 2048
O_DT = 5120
O_RW = 5152
O_QK = 11744
O_V = 13792
O_O = 15840
O_I = 17888
O_F = 17896
O_G = 17904


def fm(vec, nch):
    return np.ascontiguousarray(np.asarray(vec).reshape(nch, 128).T)


def fm_taps(w, nch):
    ntap = w.shape[0]
    return np.ascontiguousarray(np.asarray(w).reshape(ntap, nch, 128).transpose(2, 1, 0).reshape(128, nch * ntap))


def prep_A(I, l, q):
    w_in = I["w_in"][l]
    r = {}
    sl = lambda o, n: np.arange(o, o + n)
    cols = np.concatenate([sl(O_XBC + q * 512, 512), sl(O_XBC + 2048 + q * 128, 128), sl(O_XBC + 2560 + q * 128, 128),
                           sl(O_Z + q * 512, 512), sl(O_DT + q * 8, 8)])
    r["ssd_w"] = np.ascontiguousarray(w_in[:, cols])
    cch = np.concatenate([sl(q * 512, 512), sl(2048 + q * 128, 128), sl(2560 + q * 128, 128)])
    r["ssd_cw"] = fm_taps(I["ssd_conv_w"][l][:, cch], 6)
    r["ssd_cb"] = fm(I["ssd_conv_b"][l][cch], 6)
    hs = slice(q * 8, q * 8 + 8)
    r["ssd_vec"] = np.concatenate([I["ssd_dt_bias"][l][hs], I["ssd_a_log"][l][hs], np.repeat(I["ssd_d"][l][hs], 64),
                                   I["ssd_norm_w"][l][q * 512:(q + 1) * 512]]).astype(np.float32)[None, :]
    cols = np.concatenate([sl(O_QK + q * 256, 256), sl(O_QK + 1024 + q * 256, 256), sl(O_V + q * 512, 512),
                           sl(O_O + q * 512, 512), sl(O_I + 2 * q, 2), sl(O_F + 2 * q, 2)])
    r["ml_w"] = np.ascontiguousarray(w_in[:, cols])
    cch = np.concatenate([sl(q * 256, 256), sl(1024 + q * 256, 256)])
    r["ml_cw"] = fm_taps(I["mlstm_conv_w"][l][:, cch], 4)
    r["ml_cb"] = fm(I["mlstm_conv_b"][l][cch], 4)
    r["ml_vec"] = np.concatenate([I["mlstm_i_bias"][l][2 * q:2 * q + 2], I["mlstm_f_bias"][l][2 * q:2 * q + 2],
                                  I["mlstm_norm_w"][l][q * 512:(q + 1) * 512]]).astype(np.float32)[None, :]
    cs = slice(q * 512, (q + 1) * 512)
    z32 = np.zeros((D, 32), np.float32)
    rw = w_in[:, O_RW:O_RW + 6592]
    r["rw_w"] = np.ascontiguousarray(np.concatenate([rw[:, q * 512:(q + 1) * 512], rw[:, 2048 + q * 512:2048 + (q + 1) * 512],
                                                      rw[:, 4096 + q * 512:4096 + (q + 1) * 512], rw[:, 6144:6240], z32,
                                                      rw[:, 6240:6336], z32, rw[:, 6336:6592]], axis=1))
    mu = I["rwkv_mu"][l]
    zz = np.zeros(32, np.float32)
    mu_c = np.concatenate([mu[q * 512:(q + 1) * 512], mu[2048 + q * 512:2048 + (q + 1) * 512],
                           mu[4096 + q * 512:4096 + (q + 1) * 512], mu[6144:6240], zz, mu[6240:6336], zz, mu[6336:6592]])
    r["rw_mu"] = fm(mu_c, 16)
    pc = np.stack([fm(I["rwkv_w0"][l][cs], 4), fm(I["rwkv_a0"][l][cs], 4), fm(I["rwkv_k_k"][l][cs], 4),
                   fm(I["rwkv_k_a"][l][cs], 4), fm(I["rwkv_r_k"][l].reshape(-1)[cs], 4)], axis=1)
    r["rw_pc"] = np.ascontiguousarray(pc.reshape(128, 20))
    r["rw_w2"] = np.ascontiguousarray(I["rwkv_w2"][l][:, cs])
    r["rw_a2"] = np.ascontiguousarray(I["rwkv_a2"][l][:, cs])
    r["rw_g2"] = np.ascontiguousarray(I["rwkv_g2"][l][:, cs])
    lw = I["rwkv_ln_w"][l][cs].reshape(2, 4, 1, 64)
    lb = I["rwkv_ln_b"][l][cs].reshape(2, 4, 1, 64)
    r["rw_lnw"] = np.ascontiguousarray(np.broadcast_to(lw, (2, 4, 32, 64)).reshape(2, 128, 64).transpose(1, 0, 2).reshape(128, 128))
    r["rw_lnb"] = np.ascontiguousarray(np.broadcast_to(lb, (2, 4, 32, 64)).reshape(2, 128, 64).transpose(1, 0, 2).reshape(128, 128))
    return r


def prep_B(I, l):
    r = {}
    r["wg"] = np.ascontiguousarray(I["w_in"][l][:, O_G:O_G + 3 * D])
    r["p_ssd"] = I["proj_ssd"][l]; r["p_rwkv"] = I["proj_rwkv"][l]; r["p_ml"] = I["proj_mlstm"][l]
    r["w_out"] = I["w_out"][l]
    r["ln1_g"] = I["ln1_g"][l][None, :]; r["ln1_b"] = I["ln1_b"][l][None, :]
    r["ln2_g"] = I["ln2_g"][l][None, :]; r["ln2_b"] = I["ln2_b"][l][None, :]
    r["w_up"] = I["ffn_w_up"][l]
    r["cw"] = fm_taps(I["ffn_conv_w"][l], 2 * FC)
    r["cb"] = fm(I["ffn_conv_b"][l], 2 * FC)
    r["w_down"] = I["ffn_w_down"][l]
    return r


NG = 4
GROUPS = [[0, 1, 2, 3], [4, 5, 6, 7]]
OVERLAP_CC = False


def emit_L(p, nc, ident, PS, TC, io):
    x = nc.dram_tensor("x", [TC, D], F32, kind="ExternalInput").ap()
    g = nc.dram_tensor("ln_in_g", [1, D], F32, kind="ExternalInput").ap()
    b = nc.dram_tensor("ln_in_b", [1, D], F32, kind="ExternalInput").ap()
    h_loc, hT_loc, htail_loc = io["h_loc"], io["hT_loc"], io["htail_loc"]
    gt = p.sb("gt", [128, D]); bt = p.sb("bt", [128, D])
    p.dma("sp", gt[:], g[0:1, :].partition_broadcast(128), "c0")
    p.dma("sp", bt[:], b[0:1, :].partition_broadcast(128), "c1")
    xt = [p.sb(f"xt{i}", [128, D]) for i in range(2)]
    ht = [p.sb(f"ht{i}", [128, D]) for i in range(2)]
    hTt = [p.sb(f"hTt{i}", [128, KC, 128], BF16) for i in range(2)]
    st = p.sb("st", [128, 4, 6]); mv = p.sb("mv", [128, 2]); rstd = p.sb("rstd", [128, 1])
    nt = TC // 128
    for i in range(nt):
        s = i % 2
        p.dma("sp", xt[s][:], x[i * 128:(i + 1) * 128, :], f"x{s}")
        layer_norm_tile(p, ht[s], xt[s], 128, gt, bt, st, mv, rstd)
        p.dma("pool", h_loc[i * 128:(i + 1) * 128, :], ht[s][:], f"ho{s}")
        if i == nt - 1:
            p.dma("pool", htail_loc[0:2, :], ht[s][126:128, :], "htl")
        transpose_to_bf16(p, hTt[s], ht[s], 128, ident, PS[6:8], 0)
        ch_, col_ = divmod(i * 128, io["HC"])
        p.dma("pool", hT_loc[ch_ * D:(ch_ + 1) * D, col_:col_ + 128].rearrange("(k p) t -> p k t", p=128), hTt[s][:], f"hTo{s}")
        if io["overlap"] and col_ + 128 == io["HC"]:
            io["xchg_h"](ch_)
        if io["overlap"] and i == nt - 1:
            io["xchg_tail"]()


def build_fused(T, TC, depth=2):
    nc = bass.Bass("TRN2", target_bir_lowering=False)

    def internal(name, shape, dt, local=False):
        if local:
            return nc.dram_tensor(name, list(shape), dt, kind="Internal", addr_space="Local").ap()
        return nc.dram_tensor(name, list(shape), dt, kind="Internal").ap()

    io = {}
    io["h_loc"] = internal("h_loc", [TC, D], F32)
    HC = 256
    TT = min(1024, TC)
    NCHL = TC // HC
    NTC = T // TT
    io["HC"] = HC; io["TT"] = TT
    io["hT_loc"] = internal("hT_loc", [NCHL * D, HC], BF16)
    io["htail_loc"] = internal("htail_loc", [2, D], F32)
    io["hT_all"] = internal("hT_all", [NCHL * NG * D, HC], BF16, True)
    io["htail_all"] = internal("htail_all", [NG * 2, D], F32, True)
    y_loc = [internal(f"y_loc{i}", [NTC * 512, TT], BF16) for i in range(3)]
    io["y_all"] = [internal(f"y_all{i}", [NTC * NG * 512, TT], BF16, True) for i in range(3)]
    io["hmask"] = nc.dram_tensor("hmask", [128, 1], F32, kind="ExternalInput").ap()
    io["out"] = nc.dram_tensor("out", [TC, D], F32, kind="ExternalOutput").ap()
    hT_all = io["hT_all"]

    def hT_src(t0, n):
        r, lt = divmod(t0, TC)
        pieces = []
        for o_ in range(0, n, HC):
            n_ = min(HC, n - o_)
            ch_, col_ = divmod(lt + o_, HC)
            row0 = (ch_ * NG + r) * D
            pieces.append((o_, hT_all[row0:row0 + D, col_:col_ + n_].rearrange("(k p) t -> p k t", p=128)))
        return pieces

    def y_ap(i):
        def f_(t0, n):
            tc_, col_ = divmod(t0, TT)
            return y_loc[i][tc_ * 512:(tc_ + 1) * 512, col_:col_ + n]
        return f_

    def xchg_h(ch_):
        p.cc("AllGather", io["hT_loc"][ch_ * D:(ch_ + 1) * D, :], hT_all[ch_ * NG * D:(ch_ + 1) * NG * D, :], GROUPS)

    def xchg_tail():
        p.cc("AllGather", io["htail_loc"], io["htail_all"], GROUPS)

    io["xchg_h"] = xchg_h
    io["overlap"] = OVERLAP_CC
    io["xchg_tail"] = xchg_tail

    def exchange_y(i):
        for tc_ in range(NTC):
            p.cc("AllGather", y_loc[i][tc_ * 512:(tc_ + 1) * 512, :], io["y_all"][i][tc_ * NG * 512:(tc_ + 1) * NG * 512, :], GROUPS)

    p = Prog(nc)
    c = make_consts(p)
    PS = [p.ps(f"ps{i}", [128, 512]) for i in range(8)]
    m_ = p.mark()
    p.prefix = "L_"
    emit_L(p, nc, c["ident"], PS, TC, io)
    p.release(m_)
    for l in range(depth):
        sfx = f"_{l}"
        if not OVERLAP_CC:
            for ch_ in range(NCHL):
                xchg_h(ch_)
            xchg_tail()
        m_ = p.mark()
        p.prefix = f"A{l}_"
        warena = p.sb("warena", [128, KC * 2048], BF16)
        ld_ = Loader(p)
        m2 = p.mark()
        emit_ssd(p, c, nc, T, ld_, PS, warena, hT_src, sfx, y_ap(0))
        p.release(m2)
        if OVERLAP_CC:
            exchange_y(0)
        m2 = p.mark()
        emit_mlstm(p, c, nc, T, ld_, PS, warena, hT_src, sfx, y_ap(2))
        p.release(m2)
        if OVERLAP_CC:
            exchange_y(2)
        m2 = p.mark()
        emit_rwkv(p, c, nc, T, ld_, PS, warena, hT_src, sfx, y_ap(1))
        p.release(m_)
        for i_ in ((1,) if OVERLAP_CC else (0, 2, 1)):
            exchange_y(i_)
        m_ = p.mark()
        p.prefix = f"B{l}_"
        emit_B(p, nc, c["ident"], PS, TC, sfx, io, last=(l == depth - 1))
        p.release(m_)
    p.wait_all("pool")
    p.emit()
    return nc


def fused_inputs(I, NB, T, TC):
    QN = T // TC
    x = I["x"].astype(np.float32)
    maps = []
    for cidx in range(NB * QN):
        b, q = divmod(cidx, QN)
        m = {"x": np.ascontiguousarray(x[b, q * TC:(q + 1) * TC, :]),
             "ln_in_g": np.ascontiguousarray(I["ln_in_g"][None, :]), "ln_in_b": np.ascontiguousarray(I["ln_in_b"][None, :]),
             "hmask": np.full((128, 1), 0.0 if q == 0 else 1.0, np.float32)}
        for l in range(2):
            for k, v in prep_A(I, l, q).items():
                m[f"{k}_{l}"] = v
            for k, v in prep_B(I, l).items():
                m[f"{k}_{l}"] = np.ascontiguousarray(v)
        maps.append(m)
    return maps


def kernel(**inputs):
    I = {k: np.asarray(v) for k, v in inputs.items()}
    NB, T, _ = I["x"].shape
    QN = 4
    TC = T // QN
    nc = build_fused(T, TC)
    maps = fused_inputs(I, NB, T, TC)
    res = run_bass_kernel_spmd(nc, maps, core_ids=list(range(NB * QN))).results
    out = np.stack([np.asarray(r["out"]) for r in res]).reshape(NB, T, D)
    return np.ascontiguousarray(out.astype(np.float32))
```

```python
import os
import numpy as np
import ml_dtypes
from concourse.bass_utils import run_bass_kernel_spmd
import concourse.bass as bass
import concourse.mybir as mybir

F32 = mybir.dt.float32
BF16 = mybir.dt.bfloat16
AF = mybir.ActivationFunctionType
ALU = mybir.AluOpType
AX = mybir.AxisListType

SAME_ENGINE_SYNC = True


class Dual:
    def __init__(self, p, name, shape, dt):
        self.t = [p.sb(name + "_a", shape, dt), p.sb(name + "_b", shape, dt)]
        self.par = 0

    def __getitem__(self, idx):
        return self.t[self.par][idx]


class Prog:
    def __init__(self, nc):
        self.nc = nc
        self.eng = {"pe": nc.tensor, "dve": nc.vector, "act": nc.scalar,
                    "pool": nc.gpsimd, "sp": nc.sync}
        self.q = {e: [] for e in self.eng}
        self.cnt = {e: 0 for e in self.eng}
        self.waited = {e: {} for e in self.eng}
        self.acc = {}
        self.dma_cnt = {}
        self.sems = {}
        self.stack = []
        self.nops = 0
        self.prefix = ""
        self.duals = []

    def _enter(self, cm):
        v = cm.__enter__()
        self.stack.append(cm)
        return v

    def sb(self, name, shape, dt=F32):
        return self._enter(self.nc.sbuf_tensor("sb_" + self.prefix + name, list(shape), dt))

    def ps(self, name, shape, dt=F32):
        return self._enter(self.nc.psum_tensor("pp_" + name, list(shape), dt))

    def dual(self, name, shape, dt=F32):
        d = Dual(self, name, shape, dt)
        self.duals.append(d)
        return d

    def set_par(self, par):
        for d in self.duals:
            d.par = par

    def sem(self, key):
        if key not in self.sems:
            self.sems[key] = self._enter(self.nc.semaphore("s_" + str(key).replace(" ", "")))
        return self.sems[key]

    @staticmethod
    def region(ap):
        t = ap.tensor
        name = t.name
        space = str(ap.space)
        pairs = list(ap.ap)
        off = ap.offset
        if "DRAM" in space.upper() or "HBM" in space.upper():
            lo = off
            hi = off
            for st, n in pairs:
                if st >= 0:
                    hi += st * (n - 1)
                else:
                    lo += st * (n - 1)
            return (name, 0, 0, lo, hi)
        if "PSUM" in space.upper():
            return (name, 0, 127, 0, 1 << 30)
        pst, pn = pairs[0]
        p0 = ap.start_partition()
        p1 = p0 + (pn - 1 if pst != 0 else 0)
        if pst != 0:
            fo = off - p0 * pst
        else:
            fo = off
        lo = fo
        hi = fo
        for st, n in pairs[1:]:
            if st >= 0:
                hi += st * (n - 1)
            else:
                lo += st * (n - 1)
        return (name, p0, p1, lo, hi)

    def _deps(self, reads, writes):
        deps = {}

        def add(tok):
            s, v = tok
            if deps.get(s, 0) < v:
                deps[s] = v

        for ap in reads:
            name, p0, p1, lo, hi = self.region(ap)
            for r in self.acc.get(name, ()):
                if r[5] and not (r[2] < p0 or r[1] > p1 or r[4] < lo or r[3] > hi):
                    add(r[0])
        for ap in writes:
            name, p0, p1, lo, hi = self.region(ap)
            for r in self.acc.get(name, ()):
                if not (r[2] < p0 or r[1] > p1 or r[4] < lo or r[3] > hi):
                    add(r[0])
        return deps

    def _record(self, tok, reads, writes):
        for is_w, aps in ((False, reads), (True, writes)):
            for ap in aps:
                name, p0, p1, lo, hi = self.region(ap)
                lst = self.acc.setdefault(name, [])
                new = []
                for r in lst:
                    contained = (r[1] >= p0 and r[2] <= p1 and r[3] >= lo and r[4] <= hi)
                    if contained and (is_w or (r[0][0] == tok[0] and not r[5])):
                        continue
                    new.append(r)
                new.append((tok, p0, p1, lo, hi, is_w))
                self.acc[name] = new

    def _waits(self, e, deps):
        w = []
        for s, v in deps.items():
            if s == e and (e == "pe" or not SAME_ENGINE_SYNC):
                continue
            if self.waited[e].get(s, 0) < v:
                self.waited[e][s] = v
                w.append((s, v))
        return w

    def op(self, e, fn, reads, writes):
        pr_ = [a for a in reads if "PSUM" in str(a.space).upper()]
        if pr_:
            reads = [a for a in reads if "PSUM" not in str(a.space).upper()]
            writes = list(writes) + pr_
        deps = self._deps(reads, writes)
        w = self._waits(e, deps)
        self.cnt[e] += 1
        tok = (e, self.cnt[e])
        self.q[e].append((fn, w, (e, 1)))
        self._record(tok, reads, writes)
        self.nops += 1

    def dma(self, e, out, in_, key, **kw):
        deps = self._deps([in_], [out])
        w = self._waits(e, deps)
        k = ("dma", key)
        self.dma_cnt[k] = self.dma_cnt.get(k, 0) + 16
        tok = (k, self.dma_cnt[k])
        self.q[e].append((lambda en: en.dma_start(out=out, in_=in_, **kw), w, (k, 16)))
        self._record(tok, [in_], [out])
        self.nops += 1

    def barrier(self):
        for e in self.eng:
            w = []
            for f in self.eng:
                if f != e and self.cnt[f] > self.waited[e].get(f, 0):
                    self.waited[e][f] = self.cnt[f]
                    w.append((f, self.cnt[f]))
            for k, v in self.dma_cnt.items():
                if self.waited[e].get(k, 0) < v:
                    self.waited[e][k] = v
                    w.append((k, v))
            self.q[e].append((None, w, None))

    def mark(self):
        return len(self.stack)

    def release(self, m):
        self.barrier()
        while len(self.stack) > m:
            self.stack.pop().__exit__(None, None, None)

    def dma_dyn(self, e, out, in_fn, in_static, key):
        deps = self._deps([in_static], [out])
        w = self._waits(e, deps)
        k = ("dma", key)
        self.dma_cnt[k] = self.dma_cnt.get(k, 0) + 16
        tok = (k, self.dma_cnt[k])
        self.q[e].append((lambda en: en.dma_start(out=out, in_=in_fn(en)), w, (k, 16)))
        self._record(tok, [in_static], [out])
        self.nops += 1

    def cc(self, kind, src, dst, groups):
        deps = self._deps([src], [dst])
        w = self._waits("pool", deps)
        k = ("dma", "cc")
        self.dma_cnt[k] = self.dma_cnt.get(k, 0) + 1
        tok = (k, self.dma_cnt[k])
        self.q["pool"].append((lambda en: en.collective_compute(kind, ALU.bypass, replica_groups=groups,
                                                                ins=[src], outs=[dst]), w, (k, 1)))
        self._record(tok, [src], [dst])
        self.nops += 1

    def wait_all(self, e):
        w = []
        for k, v in self.dma_cnt.items():
            if self.waited[e].get(k, 0) < v:
                self.waited[e][k] = v
                w.append((k, v))
        self.q[e].append((None, w, None))

    def emit(self):
        nc = self.nc
        for e in self.eng:
            self.sem(e)
        for k in self.dma_cnt:
            self.sem(k)
        with nc.Block() as block:
            def mk(e):
                def body(en):
                    for fn, w, inc in self.q[e]:
                        for s, v in w:
                            en.wait_ge(self.sems[s], v)
                        if fn is not None:
                            ins = fn(en)
                            ins.then_inc(self.sems[inc[0]], inc[1])
                return body
            block.tensor(mk("pe"))
            block.vector(mk("dve"))
            block.scalar(mk("act"))
            block.gpsimd(mk("pool"))
            block.sync(mk("sp"))
        while self.stack:
            self.stack.pop().__exit__(None, None, None)

    def mm(self, out, lhsT, rhs, start=True, stop=True, **kw):
        self.op("pe", lambda en: en.matmul(out, lhsT, rhs, start=start, stop=stop, **kw),
                [lhsT, rhs], [out])

    def tr(self, out, in_, ident):
        self.op("pe", lambda en: en.transpose(out, in_, ident), [in_, ident], [out])

    def act(self, out, in_, func, bias=None, scale=None, accum_out=None, eng="act"):
        kw = {}
        rd = [in_]
        if bias is not None:
            kw["bias"] = bias
            if not isinstance(bias, (int, float)):
                rd.append(bias)
        if scale is not None:
            kw["scale"] = scale
            if not isinstance(scale, (int, float)):
                rd.append(scale)
        wr = [out]
        if accum_out is not None:
            kw["accum_out"] = accum_out
            wr.append(accum_out)
        self.op("act", lambda en: en.activation(out=out, in_=in_, func=func, **kw), rd, wr)

    def tt(self, out, in0, in1, op, eng="dve"):
        self.op(eng, lambda en: en.tensor_tensor(out=out, in0=in0, in1=in1, op=op), [in0, in1], [out])

    def ts(self, out, in0, s1, s2=None, op0=ALU.mult, op1=None, eng="dve", accum_out=None):
        rd = [in0]
        if not isinstance(s1, (int, float)):
            rd.append(s1)
        if s2 is not None and not isinstance(s2, (int, float)):
            rd.append(s2)
        kw = {}
        wr = [out]
        if op1 is not None:
            kw["op1"] = op1
        if accum_out is not None:
            kw["accum_out"] = accum_out
            wr.append(accum_out)
        self.op(eng, lambda en: en.tensor_scalar(out=out, in0=in0, scalar1=s1, scalar2=s2, op0=op0, **kw), rd, wr)

    def stt(self, out, in0, scalar, in1, op0, op1, accum_out=None):
        rd = [in0, in1]
        if not isinstance(scalar, (int, float)):
            rd.append(scalar)
        kw = {}
        wr = [out]
        if accum_out is not None:
            kw["accum_out"] = accum_out
            wr.append(accum_out)
        self.op("dve", lambda en: en.scalar_tensor_tensor(out=out, in0=in0, scalar=scalar, in1=in1, op0=op0, op1=op1, **kw), rd, wr)

    def copy(self, out, in_, eng="dve"):
        if eng == "act":
            self.op("act", lambda en: en.copy(out=out, in_=in_), [in_], [out])
        else:
            self.op(eng, lambda en: en.tensor_copy(out=out, in_=in_), [in_], [out])

    def memset(self, ap, val, eng="dve"):
        self.op(eng, lambda en: en.memset(ap, val), [], [ap])

    def recip(self, out, in_):
        self.op("dve", lambda en: en.reciprocal(out=out, in_=in_), [in_], [out])

    def reduce(self, out, in_, op=ALU.add, axis=AX.X):
        self.op("dve", lambda en: en.tensor_reduce(out=out, in_=in_, axis=axis, op=op), [in_], [out])


D = 2048
KC = 16
FFN = 5504
FC = 43
ALPHA = (2.0 * 2) ** 0.25
LN_EPS = 1e-5


def make_ident(p):
    ident = p.sb("ident", [128, 128])
    p.memset(ident[:], 1.0, eng="pool")
    p.op("pool", lambda en: en.affine_select(out=ident[:], in_=ident[:], pattern=[[-1, 128]],
                                              compare_op=ALU.is_equal, fill=0.0, base=0, channel_multiplier=1),
         [ident[:]], [ident[:]])
    return ident


def layer_norm_tile(p, out, xin, nb, gt, bt, st, mv, rstd, eps=LN_EPS):
    for c in range(4):
        p.op("dve", lambda en, c=c: en.bn_stats(out=st[0:nb, c, :], in_=xin[0:nb, c * 512:(c + 1) * 512]),
             [xin[0:nb, c * 512:(c + 1) * 512]], [st[0:nb, c, :]])
    p.op("dve", lambda en: en.bn_aggr(out=mv[0:nb, :], in_=st[0:nb].rearrange("p a b -> p (a b)")),
         [st[0:nb]], [mv[0:nb, :]])
    p.act(rstd[0:nb, :], mv[0:nb, 1:2], AF.Sqrt, bias=eps, scale=1.0)
    p.recip(rstd[0:nb, :], rstd[0:nb, :])
    p.ts(out[0:nb, :], xin[0:nb, :], mv[0:nb, 0:1], rstd[0:nb, 0:1], op0=ALU.subtract, op1=ALU.mult)
    p.tt(out[0:nb, :], out[0:nb, :], gt[0:nb, :], ALU.mult)
    p.tt(out[0:nb, :], out[0:nb, :], bt[0:nb, :], ALU.add)


def transpose_to_bf16(p, dstT, src, nb, ident, pst, tok0):
    for c4 in range(4):
        ps = pst[c4 % 2]
        for j in range(4):
            kc = c4 * 4 + j
            p.tr(ps[:, j * 128:j * 128 + nb], src[0:nb, kc * 128:(kc + 1) * 128], ident[0:nb, 0:nb])
        p.copy(dstT[:, c4 * 4:(c4 + 1) * 4, tok0:tok0 + nb],
               ps[:].rearrange("p (a b) -> p a b", a=4)[:, :, 0:nb], eng="act")


def emit_B(p, nc, ident, PS, TC, sfx, io, last, TB=512):
    T = TC

    def din(name, shape, dt=F32):
        return nc.dram_tensor(name + sfx, list(shape), dt, kind="ExternalInput").ap()

    hmask_d = io["hmask"]
    wg_d = din("wg", [D, 3 * D])
    pj_d = [din(n, [D, D]) for n in ("p_ssd", "p_rwkv", "p_ml")]
    wo_d = din("w_out", [D, D])
    ln_d = [din(n, [1, D]) for n in ("ln1_g", "ln1_b", "ln2_g", "ln2_b")]
    wup_d = din("w_up", [D, 2 * FFN])
    cw_d = din("cw", [128, 2 * FC * 3])
    cb_d = din("cb", [128, 2 * FC])
    wdn_d = din("w_down", [FFN, D])
    h_loc, hT_loc, htail_loc = io["h_loc"], io["hT_loc"], io["htail_loc"]
    hT_all, htail_all, y_all = io["hT_all"], io["htail_all"], io["y_all"]
    ho_d = io["out"] if last else h_loc
    NG = 4

    rank_cache = io.setdefault("rank_cache", {})

    def q_of(en):
        k = ("q", id(en))
        if k not in rank_cache:
            rank_cache[k] = en.partition_id() % NG
        return rank_cache[k]

    def rr_of(en):
        k = ("rr", id(en))
        if k not in rank_cache:
            rank_cache[k] = (en.partition_id() + (NG - 1)) % NG
        return rank_cache[k]

    lnt = [p.sb(f"lnt{i}", [128, D]) for i in range(2)]
    cw = p.sb("cw", [128, 2 * FC, 3])
    cb = p.sb("cb", [128, 2 * FC])
    hmask = p.sb("hmask", [128, 1])
    p.dma("sp", cw[:].rearrange("p c j -> p (c j)"), cw_d[:, :], "c4")
    p.dma("sp", cb[:], cb_d[:, :], "c5")
    p.dma("sp", hmask[:], hmask_d[:, :], "c6")
    tails = p.sb("tails", [128, 2 * FC, 2])

    st = p.sb("st", [128, 4, 6])
    mv = p.sb("mv", [128, 2])
    rstd = p.sb("rstd", [128, 1])

    WS = [p.sb(f"ws{i}", [128, KC, 256]) for i in range(2)]
    WB = [p.sb(f"wb{i}", [128, KC, 256], BF16) for i in range(2)]
    hT = p.sb("hTb", [128, KC, TB], BF16)
    h1T = hT
    ybig = p.sb("ybig", [128, 3 * KC * TB], BF16)
    yT = [ybig[:, i * KC * TB:(i + 1) * KC * TB].rearrange("p (k t) -> p k t", k=KC) for i in range(3)]
    actT = ybig[:, 0:FC * TB].rearrange("p (k t) -> p k t", k=FC)
    mT = p.sb("mT", [128, KC, TB], BF16)
    htm = [p.sb(f"htm{i}", [128, D]) for i in range(4)]
    G = p.sb("G", [128, 9 * TB])
    gst = [G[:, i * TB:(i + 1) * TB] for i in range(6)]
    gsb = [G[:, (6 + i) * TB:(7 + i) * TB] for i in range(3)]
    tmp = [G[:, 0:D], G[:, D:2 * D]]
    uext = [p.sb(f"uext{i}", [128, TB + 2]) for i in range(2)]
    wcnt = [0]

    def load_ln(i0):
        for i in range(2):
            p.dma("sp", lnt[i][:], ln_d[i0 + i][0:1, :].partition_broadcast(128), f"ln{i}")

    def load_panel(wd, col0, ncols, krows=None):
        s = wcnt[0] % 2
        wcnt[0] += 1
        p.dma("sp", WS[s][:, :, 0:ncols], wd.rearrange("(k p) n -> p k n", p=128)[:, :, col0:col0 + ncols], f"ws{s}")
        eng = "act" if s == 0 else "dve"
        p.copy(WB[s][:, :, 0:ncols], WS[s][:, :, 0:ncols], eng=eng)
        return WB[s]

    blocks = [(0, 2)] + [(2 + b * TB, min(TB, T - b * TB)) for b in range((T + TB - 1) // TB)]
    for bi, (t0, nb) in enumerate(blocks):
        halo = bi == 0
        ntile = (nb + 127) // 128
        lt0 = t0 - 2
        HC = io["HC"]; NCHL = TC // HC; TT = io["TT"]; NPR = TC // TT
        if halo:
            p.dma_dyn("act", hT[:, :, 0:2],
                      lambda en: hT_all[bass.DynSlice(rr_of(en) * D + (NCHL - 1) * NG * D, D), HC - 2:HC].rearrange("(k p) t -> p k t", p=128),
                      hT_all, "hT")
            for i in range(3):
                p.dma_dyn("sp", yT[i][:, :, 0:2],
                          lambda en, i=i: y_all[i][bass.DynSlice((rr_of(en) * NPR + NPR - 1) * D, D), TT - 2:TT].rearrange("(k p) t -> p k t", p=128),
                          y_all[i], f"yl{i}")
        else:
            for o_ in range(0, nb, HC):
                n_ = min(HC, nb - o_)
                ch_, col_ = divmod(lt0 + o_, HC)
                p.dma("act", hT[:, :, o_:o_ + n_], hT_loc[ch_ * D:(ch_ + 1) * D, col_:col_ + n_].rearrange("(k p) t -> p k t", p=128), "hT")
            tcl, colb = divmod(lt0, TT)
            for i in range(3):
                p.dma_dyn("sp", yT[i][:, :, 0:nb],
                          lambda en, i=i, tcl=tcl, colb=colb, nb=nb: y_all[i][bass.DynSlice((q_of(en) * NPR + tcl) * D, D), colb:colb + nb].rearrange("(k p) t -> p k t", p=128),
                          y_all[i], f"yl{i}")
        for oc2 in range(D // 256):
            for i in range(3):
                wb = load_panel(wg_d, i * D + oc2 * 256, 256)
                for sub in range(2):
                    ps = PS[sub * 3 + i]
                    for kc in range(KC):
                        p.mm(ps[:, 0:nb], wb[:, kc, sub * 128:(sub + 1) * 128], hT[:, kc, 0:nb],
                             start=(kc == 0), stop=(kc == KC - 1))
            for sub in range(2):
                for i in range(3):
                    p.act(gst[sub * 3 + i][:, 0:nb], PS[sub * 3 + i][:, 0:nb], AF.Sigmoid)
            for i in range(3):
                wb = load_panel(pj_d[i], oc2 * 256, 256)
                for sub in range(2):
                    ps = PS[sub * 3 + i]
                    for kc in range(KC):
                        p.mm(ps[:, 0:nb], wb[:, kc, sub * 128:(sub + 1) * 128], yT[i][:, kc, 0:nb],
                             start=(kc == 0), stop=(kc == KC - 1))
            for sub in range(2):
                oc = oc2 * 2 + sub
                a = gsb[0]
                b_ = gsb[1]
                p.tt(a[:, 0:nb], gst[sub * 3 + 0][:, 0:nb], PS[sub * 3 + 0][:, 0:nb], ALU.mult)
                p.tt(b_[:, 0:nb], gst[sub * 3 + 1][:, 0:nb], PS[sub * 3 + 1][:, 0:nb], ALU.mult)
                p.tt(a[:, 0:nb], a[:, 0:nb], b_[:, 0:nb], ALU.add)
                p.tt(b_[:, 0:nb], gst[sub * 3 + 2][:, 0:nb], PS[sub * 3 + 2][:, 0:nb], ALU.mult)
                p.tt(mT[:, oc, 0:nb], a[:, 0:nb], b_[:, 0:nb], ALU.add)
        for ti in range(ntile):
            tn = min(128, nb - ti * 128)
            if halo:
                p.dma_dyn("act", htm[ti][0:tn, :], lambda en: htail_all[bass.DynSlice(rr_of(en) * 2, 2), :], htail_all, f"htm{ti}")
            else:
                p.dma("act", htm[ti][0:tn, :], h_loc[lt0 + ti * 128:lt0 + ti * 128 + tn, :], f"htm{ti}")
        for cg2 in range(D // 256):
            wb = load_panel(wo_d, cg2 * 256, 256)
            for ti in range(ntile):
                tn = min(128, nb - ti * 128)
                ps = PS[ti]
                for kc in range(KC):
                    p.mm(ps[0:tn, 0:256], mT[:, kc, ti * 128:ti * 128 + tn], wb[:, kc, 0:256],
                         start=(kc == 0), stop=(kc == KC - 1))
                sl = slice(cg2 * 256, (cg2 + 1) * 256)
                p.stt(htm[ti][0:tn, sl], htm[ti][0:tn, sl], ALPHA, ps[0:tn, 0:256], ALU.mult, ALU.add)
        load_ln(0)
        for ti in range(ntile):
            tn = min(128, nb - ti * 128)
            layer_norm_tile(p, htm[ti], htm[ti], tn, lnt[0], lnt[1], st, mv, rstd)
            transpose_to_bf16(p, h1T, htm[ti], tn, ident, PS[6:8], ti * 128)
        for c2 in range(0, FC, 2):
            ncol = min(2, FC - c2) * 128
            wbs = [load_panel(wup_d, half * FFN + c2 * 128, ncol) for half in range(2)]
            for sub in range(ncol // 128):
                c = c2 + sub
                accs = []
                for half in range(2):
                    ch = half * FC + c
                    ps = PS[half * 2 + sub]
                    for kc in range(KC):
                        p.mm(ps[:, 0:nb], wbs[half][:, kc, sub * 128:(sub + 1) * 128], h1T[:, kc, 0:nb],
                             start=(kc == 0), stop=(kc == KC - 1))
                    ue = uext[half]
                    if halo:
                        p.ts(tails[:, ch, :], ps[:, 0:2], hmask[:, 0:1], None, op0=ALU.mult)
                        continue
                    p.copy(ue[:, 2:2 + nb], ps[:, 0:nb], eng="act")
                    p.copy(ue[:, 0:2], tails[:, ch, :], eng="act")
                    acc = gsb[half]
                    p.ts(acc[:, 0:nb], ue[:, 2:2 + nb], cw[:, ch, 2:3], cb[:, ch:ch + 1], op0=ALU.mult, op1=ALU.add)
                    p.stt(acc[:, 0:nb], ue[:, 1:1 + nb], cw[:, ch, 1:2], acc[:, 0:nb], ALU.mult, ALU.add)
                    p.stt(acc[:, 0:nb], ue[:, 0:nb], cw[:, ch, 0:1], acc[:, 0:nb], ALU.mult, ALU.add)
                    p.copy(tails[:, ch, :], ue[:, nb:nb + 2], eng="act")
                    accs.append(acc)
                if halo:
                    continue
                p.act(gsb[2][:, 0:nb], accs[0][:, 0:nb], AF.Gelu)
                p.tt(actT[:, c, 0:nb], gsb[2][:, 0:nb], accs[1][:, 0:nb], ALU.mult)
        if halo:
            continue
        for cg2 in range(D // 256):
            sl = slice(cg2 * 256, (cg2 + 1) * 256)
            for k0 in range(0, FC, KC):
                kn = min(KC, FC - k0)
                s = wcnt[0] % 2
                wcnt[0] += 1
                p.dma("sp", WS[s][:, 0:kn, 0:256],
                      wdn_d.rearrange("(k p) n -> p k n", p=128)[:, k0:k0 + kn, cg2 * 256:(cg2 + 1) * 256], f"ws{s}")
                p.copy(WB[s][:, 0:kn, 0:256], WS[s][:, 0:kn, 0:256], eng="act" if s == 0 else "dve")
                for ti in range(ntile):
                    tn = min(128, nb - ti * 128)
                    ps = PS[ti]
                    for k in range(kn):
                        p.mm(ps[0:tn, 0:256], actT[:, k0 + k, ti * 128:ti * 128 + tn], WB[s][:, k, 0:256],
                             start=(k0 + k == 0), stop=(k0 + k == FC - 1))
            for ti in range(ntile):
                tn = min(128, nb - ti * 128)
                p.stt(htm[ti][0:tn, sl], htm[ti][0:tn, sl], ALPHA, PS[ti][0:tn, 0:256], ALU.mult, ALU.add)
        load_ln(2)
        for ti in range(ntile):
            tn = min(128, nb - ti * 128)
            o = tmp[ti % 2]
            layer_norm_tile(p, o, htm[ti], tn, lnt[0], lnt[1], st, mv, rstd)
            tg = t0 - 2 + ti * 128
            p.dma("pool", ho_d[tg:tg + tn, :], o[0:tn, :], f"ho{ti % 2}")
            if (not last) and tg + tn == TC:
                p.dma("pool", htail_loc[0:2, :], o[tn - 2:tn, :], "htl")
            transpose_to_bf16(p, h1T, o, tn, ident, PS[6:8], ti * 128)
        if not last:
            for o_ in range(0, nb, HC):
                n_ = min(HC, nb - o_)
                ch_, col_ = divmod(lt0 + o_, HC)
                p.dma("pool", hT_loc[ch_ * D:(ch_ + 1) * D, col_:col_ + n_].rearrange("(k p) t -> p k t", p=128), h1T[:, :, o_:o_ + n_], "hTo")
                if io["overlap"] and col_ + n_ == HC:
                    io["xchg_h"](ch_)
            if io["overlap"] and lt0 + nb == TC:
                io["xchg_tail"]()


RTB = 256
CH = 32
NEG_E = -0.6065306597126334


def emit_rwkv(p, c, nc, T, ld_, PS, warena, hT_src, sfx, y_d):
    def din(name, shape, dt=F32):
        return nc.dram_tensor(name, list(shape), dt, kind="ExternalInput").ap()

    TB = RTB
    NW = 2048
    w_d = din("rw_w" + sfx, [D, NW])
    mu_d = din("rw_mu" + sfx, [128, 16])
    pc_d = din("rw_pc" + sfx, [128, 20])
    w2_d = din("rw_w2" + sfx, [96, 512]); a2_d = din("rw_a2" + sfx, [96, 512]); g2_d = din("rw_g2" + sfx, [256, 512])
    lnw_d = din("rw_lnw" + sfx, [128, 128]); lnb_d = din("rw_lnb" + sfx, [128, 128])

    w = warena[:, 0:KC * NW].rearrange("p (k n) -> p k n", k=KC)
    ld_.load(w_d, w, NW)
    mu = p.sb("r_mu", [128, 16]); pc = p.sb("r_pc", [128, 5, 4])
    w2 = p.sb("r_w2", [96, 512]); a2 = p.sb("r_a2", [96, 512])
    g2f = p.sb("r_g2f", [128, 2, 512]); g2 = p.sb("r_g2", [128, 2, 512], BF16)
    lnw = p.sb("r_lnw", [128, 2, 64]); lnb = p.sb("r_lnb", [128, 2, 64])
    p.dma("sp", mu[:], mu_d[:, :], "k0")
    p.dma("sp", pc[:].rearrange("p a b -> p (a b)"), pc_d[:, :], "k1")
    p.dma("sp", w2[:], w2_d[:, :], "k2")
    p.dma("sp", a2[:], a2_d[:, :], "k0")
    p.dma("sp", g2f[:], g2_d.rearrange("(k p) n -> p k n", p=128), "k1")
    p.dma("sp", lnw[:].rearrange("p a b -> p (a b)"), lnw_d[:, :], "k2")
    p.dma("sp", lnb[:].rearrange("p a b -> p (a b)"), lnb_d[:, :], "k0")
    p.copy(g2[:], g2f[:])

    def tri_mask(name, base):
        t = p.sb(name, [128, 4, 32])
        p.memset(t[:], 1.0, eng="pool")
        for rb in range(4):
            v = t[32 * rb:32 * rb + 32, :, :]
            p.op("pool", lambda en, v=v: en.affine_select(out=v, in_=v, pattern=[[0, 4], [1, 32]], compare_op=ALU.is_ge,
                                                          fill=0.0, base=base, channel_multiplier=-1), [v], [v])
        return t
    m_su = tri_mask("r_msu", -1)
    m_le = tri_mask("r_mle", 0)
    m_sl = p.sb("r_msl", [128, 4, 32])
    p.memset(m_sl[:], 1.0, eng="pool")
    for rb in range(4):
        v = m_sl[32 * rb:32 * rb + 32, :, :]
        p.op("pool", lambda en, v=v: en.affine_select(out=v, in_=v, pattern=[[0, 4], [-1, 32]], compare_op=ALU.is_ge,
                                                      fill=0.0, base=-1, channel_multiplier=1), [v], [v])
    m4 = p.sb("r_m4", [128, 4, 128])
    for i, m in enumerate((m_su, m_le, m_su, m_le)):
        p.copy(m4[:, i, :], m[:].rearrange("p a b -> p (a b)"), eng="pool")
    E2 = p.sb("r_E2", [128, 64])
    p.copy(E2[0:64, :], c["ident"][0:64, 0:64], eng="pool")
    p.copy(E2[64:128, :], c["ident"][64:128, 64:128], eng="pool")
    BO = p.sb("r_BO", [128, 128])
    p.memset(BO[:], 0.0, eng="pool")
    p.memset(BO[0:64, 0:64], 1.0, eng="pool")
    p.memset(BO[64:128, 64:128], 1.0, eng="pool")
    zb = p.sb("r_zb", [128, 512], BF16)
    p.memset(zb[:], 0.0, eng="pool")
    for i in range(8):
        p.mm(PS[i][:, 0:512], zb[:, 0:128], zb[:, 0:512])
    rmask = p.sb("r_rmask", [128, TB])
    p.memset(rmask[:], 1.0, eng="pool")
    p.memset(rmask[:].rearrange("p (a b) -> p a b", b=CH)[:, :, 0:1], 0.0, eng="pool")

    hT = [p.sb("r_hT0", [128, KC, TB], BF16)] * 2
    pe = [p.sb("r_pe0", [128, TB + 1])] * 2
    ptail = p.sb("r_ptail", [128, 16, 1])
    p.memset(ptail[:], 0.0)
    dtmp = p.sb("r_dtmp", [128, TB])
    tw = p.sb("r_tw", [128, TB]); alo = p.sb("r_alo", [128, TB]); sgb = p.sb("r_sgb", [128, 2, TB], BF16)
    rr = p.sb("r_rr", [128, TB]); kx = p.sb("r_kx", [128, TB]); vv = p.sb("r_vv", [128, TB])
    sg = p.sb("r_sg", [128, TB]); cs = p.sb("r_cs", [128, TB]); ecl = p.sb("r_ecl", [128, 4, TB])
    em = p.sb("r_em", [128, TB]); ecp = p.sb("r_ecp", [128, TB]); kk = p.sb("r_kk", [128, TB])
    t1 = p.sb("r_t1", [128, TB]); asg = p.sb("r_asg", [128, TB]); kh = p.sb("r_kh", [128, TB])
    NCK = TB // CH
    AR = [p.sb(f"r_AR{i}", [128, 2, NCK, 2, CH]) for i in range(4)]
    Bz = [p.sb(f"r_Bz{i}", [128, NCK, 2, CH]) for i in range(4)]
    Kz = [p.sb(f"r_Kz{i}", [128, NCK, 2, CH]) for i in range(4)]
    Vz = [p.sb(f"r_Vz{i}", [128, NCK, 2, CH]) for i in range(4)]
    RKz = [p.sb(f"r_RKz{i}", [128, NCK, 2, CH]) for i in range(4)]
    for lst in (AR, Bz, Kz, Vz, RKz):
        for t in lst:
            p.memset(t[:], 0.0, eng="pool")
    ST = p.sb("r_ST", [128, 4, 64])
    p.memset(ST[:], 0.0)
    gfm = p.sb("r_gfm", [64, 8, TB])
    ystage = [p.sb(f"r_ys{i}", [64, 8, TB], BF16) for i in range(2)]
    N4 = p.sb("r_N4", [128, 4, 128]); NT = p.sb("r_NT", [128, 128])
    Pk = [p.sb(f"r_P{i}", [128, 128]) for i in range(2)]; Qk = [p.sb(f"r_Q{i}", [128, 128]) for i in range(2)]
    Vst = p.sb("r_Vst", [128, 64]); Wk = [p.sb(f"r_W{i}", [128, 64]) for i in range(2)]
    KTx = [p.sb(f"r_KTs{i}", [128, 128]) for i in range(2)]; BTx = [p.sb(f"r_BTs{i}", [128, 128]) for i in range(2)]
    for t_ in KTx + BTx:
        p.memset(t_[:], 0.0, eng="pool")
    stmp = p.sb("r_stmp", [128, 2, 64])
    st6 = p.sb("r_st6", [128, 6]); mv = p.sb("r_mv", [128, 2]); rstd = p.sb("r_rstd", [128, 1])
    Pb16 = p.sb("r_Pb16", [128, 128], BF16); Qb16 = p.sb("r_Qb16", [128, 128], BF16)
    yn = p.sb("r_yn", [128, 64]); Ysb = p.sb("r_Ysb", [128, 64]); bsc = p.sb("r_bsc", [128, 2]); yfin = p.sb("r_yfin", [128, 64])

    w0 = lambda cc: pc[:, 0, cc:cc + 1]
    a0 = lambda cc: pc[:, 1, cc:cc + 1]
    k_k = lambda cc: pc[:, 2, cc:cc + 1]
    k_a = lambda cc: pc[:, 3, cc:cc + 1]
    r_k = lambda cc: pc[:, 4, cc:cc + 1]

    def proj_lerp(hb, ch, out):
        ps = PS[ch % 2]
        for kc in range(KC):
            p.mm(ps[:, 0:TB], w[:, kc, ch * 128:(ch + 1) * 128], hb[:, kc, :], start=(kc == 0), stop=(kc == KC - 1))
        x_ = pe[ch % 2]
        p.copy(x_[:, 1:1 + TB], ps[:, 0:TB], eng="act")
        p.copy(x_[:, 0:1], ptail[:, ch, :], eng="act")
        p.tt(dtmp[:], x_[:, 0:TB], x_[:, 1:1 + TB], ALU.subtract)
        p.stt(out, dtmp[:], mu[:, ch:ch + 1], x_[:, 1:1 + TB], ALU.mult, ALU.add)
        p.copy(ptail[:, ch, :], x_[:, TB:TB + 1], eng="act")

    nblk = T // TB
    H = slice(0, 64), slice(64, 128)
    for b in range(nblk):
        hb = hT[b % 2]
        for off_, ap_ in hT_src(b * TB, TB):
            p.dma("act", hb[:, :, off_:off_ + ap_.shape[2]], ap_, f"r_hT{b % 2}")
        proj_lerp(hb, 12, tw[:])
        p.act(tw[0:96, :], tw[0:96, :], AF.Tanh)
        proj_lerp(hb, 13, alo[:])
        for j in range(2):
            proj_lerp(hb, 14 + j, dtmp[:])
            p.act(sgb[:, j, :], dtmp[:], AF.Sigmoid)
        for hd in range(8):
            ps = PS[6 + hd % 2]
            for j in range(2):
                p.mm(ps[0:64, 0:TB], g2[:, j, hd * 64:(hd + 1) * 64], sgb[:, j, :], start=(j == 0), stop=(j == 1))
            p.copy(gfm[:, hd, :], ps[0:64, 0:TB], eng="act")
        for cc in range(4):
            proj_lerp(hb, cc, rr[:])
            proj_lerp(hb, 4 + cc, kx[:])
            proj_lerp(hb, 8 + cc, vv[:])
            p.mm(PS[6][:, 0:TB], w2[:, cc * 128:(cc + 1) * 128], tw[0:96, :])
            p.act(sg[:], PS[6][:, 0:TB], AF.Sigmoid, bias=w0(cc), scale=1.0)
            p.op("dve", lambda en: en.tensor_tensor_scan(out=cs[:], data0=rmask[:], data1=sg[:], initial=0.0,
                                                         op0=ALU.mult, op1=ALU.add), [rmask[:], sg[:]], [cs[:]])
            p.act(ecl[:, cc, :], cs[:], AF.Exp, scale=NEG_E)
            p.act(em[:], cs[:], AF.Exp, scale=-NEG_E)
            p.tt(t1[:], cs[:], sg[:], ALU.subtract)
            p.act(ecp[:], t1[:], AF.Exp, scale=NEG_E)
            p.mm(PS[7][:, 0:TB], a2[:, cc * 128:(cc + 1) * 128], alo[0:96, :])
            p.act(asg[:], PS[7][:, 0:TB], AF.Sigmoid, bias=a0(cc), scale=1.0)
            p.ts(kk[:], kx[:], k_k(cc), None, op0=ALU.mult)
            p.tt(t1[:], kk[:], kk[:], ALU.mult)
            p.mm(PS[6][:, 0:TB], BO[:], t1[:])
            p.ts(t1[:], PS[6][:, 0:TB], 1e-24, None, op0=ALU.max)
            p.act(t1[:], t1[:], AF.Sqrt)
            p.recip(t1[:], t1[:])
            p.tt(kk[:], kk[:], t1[:], ALU.mult)
            p.ts(t1[:], asg[:], -1.0, k_a(cc), op0=ALU.add, op1=ALU.mult)
            p.stt(kh[:], t1[:], 1.0, kx[:], ALU.add, ALU.mult)
            c3 = lambda ap: ap.rearrange("p (a b) -> p a b", b=CH)
            for h2 in range(2):
                hs = H[h2]
                p.stt(AR[cc][hs, 0, :, h2, :], c3(kk[hs, :]), -1.0, c3(ecp[hs, :]), ALU.mult, ALU.mult)
                p.tt(AR[cc][hs, 1, :, h2, :], c3(rr[hs, :]), c3(ecl[hs, cc, :]), ALU.mult)
                p.tt(t1[hs, :], kk[hs, :], asg[hs, :], ALU.mult)
                p.tt(Bz[cc][hs, :, h2, :], c3(t1[hs, :]), c3(em[hs, :]), ALU.mult)
                p.tt(Kz[cc][hs, :, h2, :], c3(kh[hs, :]), c3(em[hs, :]), ALU.mult)
                p.copy(Vz[cc][hs, :, h2, :], c3(vv[hs, :]), eng="act")
                p.stt(RKz[cc][hs, :, h2, :], c3(rr[hs, :]), r_k(cc)[hs, :], c3(kh[hs, :]), ALU.mult, ALU.mult)
        ysb = ystage[b % 2]
        import os
        STG = int(os.environ.get('RW_STAGE', '9'))
        for ci in range(TB // CH if STG > 0 else 0):
            tk = slice(ci * CH, (ci + 1) * CH)
            for g in range(2):
                cs2 = (2 * g, 2 * g + 1)
                for cl, cc in enumerate(cs2):
                    rhsAR = AR[cc][:, :, ci, :, :].rearrange("p m a b -> p m (a b)")
                    o1 = PS[2][64 * cl:64 * cl + 64, :].rearrange("p (m c x) -> p m c x", m=4, c=2)
                    p.mm(o1[:, 0:2, cl, :], Bz[cc][:, ci, :, :].rearrange("p a b -> p (a b)"), rhsAR)
                    p.mm(o1[:, 2:4, cl, :], Kz[cc][:, ci, :, :].rearrange("p a b -> p (a b)"), rhsAR)
                    p.mm(PS[3][64 * cl:64 * cl + 64, 64 * cl:64 * cl + 64], AR[cc][:, 0, ci, :, :].rearrange("p a b -> p (a b)"), Bz[cc][:, ci, :, :].rearrange("p a b -> p (a b)"))
                    p.mm(PS[4][64 * cl:64 * cl + 64, 0:64], Vz[cc][:, ci, :, :].rearrange("p a b -> p (a b)"), E2[:])
                p.tt(N4[:].rearrange("p a b -> p (a b)"), PS[2][:, 0:512], m4[:].rearrange("p a b -> p (a b)"), ALU.mult)
                p.tt(NT[:], PS[3][:, 0:128], m_sl[:].rearrange("p a b -> p (a b)"), ALU.mult)
                p.copy(Vst[:], PS[4][:, 0:64], eng="act")
                Nab, Nrb, Nak, Nrk = N4[:, 0, :], N4[:, 1, :], N4[:, 2, :], N4[:, 3, :]
                if STG < 2:
                    continue
                SUB = int(os.environ.get('RW_SUB', '9'))
                for cl, cc in enumerate(cs2):
                    hs = slice(64 * cl, 64 * cl + 64)
                    p.mm((PS[7][hs, 256:320] if SUB == 0 else PS[4][hs, 64:128]), AR[cc][:, 0, ci, :, :].rearrange("p a b -> p (a b)"), ST[:, cc, :])
                if SUB >= 2:
                    p.mm(PS[4][:, 320:384], Nak, Vst[:])
                if SUB >= 3:
                    p.copy(Wk[0][:], PS[4][:, 64:128], eng="act")
                if SUB >= 4:
                    p.tt(Wk[0][:], Wk[0][:], PS[4][:, 320:384], ALU.add)
                if STG < 3:
                    continue
                Pc, Qc = Nab, NT[:]
                wi = 0
                QPS = PS[7][:, 384:512] if os.environ.get('RW_QPS', '1') == '1' else PS[3][:, 256:384]
                RK = int(os.environ.get('RW_K', '5')); RQ = int(os.environ.get('RW_Q', '9'))
                for k in range(min(5, RK)):
                    p.mm(PS[5][:, 0:64], Pc, Wk[wi][:])
                    p.tt(Wk[1 - wi][:], Wk[wi][:], PS[5][:, 0:64], ALU.add)
                    wi = 1 - wi
                    if RQ < 1:
                        continue
                    if k < 4:
                        if os.environ.get('RW_BF', '0') == '1':
                            p.copy(Pb16[:], Pc); p.copy(Qb16[:], Qc, eng='pool')
                            Pm, Qm = Pb16[:], Qb16[:]
                        else:
                            Pm, Qm = Pc, Qc
                        p.mm(PS[3][:, 128:256], Qm, Pm)
                        if k < 3 and RQ != 7:
                            p.mm(QPS, Pm, Qm)
                        Pn = Pk[k % 2]
                        p.copy(Pn[:], PS[3][:, 128:256], eng="act")
                        if k < 3 and RQ != 7:
                            Qn = Qk[k % 2]
                            p.copy(Qn[:], QPS, eng=('act' if RQ == 8 else 'dve'))
                            Qc = Qn[:]
                        Pc = Pn[:]
                if STG < 4:
                    continue
                U = Wk[wi]
                for cl, cc in enumerate(cs2):
                    hs = slice(64 * cl, 64 * cl + 64)
                    p.mm(PS[4][hs, 128:192], AR[cc][:, 1, ci, :, :].rearrange("p a b -> p (a b)"), ST[:, cc, :])
                p.mm(PS[4][:, 384:448], Nrk, Vst[:])
                p.mm(PS[4][:, 448:512], Nrb, U[:])
                p.copy(Ysb[:], PS[4][:, 128:192], eng="act")
                p.tt(Ysb[:], Ysb[:], PS[4][:, 384:448], ALU.add)
                p.tt(Ysb[:], Ysb[:], PS[4][:, 448:512], ALU.add)
                for cl, cc in enumerate(cs2):
                    p.mm(PS[4][64 * cl:64 * cl + 64, 192:194], RKz[cc][:, ci, :, :].rearrange("p a b -> p (a b)"), c["ones"][:, 0:2])
                if STG < 5:
                    continue
                for cl, cc in enumerate(cs2):
                    p.mm(PS[5][64 * cl:64 * cl + 64, 128:256], Kz[cc][:, ci, :, :].rearrange("p a b -> p (a b)"), c["ident"][:])
                    p.mm(PS[7][64 * cl:64 * cl + 64, 256:384], Bz[cc][:, ci, :, :].rearrange("p a b -> p (a b)"), c["ident"][:])
                for cl in range(2):
                    hs = slice(64 * cl, 64 * cl + 64)
                    p.copy(KTx[cl][hs, :], PS[5][hs, 128:256], eng="act")
                    p.copy(BTx[cl][hs, :], PS[7][hs, 256:384])
                if STG < 6:
                    continue
                SUk = (PS[6][:, 256:320], PS[6][:, 320:384]); SUb = (PS[6][:, 384:448], PS[6][:, 448:512])
                for cl, cc in enumerate(cs2):
                    p.mm(SUk[cl], KTx[cl][:], Vst[:])
                    p.mm(SUb[cl], BTx[cl][:], U[:])
                if STG < 7:
                    continue
                p.op("dve", lambda en: en.bn_stats(out=st6[:], in_=Ysb[:]), [Ysb[:]], [st6[:]])
                p.op("dve", lambda en: en.bn_aggr(out=mv[:], in_=st6[:]), [st6[:]], [mv[:]])
                p.act(rstd[:], mv[:, 1:2], AF.Sqrt, bias=64e-5, scale=1.0)
                p.recip(rstd[:], rstd[:])
                p.ts(yn[:], Ysb[:], mv[:, 0:1], rstd[:, 0:1], op0=ALU.subtract, op1=ALU.mult)
                p.tt(yn[:], yn[:], lnw[:, g, :], ALU.mult)
                p.tt(yn[:], yn[:], lnb[:, g, :], ALU.add)
                p.copy(bsc[:], PS[4][:, 192:194], eng="act")
                p.stt(yfin[:], Vst[:], bsc[:, 0:1], yn[:], ALU.mult, ALU.add)
                p.tr(PS[5][0:64, 384:512], yfin[:], c["ident"][:])
                p.tt(ysb[:, 4 * g:4 * g + 4, tk], PS[5][0:64, 384:512].rearrange("p (h t) -> p h t", h=4),
                     gfm[:, 4 * g:4 * g + 4, tk], ALU.mult)
                for cl, cc in enumerate(cs2):
                    p.tt(stmp[:, cl, :], ST[:, cc, :], SUk[cl], ALU.add)
                    p.tt(stmp[:, cl, :], stmp[:, cl, :], SUb[cl], ALU.add)
                for cl, cc in enumerate(cs2):
                    te = ci * CH + CH - 1
                    p.ts(ST[:, cc, :], stmp[:, cl, :], ecl[:, cc, te:te + 1], None, op0=ALU.mult)
        p.dma("pool", y_d(b * TB, TB).rearrange("(h i) t -> i h t", i=64), ysb[:], f"r_yo{b % 2}")


TB = 512


def make_consts(p):
    c = {}
    c["ident"] = make_ident(p)
    for name, pat, cm, base in (("le", [[1, 128]], -1, 0), ("gt", [[-1, 128]], 1, -1)):
        t = p.sb("mask_" + name, [128, 128])
        p.memset(t[:], 1.0, eng="pool")
        p.op("pool", lambda en, t=t, pat=pat, cm=cm, base=base: en.affine_select(
            out=t[:], in_=t[:], pattern=pat, compare_op=ALU.is_ge, fill=0.0, base=base, channel_multiplier=cm),
            [t[:]], [t[:]])
        c[name] = t
    ones = p.sb("ones", [128, 128])
    p.memset(ones[:], 1.0, eng="pool")
    c["ones"] = ones
    return c


class Loader:
    def __init__(self, p):
        self.p = p
        self.WS = [p.sb(f"lws{i}", [128, KC, 128]) for i in range(2)]
        self.n = 0

    def load(self, wd, dst, N, kc=KC):
        p = self.p
        for n0 in range(0, N, 128):
            nn = min(128, N - n0)
            s = self.n % 2
            self.n += 1
            p.dma("sp", self.WS[s][:, 0:kc, 0:nn], wd.rearrange("(k p) n -> p k n", p=128)[:, :, n0:n0 + nn], f"lws{s}")
            p.copy(dst[:, :, n0:n0 + nn], self.WS[s][:, 0:kc, 0:nn], eng="act" if s == 0 else "dve")


def gla_chunk(p, c, g, QT, KT, Kt, V, ld, nh, dvh, Yout):
    nhp = max(nh, 2)
    W = nh * dvh
    PS = g["PS"]
    p.mm(PS[0][:, 0:128], KT, QT)
    p.tt(g["sTm"][:], PS[0][:, 0:128], c["le"][:], ALU.mult)
    for h in range(nh):
        p.act(g["rhs_all"][:, h, :], c["le"][:], AF.Identity, scale=ld[:, h:h + 1])
    for h0 in range(0, nh, 4):
        hn = min(4, nh - h0)
        ps = PS[1 + h0 // 4]
        p.mm(ps[:, 0:hn * 128], c["gt"][:], g["rhs_all"][:, h0:h0 + hn, :].rearrange("p h l -> p (h l)"))
        p.act(g["dec"][:, h0:h0 + hn, :].rearrange("p h l -> p (h l)"), ps[:, 0:hn * 128], AF.Exp)
    sm = PS[3]
    p.mm(sm[:, 0:nhp], c["le"][:], ld[:, 0:nhp])
    p.mm(sm[:, 8:8 + nhp], c["gt"][:], ld[:, 0:nhp])
    p.mm(sm[:, 16:16 + nhp], c["ones"][:], ld[:, 0:nhp])
    p.act(g["sme"][:, 0:24], sm[:, 0:24], AF.Exp)
    ea = g["sme"][:, 0:nh]
    dte = g["sme"][:, 8:8 + nh]
    cdb = g["sme"][:, 16:16 + nh]
    p.tt(g["MT"][:, 0:nh, :], g["dec"][:, 0:nh, :], g["sTm"][:].unsqueeze(1).broadcast_to([128, nh, 128]), ALU.mult)
    for h in range(nh):
        p.mm(PS[4][:, h * dvh:(h + 1) * dvh] if W <= 512 else PS[4][:, 0:dvh], g["MT"][:, h, :], V[:, h * dvh:(h + 1) * dvh])
    p.mm(PS[5][:, 0:W], QT, g["Sb"][:, 0:W])
    v3 = lambda ap: ap.rearrange("p (h d) -> p h d", h=nh)
    p.tt(v3(g["tmpY"][:, 0:W]), v3(PS[5][:, 0:W]), ea.unsqueeze(2).broadcast_to([128, nh, dvh]), ALU.mult)
    p.tt(Yout[:, 0:W], g["tmpY"][:, 0:W], PS[4][:, 0:W], ALU.add)
    p.tt(v3(g["Vw"][:, 0:W]), v3(V[:, 0:W]), dte.unsqueeze(2).broadcast_to([128, nh, dvh]), ALU.mult)
    p.mm(PS[6][:, 0:W], Kt, g["Vw"][:, 0:W])
    p.tt(v3(g["S"][:, 0:W]), v3(g["S"][:, 0:W]), cdb.unsqueeze(2).broadcast_to([128, nh, dvh]), ALU.mult)
    p.tt(g["S"][:, 0:W], g["S"][:, 0:W], PS[6][:, 0:W], ALU.add)
    p.copy(g["Sb"][:, 0:W], g["S"][:, 0:W], eng="act")


def gla_alloc(p, tag, nh, dvh, PS):
    W = nh * dvh
    g = {"PS": PS}
    g["sTm"] = p.dual(tag + "sTm", [128, 128])
    g["rhs_all"] = p.dual(tag + "rhs", [128, nh, 128])
    g["dec"] = p.dual(tag + "dec", [128, nh, 128])
    g["sme"] = p.dual(tag + "sme", [128, 24])
    g["MT"] = p.dual(tag + "MT", [128, nh, 128], BF16)
    g["tmpY"] = p.dual(tag + "tmpY", [128, W])
    g["Vw"] = p.dual(tag + "Vw", [128, W], BF16)
    g["S"] = p.sb(tag + "S", [128, W])
    g["Sb"] = p.sb(tag + "Sb", [128, W], BF16)
    p.memset(g["S"][:], 0.0)
    p.memset(g["Sb"][:], 0.0)
    return g


def conv_fm(p, out_acc, xe, nb, cwt, cbt, ch, ntap):
    p.ts(out_acc[:, 0:nb], xe[:, ntap - 1:ntap - 1 + nb], cwt[:, ch, ntap - 1:ntap], cbt[:, ch:ch + 1], op0=ALU.mult, op1=ALU.add)
    for j in range(ntap - 2, -1, -1):
        p.stt(out_acc[:, 0:nb], xe[:, j:j + nb], cwt[:, ch, j:j + 1], out_acc[:, 0:nb], ALU.mult, ALU.add)


def emit_ssd(p, c, nc, T, ld_, PS, warena, hT_src, sfx, y_d):
    def din(name, shape, dt=F32):
        return nc.dram_tensor(name, list(shape), dt, kind="ExternalInput").ap()

    NW = 1288
    w_d = din("ssd_w" + sfx, [D, NW])
    cw_d = din("ssd_cw" + sfx, [128, 6 * 4])
    cb_d = din("ssd_cb" + sfx, [128, 6])
    vec_d = din("ssd_vec" + sfx, [1, 8 + 8 + 512 + 512])

    w = warena[:, 0:KC * NW].rearrange("p (k n) -> p k n", k=KC)
    ld_.load(w_d, w, NW)
    cw = p.sb("s_cw", [128, 6, 4]); cb = p.sb("s_cb", [128, 6])
    p.dma("sp", cw[:].rearrange("p c j -> p (c j)"), cw_d[:, :], "k0")
    p.dma("sp", cb[:], cb_d[:, :], "k1")
    vec = p.sb("s_vec", [128, 1040])
    p.dma("sp", vec[:], vec_d[0:1, :].partition_broadcast(128), "k2")
    dtb = vec[:, 0:8]; Dx = vec[:, 16:528]; nw = vec[:, 528:1040]
    aneg = p.sb("s_aneg", [128, 8])
    p.act(aneg[:], vec[:, 8:16], AF.Exp)
    p.ts(aneg[:], aneg[:], -1.0, None, op0=ALU.mult)

    hT = [p.sb(f"s_hT{i}", [128, KC, TB], BF16) for i in range(2)]
    xe = [p.sb(f"s_xe{i}", [128, TB + 3]) for i in range(2)]
    tails = p.sb("s_tails", [128, 6, 3])
    p.memset(tails[:], 0.0)
    acc = p.sb("s_acc", [128, TB])
    xs = p.sb("s_xs", [128, 4, TB])
    Bf = p.sb("s_Bf", [128, TB]); Bb = p.sb("s_Bb", [128, TB], BF16); Cb = p.sb("s_Cb", [128, TB], BF16)
    zs = p.dual("s_zs", [128, 512]); xtm = p.dual("s_xtm", [128, 512]); V = p.dual("s_V", [128, 512], BF16)
    Kt = p.dual("s_Kt", [128, 128], BF16)
    dt = p.dual("s_dt", [128, 8]); ld = p.dual("s_ld", [128, 8])
    Y = p.dual("s_Y", [128, 512]); yo = [p.sb(f"s_yo{i}", [128, 512]) for i in range(2)]
    ss = p.dual("s_ss", [128, 1]); junk = p.dual("s_junk", [128, 512])
    yob = [p.sb(f"s_yob{i}", [128, 4, 128], BF16) for i in range(2)]
    g = gla_alloc(p, "sg_", 8, 64, PS)

    nblk = T // TB
    for b in range(nblk):
        hb = hT[b % 2]
        for off_, ap_ in hT_src(b * TB, TB):
            p.dma("act", hb[:, :, off_:off_ + ap_.shape[2]], ap_, f"s_hT{b % 2}")
        for ch in range(6):
            ps = PS[ch % 2]
            for kc in range(KC):
                p.mm(ps[:, 0:TB], w[:, kc, ch * 128:(ch + 1) * 128], hb[:, kc, :], start=(kc == 0), stop=(kc == KC - 1))
            x_ = xe[ch % 2]
            p.copy(x_[:, 3:3 + TB], ps[:, 0:TB], eng="act")
            p.copy(x_[:, 0:3], tails[:, ch, :], eng="act")
            conv_fm(p, acc, x_, TB, cw, cb, ch, 4)
            p.copy(tails[:, ch, :], x_[:, TB:TB + 3], eng="act")
            if ch < 4:
                p.act(xs[:, ch, :], acc[:], AF.Silu)
            elif ch == 4:
                p.act(Bf[:], acc[:], AF.Silu)
                p.copy(Bb[:], Bf[:], eng="act")
            else:
                p.act(Cb[:], acc[:], AF.Silu)
        for ci in range(TB // 128):
            tk = slice(ci * 128, (ci + 1) * 128)
            p.set_par(ci % 2)
            for kc in range(KC):
                p.mm(PS[7][:, 0:512], hb[:, kc, tk], w[:, kc, 768:1280], start=(kc == 0), stop=(kc == KC - 1))
            p.act(zs[:], PS[7][:, 0:512], AF.Silu)
            for kc in range(KC):
                p.mm(PS[3][:, 32:40], hb[:, kc, tk], w[:, kc, 1280:1288], start=(kc == 0), stop=(kc == KC - 1))
            p.tt(dt[:], PS[3][:, 32:40], dtb, ALU.add)
            p.act(dt[:], dt[:], AF.Exp)
            p.act(dt[:], dt[:], AF.Ln, bias=1.0, scale=1.0)
            p.tt(ld[:], dt[:], aneg[:], ALU.mult)
            for j in range(4):
                p.tr(PS[7][:, j * 128:(j + 1) * 128], xs[:, j, tk], c["ident"][:])
            p.copy(xtm[:], PS[7][:, 0:512], eng="act")
            p.tt(V[:].rearrange("p (h d) -> p h d", h=8), xtm[:].rearrange("p (h d) -> p h d", h=8),
                 dt[:].unsqueeze(2).broadcast_to([128, 8, 64]), ALU.mult)
            p.tr(PS[0][:, 128:256], Bf[:, tk], c["ident"][:])
            p.copy(Kt[:], PS[0][:, 128:256], eng="act")
            gla_chunk(p, c, g, Cb[:, tk], Bb[:, tk], Kt[:], V, ld, 8, 64, Y)
            p.tt(xtm[:], xtm[:], Dx, ALU.mult)
            p.tt(Y[:], Y[:], xtm[:], ALU.add)
            p.tt(Y[:], Y[:], zs[:], ALU.mult)
            p.act(junk[:], Y[:], AF.Square, accum_out=ss[:])
            p.act(ss[:], ss[:], AF.Sqrt, bias=1e-5, scale=1.0 / 512)
            p.recip(ss[:], ss[:])
            o = yo[ci % 2]
            p.stt(o[:], Y[:], ss[:, 0:1], nw, ALU.mult, ALU.mult)
            t0 = b * TB + ci * 128
            ob = yob[ci % 2]
            for j in range(4):
                p.tr(PS[7][:, j * 128:(j + 1) * 128], o[:, j * 128:(j + 1) * 128], c["ident"][:])
            p.copy(ob[:], PS[7][:, 0:512].rearrange("p (a b) -> p a b", a=4), eng="act")
            p.dma("pool", y_d(t0, 128).rearrange("(c p) t -> p c t", p=128), ob[:], f"s_yo{ci % 2}")


def emit_mlstm(p, c, nc, T, ld_, PS, warena, hT_src, sfx, y_d):
    def din(name, shape, dt=F32):
        return nc.dram_tensor(name, list(shape), dt, kind="ExternalInput").ap()

    NW = 1540
    w_d = din("ml_w" + sfx, [D, NW])
    cw_d = din("ml_cw" + sfx, [128, 4 * 4])
    cb_d = din("ml_cb" + sfx, [128, 4])
    vec_d = din("ml_vec" + sfx, [1, 2 + 2 + 512])
    w = warena[:, 0:KC * NW].rearrange("p (k n) -> p k n", k=KC)
    ld_.load(w_d, w, NW)
    cw = p.sb("m_cw", [128, 4, 4]); cb = p.sb("m_cb", [128, 4])
    p.dma("sp", cw[:].rearrange("p c j -> p (c j)"), cw_d[:, :], "k0")
    p.dma("sp", cb[:], cb_d[:, :], "k1")
    vec = p.sb("m_vec", [128, 516])
    p.dma("sp", vec[:], vec_d[0:1, :].partition_broadcast(128), "k2")
    nw = vec[:, 4:516]

    hT = [p.sb(f"m_hT{i}", [128, KC, TB], BF16) for i in range(2)]
    xe = [p.sb(f"m_xe{i}", [128, TB + 3]) for i in range(2)]
    tails = p.sb("m_tails", [128, 4, 3])
    p.memset(tails[:], 0.0)
    acc = p.sb("m_acc", [128, TB])
    qb = p.sb("m_qb", [128, 2, TB], BF16); kf = p.sb("m_kf", [128, 2, TB]); kb = p.sb("m_kb", [128, 2, TB], BF16)
    vtm = p.dual("m_vtm", [128, 512]); osg = p.dual("m_osg", [128, 512])
    Vp = [p.dual(f"m_Vp{i}", [128, 257], BF16) for i in range(2)]
    Kt = [p.dual(f"m_Kt{i}", [128, 128], BF16) for i in range(2)]
    gi = p.dual("m_gi", [128, 4]); ei = p.dual("m_ei", [128, 2]); lf = p.dual("m_lf", [128, 2, 2])
    Y = [p.dual(f"m_Y{i}", [128, 257]) for i in range(2)]; hh = p.dual("m_h", [128, 512]); yo = [p.sb(f"m_yo{i}", [128, 512]) for i in range(2)]
    yob = [p.sb(f"m_yob{i}", [128, 4, 128], BF16) for i in range(2)]
    den = [p.dual(f"m_den{i}", [128, 1]) for i in range(2)]; ss = [p.dual(f"m_ss{i}", [128, 1]) for i in range(2)]; junk = [p.dual(f"m_junk{i}", [128, 256]) for i in range(2)]
    gs = [gla_alloc(p, f"mg{i}_", 1, 257, PS) for i in range(2)]

    nblk = T // TB
    for b in range(nblk):
        hb = hT[b % 2]
        for off_, ap_ in hT_src(b * TB, TB):
            p.dma("act", hb[:, :, off_:off_ + ap_.shape[2]], ap_, f"m_hT{b % 2}")
        for ch in range(4):
            ps = PS[ch % 2]
            for kc in range(KC):
                p.mm(ps[:, 0:TB], w[:, kc, ch * 128:(ch + 1) * 128], hb[:, kc, :], start=(kc == 0), stop=(kc == KC - 1))
            x_ = xe[ch % 2]
            p.copy(x_[:, 3:3 + TB], ps[:, 0:TB], eng="act")
            p.copy(x_[:, 0:3], tails[:, ch, :], eng="act")
            conv_fm(p, acc, x_, TB, cw, cb, ch, 4)
            p.copy(tails[:, ch, :], x_[:, TB:TB + 3], eng="act")
            if ch < 2:
                p.act(acc[:], acc[:], AF.Silu)
                p.ts(qb[:, ch, :], acc[:], 128.0 ** -0.5, None, op0=ALU.mult)
            else:
                p.act(kf[:, ch - 2, :], acc[:], AF.Silu)
                p.copy(kb[:, ch - 2, :], kf[:, ch - 2, :], eng="act")
        for ci in range(TB // 128):
            tk = slice(ci * 128, (ci + 1) * 128)
            p.set_par(ci % 2)
            for kc in range(KC):
                p.mm(PS[7][:, 0:512], hb[:, kc, tk], w[:, kc, 512:1024], start=(kc == 0), stop=(kc == KC - 1))
            p.copy(vtm[:], PS[7][:, 0:512], eng="act")
            for kc in range(KC):
                p.mm(PS[7][:, 0:512], hb[:, kc, tk], w[:, kc, 1024:1536], start=(kc == 0), stop=(kc == KC - 1))
            p.act(osg[:], PS[7][:, 0:512], AF.Sigmoid)
            for kc in range(KC):
                p.mm(PS[3][:, 32:36], hb[:, kc, tk], w[:, kc, 1536:1540], start=(kc == 0), stop=(kc == KC - 1))
            p.tt(gi[:], PS[3][:, 32:36], vec[:, 0:4], ALU.add)
            p.act(ei[:], gi[:, 0:2], AF.Exp)
            p.act(gi[:, 2:4], gi[:, 2:4], AF.Exp, scale=-1.0)
            p.act(gi[:, 2:4], gi[:, 2:4], AF.Ln, bias=1.0, scale=1.0)
            for hd in range(2):
                p.ts(lf[:, hd, :], gi[:, 2 + hd:3 + hd].to_broadcast([128, 2]), -1.0, None, op0=ALU.mult)
            t0 = b * TB + ci * 128
            for hd in range(2):
                vp = Vp[hd]; Kt_ = Kt[hd]; Y_ = Y[hd]; den_ = den[hd]; ss_ = ss[hd]; junk_ = junk[hd]
                p.ts(vp[:, 0:256], vtm[:, hd * 256:(hd + 1) * 256], ei[:, hd:hd + 1], None, op0=ALU.mult)
                p.copy(vp[:, 256:257], ei[:, hd:hd + 1], eng="act")
                p.tr(PS[0][:, 128:256], kf[:, hd, tk], c["ident"][:])
                p.copy(Kt_[:], PS[0][:, 128:256], eng="act")
                gla_chunk(p, c, gs[hd], qb[:, hd, tk], kb[:, hd, tk], Kt_[:], vp, lf[:, hd, :], 1, 257, Y_)
                p.ts(den_[:], Y_[:, 256:257], -1.0, 1.0, op0=ALU.mult, op1=ALU.max)
                p.ts(junk_[:, 0:1], Y_[:, 256:257], 1.0, None, op0=ALU.max)
                p.tt(den_[:], den_[:], junk_[:, 0:1], ALU.max)
                p.recip(den_[:], den_[:])
                hv = hh[:, hd * 256:(hd + 1) * 256]
                p.ts(hv, Y_[:, 0:256], den_[:, 0:1], None, op0=ALU.mult)
                p.act(junk_[:], hv, AF.Square, accum_out=ss_[:])
                p.act(ss_[:], ss_[:], AF.Sqrt, bias=1e-6, scale=1.0 / 256)
                p.recip(ss_[:], ss_[:])
                p.ts(hv, hv, ss_[:, 0:1], None, op0=ALU.mult)
            o = yo[ci % 2]
            p.tt(o[:], hh[:], nw, ALU.mult)
            p.tt(o[:], o[:], osg[:], ALU.mult)
            ob = yob[ci % 2]
            for j in range(4):
                p.tr(PS[7][:, j * 128:(j + 1) * 128], o[:, j * 128:(j + 1) * 128], c["ident"][:])
            p.copy(ob[:], PS[7][:, 0:512].rearrange("p (a b) -> p a b", a=4), eng="act")
            p.dma("pool", y_d(t0, 128).rearrange("(c p) t -> p c t", p=128), ob[:], f"m_yo{ci % 2}")


D = 2048
FFN = 5504
FC = 43
O_Z = 0
O_XBC = 2048
O_DT = 5120
O_RW = 5152
O_QK = 11744
O_V = 13792
O_O = 15840
O_I = 17888
O_F = 17896
O_G = 17904


def fm(vec, nch):
    return np.ascontiguousarray(np.asarray(vec).reshape(nch, 128).T)


def fm_taps(w, nch):
    ntap = w.shape[0]
    return np.ascontiguousarray(np.asarray(w).reshape(ntap, nch, 128).transpose(2, 1, 0).reshape(128, nch * ntap))


def prep_A(I, l, q):
    w_in = I["w_in"][l]
    r = {}
    sl = lambda o, n: np.arange(o, o + n)
    cols = np.concatenate([sl(O_XBC + q * 512, 512), sl(O_XBC + 2048 + q * 128, 128), sl(O_XBC + 2560 + q * 128, 128),
                           sl(O_Z + q * 512, 512), sl(O_DT + q * 8, 8)])
    r["ssd_w"] = np.ascontiguousarray(w_in[:, cols])
    cch = np.concatenate([sl(q * 512, 512), sl(2048 + q * 128, 128), sl(2560 + q * 128, 128)])
    r["ssd_cw"] = fm_taps(I["ssd_conv_w"][l][:, cch], 6)
    r["ssd_cb"] = fm(I["ssd_conv_b"][l][cch], 6)
    hs = slice(q * 8, q * 8 + 8)
    r["ssd_vec"] = np.concatenate([I["ssd_dt_bias"][l][hs], I["ssd_a_log"][l][hs], np.repeat(I["ssd_d"][l][hs], 64),
                                   I["ssd_norm_w"][l][q * 512:(q + 1) * 512]]).astype(np.float32)[None, :]
    cols = np.concatenate([sl(O_QK + q * 256, 256), sl(O_QK + 1024 + q * 256, 256), sl(O_V + q * 512, 512),
                           sl(O_O + q * 512, 512), sl(O_I + 2 * q, 2), sl(O_F + 2 * q, 2)])
    r["ml_w"] = np.ascontiguousarray(w_in[:, cols])
    cch = np.concatenate([sl(q * 256, 256), sl(1024 + q * 256, 256)])
    r["ml_cw"] = fm_taps(I["mlstm_conv_w"][l][:, cch], 4)
    r["ml_cb"] = fm(I["mlstm_conv_b"][l][cch], 4)
    r["ml_vec"] = np.concatenate([I["mlstm_i_bias"][l][2 * q:2 * q + 2], I["mlstm_f_bias"][l][2 * q:2 * q + 2],
                                  I["mlstm_norm_w"][l][q * 512:(q + 1) * 512]]).astype(np.float32)[None, :]
    cs = slice(q * 512, (q + 1) * 512)
    z32 = np.zeros((D, 32), np.float32)
    rw = w_in[:, O_RW:O_RW + 6592]
    r["rw_w"] = np.ascontiguousarray(np.concatenate([rw[:, q * 512:(q + 1) * 512], rw[:, 2048 + q * 512:2048 + (q + 1) * 512],
                                                      rw[:, 4096 + q * 512:4096 + (q + 1) * 512], rw[:, 6144:6240], z32,
                                                      rw[:, 6240:6336], z32, rw[:, 6336:6592]], axis=1))
    mu = I["rwkv_mu"][l]
    zz = np.zeros(32, np.float32)
    mu_c = np.concatenate([mu[q * 512:(q + 1) * 512], mu[2048 + q * 512:2048 + (q + 1) * 512],
                           mu[4096 + q * 512:4096 + (q + 1) * 512], mu[6144:6240], zz, mu[6240:6336], zz, mu[6336:6592]])
    r["rw_mu"] = fm(mu_c, 16)
    pc = np.stack([fm(I["rwkv_w0"][l][cs], 4), fm(I["rwkv_a0"][l][cs], 4), fm(I["rwkv_k_k"][l][cs], 4),
                   fm(I["rwkv_k_a"][l][cs], 4), fm(I["rwkv_r_k"][l].reshape(-1)[cs], 4)], axis=1)
    r["rw_pc"] = np.ascontiguousarray(pc.reshape(128, 20))
    r["rw_w2"] = np.ascontiguousarray(I["rwkv_w2"][l][:, cs])
    r["rw_a2"] = np.ascontiguousarray(I["rwkv_a2"][l][:, cs])
    r["rw_g2"] = np.ascontiguousarray(I["rwkv_g2"][l][:, cs])
    lw = I["rwkv_ln_w"][l][cs].reshape(2, 4, 1, 64)
    lb = I["rwkv_ln_b"][l][cs].reshape(2, 4, 1, 64)
    r["rw_lnw"] = np.ascontiguousarray(np.broadcast_to(lw, (2, 4, 32, 64)).reshape(2, 128, 64).transpose(1, 0, 2).reshape(128, 128))
    r["rw_lnb"] = np.ascontiguousarray(np.broadcast_to(lb, (2, 4, 32, 64)).reshape(2, 128, 64).transpose(1, 0, 2).reshape(128, 128))
    return r


def prep_B(I, l):
    r = {}
    r["wg"] = np.ascontiguousarray(I["w_in"][l][:, O_G:O_G + 3 * D])
    r["p_ssd"] = I["proj_ssd"][l]; r["p_rwkv"] = I["proj_rwkv"][l]; r["p_ml"] = I["proj_mlstm"][l]
    r["w_out"] = I["w_out"][l]
    r["ln1_g"] = I["ln1_g"][l][None, :]; r["ln1_b"] = I["ln1_b"][l][None, :]
    r["ln2_g"] = I["ln2_g"][l][None, :]; r["ln2_b"] = I["ln2_b"][l][None, :]
    r["w_up"] = I["ffn_w_up"][l]
    r["cw"] = fm_taps(I["ffn_conv_w"][l], 2 * FC)
    r["cb"] = fm(I["ffn_conv_b"][l], 2 * FC)
    r["w_down"] = I["ffn_w_down"][l]
    return r


NG = 4
GROUPS = [[0, 1, 2, 3], [4, 5, 6, 7]]
OVERLAP_CC = False


def emit_L(p, nc, ident, PS, TC, io):
    x = nc.dram_tensor("x", [TC, D], F32, kind="ExternalInput").ap()
    g = nc.dram_tensor("ln_in_g", [1, D], F32, kind="ExternalInput").ap()
    b = nc.dram_tensor("ln_in_b", [1, D], F32, kind="ExternalInput").ap()
    h_loc, hT_loc, htail_loc = io["h_loc"], io["hT_loc"], io["htail_loc"]
    gt = p.sb("gt", [128, D]); bt = p.sb("bt", [128, D])
    p.dma("sp", gt[:], g[0:1, :].partition_broadcast(128), "c0")
    p.dma("sp", bt[:], b[0:1, :].partition_broadcast(128), "c1")
    xt = [p.sb(f"xt{i}", [128, D]) for i in range(2)]
    ht = [p.sb(f"ht{i}", [128, D]) for i in range(2)]
    hTt = [p.sb(f"hTt{i}", [128, KC, 128], BF16) for i in range(2)]
    st = p.sb("st", [128, 4, 6]); mv = p.sb("mv", [128, 2]); rstd = p.sb("rstd", [128, 1])
    nt = TC // 128
    for i in range(nt):
        s = i % 2
        p.dma("sp", xt[s][:], x[i * 128:(i + 1) * 128, :], f"x{s}")
        layer_norm_tile(p, ht[s], xt[s], 128, gt, bt, st, mv, rstd)
        p.dma("pool", h_loc[i * 128:(i + 1) * 128, :], ht[s][:], f"ho{s}")
        if i == nt - 1:
            p.dma("pool", htail_loc[0:2, :], ht[s][126:128, :], "htl")
        transpose_to_bf16(p, hTt[s], ht[s], 128, ident, PS[6:8], 0)
        ch_, col_ = divmod(i * 128, io["HC"])
        p.dma("pool", hT_loc[ch_ * D:(ch_ + 1) * D, col_:col_ + 128].rearrange("(k p) t -> p k t", p=128), hTt[s][:], f"hTo{s}")
        if io["overlap"] and col_ + 128 == io["HC"]:
            io["xchg_h"](ch_)
        if io["overlap"] and i == nt - 1:
            io["xchg_tail"]()


def build_fused(T, TC, depth=2):
    nc = bass.Bass("TRN2", target_bir_lowering=False)

    def internal(name, shape, dt, local=False):
        if local:
            return nc.dram_tensor(name, list(shape), dt, kind="Internal", addr_space="Local").ap()
        return nc.dram_tensor(name, list(shape), dt, kind="Internal").ap()

    io = {}
    io["h_loc"] = internal("h_loc", [TC, D], F32)
    HC = 256
    TT = min(1024, TC)
    NCHL = TC // HC
    NTC = T // TT
    io["HC"] = HC; io["TT"] = TT
    io["hT_loc"] = internal("hT_loc", [NCHL * D, HC], BF16)
    io["htail_loc"] = internal("htail_loc", [2, D], F32)
    io["hT_all"] = internal("hT_all", [NCHL * NG * D, HC], BF16, True)
    io["htail_all"] = internal("htail_all", [NG * 2, D], F32, True)
    y_loc = [internal(f"y_loc{i}", [NTC * 512, TT], BF16) for i in range(3)]
    io["y_all"] = [internal(f"y_all{i}", [NTC * NG * 512, TT], BF16, True) for i in range(3)]
    io["hmask"] = nc.dram_tensor("hmask", [128, 1], F32, kind="ExternalInput").ap()
    io["out"] = nc.dram_tensor("out", [TC, D], F32, kind="ExternalOutput").ap()
    hT_all = io["hT_all"]

    def hT_src(t0, n):
        r, lt = divmod(t0, TC)
        pieces = []
        for o_ in range(0, n, HC):
            n_ = min(HC, n - o_)
            ch_, col_ = divmod(lt + o_, HC)
            row0 = (ch_ * NG + r) * D
            pieces.append((o_, hT_all[row0:row0 + D, col_:col_ + n_].rearrange("(k p) t -> p k t", p=128)))
        return pieces

    def y_ap(i):
        def f_(t0, n):
            tc_, col_ = divmod(t0, TT)
            return y_loc[i][tc_ * 512:(tc_ + 1) * 512, col_:col_ + n]
        return f_

    def xchg_h(ch_):
        p.cc("AllGather", io["hT_loc"][ch_ * D:(ch_ + 1) * D, :], hT_all[ch_ * NG * D:(ch_ + 1) * NG * D, :], GROUPS)

    def xchg_tail():
        p.cc("AllGather", io["htail_loc"], io["htail_all"], GROUPS)

    io["xchg_h"] = xchg_h
    io["overlap"] = OVERLAP_CC
    io["xchg_tail"] = xchg_tail

    def exchange_y(i):
        for tc_ in range(NTC):
            p.cc("AllGather", y_loc[i][tc_ * 512:(tc_ + 1) * 512, :], io["y_all"][i][tc_ * NG * 512:(tc_ + 1) * NG * 512, :], GROUPS)

    p = Prog(nc)
    c = make_consts(p)
    PS = [p.ps(f"ps{i}", [128, 512]) for i in range(8)]
    m_ = p.mark()
    p.prefix = "L_"
    emit_L(p, nc, c["ident"], PS, TC, io)
    p.release(m_)
    for l in range(depth):
        sfx = f"_{l}"
        if not OVERLAP_CC:
            for ch_ in range(NCHL):
                xchg_h(ch_)
            xchg_tail()
        m_ = p.mark()
        p.prefix = f"A{l}_"
        warena = p.sb("warena", [128, KC * 2048], BF16)
        ld_ = Loader(p)
        m2 = p.mark()
        emit_ssd(p, c, nc, T, ld_, PS, warena, hT_src, sfx, y_ap(0))
        p.release(m2)
        if OVERLAP_CC:
            exchange_y(0)
        m2 = p.mark()
        emit_mlstm(p, c, nc, T, ld_, PS, warena, hT_src, sfx, y_ap(2))
        p.release(m2)
        if OVERLAP_CC:
            exchange_y(2)
        m2 = p.mark()
        emit_rwkv(p, c, nc, T, ld_, PS, warena, hT_src, sfx, y_ap(1))
        p.release(m_)
        for i_ in ((1,) if OVERLAP_CC else (0, 2, 1)):
            exchange_y(i_)
        m_ = p.mark()
        p.prefix = f"B{l}_"
        emit_B(p, nc, c["ident"], PS, TC, sfx, io, last=(l == depth - 1))
        p.release(m_)
    p.wait_all("pool")
    p.emit()
    return nc


def fused_inputs(I, NB, T, TC):
    QN = T // TC
    x = I["x"].astype(np.float32)
    maps = []
    for cidx in range(NB * QN):
        b, q = divmod(cidx, QN)
        m = {"x": np.ascontiguousarray(x[b, q * TC:(q + 1) * TC, :]),
             "ln_in_g": np.ascontiguousarray(I["ln_in_g"][None, :]), "ln_in_b": np.ascontiguousarray(I["ln_in_b"][None, :]),
             "hmask": np.full((128, 1), 0.0 if q == 0 else 1.0, np.float32)}
        for l in range(2):
            for k, v in prep_A(I, l, q).items():
                m[f"{k}_{l}"] = v
            for k, v in prep_B(I, l).items():
                m[f"{k}_{l}"] = np.ascontiguousarray(v)
        maps.append(m)
    return maps


def kernel(**inputs):
    I = {k: np.asarray(v) for k, v in inputs.items()}
    NB, T, _ = I["x"].shape
    QN = 4
    TC = T // QN
    nc = build_fused(T, TC)
    maps = fused_inputs(I, NB, T, TC)
    res = run_bass_kernel_spmd(nc, maps, core_ids=list(range(NB * QN))).results
    out = np.stack([np.asarray(r["out"]) for r in res]).reshape(NB, T, D)
    return np.ascontiguousarray(out.astype(np.float32))
```

```python
import os
import numpy as np
import ml_dtypes
from concourse.bass_utils import run_bass_kernel_spmd
import concourse.bass as bass
import concourse.mybir as mybir

F32 = mybir.dt.float32
BF16 = mybir.dt.bfloat16
AF = mybir.ActivationFunctionType
ALU = mybir.AluOpType
AX = mybir.AxisListType

SAME_ENGINE_SYNC = True


class Dual:
    def __init__(self, p, name, shape, dt):
        self.t = [p.sb(name + "_a", shape, dt), p.sb(name + "_b", shape, dt)]
        self.par = 0

    def __getitem__(self, idx):
        return self.t[self.par][idx]


class Prog:
    def __init__(self, nc):
        self.nc = nc
        self.eng = {"pe": nc.tensor, "dve": nc.vector, "act": nc.scalar,
                    "pool": nc.gpsimd, "sp": nc.sync}
        self.q = {e: [] for e in self.eng}
        self.cnt = {e: 0 for e in self.eng}
        self.waited = {e: {} for e in self.eng}
        self.acc = {}
        self.dma_cnt = {}
        self.sems = {}
        self.stack = []
        self.nops = 0
        self.prefix = ""
        self.duals = []

    def _enter(self, cm):
        v = cm.__enter__()
        self.stack.append(cm)
        return v

    def sb(self, name, shape, dt=F32):
        return self._enter(self.nc.sbuf_tensor("sb_" + self.prefix + name, list(shape), dt))

    def ps(self, name, shape, dt=F32):
        return self._enter(self.nc.psum_tensor("pp_" + name, list(shape), dt))

    def dual(self, name, shape, dt=F32):
        d = Dual(self, name, shape, dt)
        self.duals.append(d)
        return d

    def set_par(self, par):
        for d in self.duals:
            d.par = par

    def sem(self, key):
        if key not in self.sems:
            self.sems[key] = self._enter(self.nc.semaphore("s_" + str(key).replace(" ", "")))
        return self.sems[key]

    @staticmethod
    def region(ap):
        t = ap.tensor
        name = t.name
        space = str(ap.space)
        pairs = list(ap.ap)
        off = ap.offset
        if "DRAM" in space.upper() or "HBM" in space.upper():
            lo = off
            hi = off
            for st, n in pairs:
                if st >= 0:
                    hi += st * (n - 1)
                else:
                    lo += st * (n - 1)
            return (name, 0, 0, lo, hi)
        if "PSUM" in space.upper():
            return (name, 0, 127, 0, 1 << 30)
        pst, pn = pairs[0]
        p0 = ap.start_partition()
        p1 = p0 + (pn - 1 if pst != 0 else 0)
        if pst != 0:
            fo = off - p0 * pst
        else:
            fo = off
        lo = fo
        hi = fo
        for st, n in pairs[1:]:
            if st >= 0:
                hi += st * (n - 1)
            else:
                lo += st * (n - 1)
        return (name, p0, p1, lo, hi)

    def _deps(self, reads, writes):
        deps = {}

        def add(tok):
            s, v = tok
            if deps.get(s, 0) < v:
                deps[s] = v

        for ap in reads:
            name, p0, p1, lo, hi = self.region(ap)
            for r in self.acc.get(name, ()):
                if r[5] and not (r[2] < p0 or r[1] > p1 or r[4] < lo or r[3] > hi):
                    add(r[0])
        for ap in writes:
            name, p0, p1, lo, hi = self.region(ap)
            for r in self.acc.get(name, ()):
                if not (r[2] < p0 or r[1] > p1 or r[4] < lo or r[3] > hi):
                    add(r[0])
        return deps

    def _record(self, tok, reads, writes):
        for is_w, aps in ((False, reads), (True, writes)):
            for ap in aps:
                name, p0, p1, lo, hi = self.region(ap)
                lst = self.acc.setdefault(name, [])
                new = []
                for r in lst:
                    contained = (r[1] >= p0 and r[2] <= p1 and r[3] >= lo and r[4] <= hi)
                    if contained and (is_w or (r[0][0] == tok[0] and not r[5])):
                        continue
                    new.append(r)
                new.append((tok, p0, p1, lo, hi, is_w))
                self.acc[name] = new

    def _waits(self, e, deps):
        w = []
        for s, v in deps.items():
            if s == e and (e == "pe" or not SAME_ENGINE_SYNC):
                continue
            if self.waited[e].get(s, 0) < v:
                self.waited[e][s] = v
                w.append((s, v))
        return w

    def op(self, e, fn, reads, writes):
        pr_ = [a for a in reads if "PSUM" in str(a.space).upper()]
        if pr_:
            reads = [a for a in reads if "PSUM" not in str(a.space).upper()]
            writes = list(writes) + pr_
        deps = self._deps(reads, writes)
        w = self._waits(e, deps)
        self.cnt[e] += 1
        tok = (e, self.cnt[e])
        self.q[e].append((fn, w, (e, 1)))
        self._record(tok, reads, writes)
        self.nops += 1

    def dma(self, e, out, in_, key, **kw):
        deps = self._deps([in_], [out])
        w = self._waits(e, deps)
        k = ("dma", key)
        self.dma_cnt[k] = self.dma_cnt.get(k, 0) + 16
        tok = (k, self.dma_cnt[k])
        self.q[e].append((lambda en: en.dma_start(out=out, in_=in_, **kw), w, (k, 16)))
        self._record(tok, [in_], [out])
        self.nops += 1

    def barrier(self):
        for e in self.eng:
            w = []
            for f in self.eng:
                if f != e and self.cnt[f] > self.waited[e].get(f, 0):
                    self.waited[e][f] = self.cnt[f]
                    w.append((f, self.cnt[f]))
            for k, v in self.dma_cnt.items():
                if self.waited[e].get(k, 0) < v:
                    self.waited[e][k] = v
                    w.append((k, v))
            self.q[e].append((None, w, None))

    def mark(self):
        return len(self.stack)

    def release(self, m):
        self.barrier()
        while len(self.stack) > m:
            self.stack.pop().__exit__(None, None, None)

    def dma_dyn(self, e, out, in_fn, in_static, key):
        deps = self._deps([in_static], [out])
        w = self._waits(e, deps)
        k = ("dma", key)
        self.dma_cnt[k] = self.dma_cnt.get(k, 0) + 16
        tok = (k, self.dma_cnt[k])
        self.q[e].append((lambda en: en.dma_start(out=out, in_=in_fn(en)), w, (k, 16)))
        self._record(tok, [in_static], [out])
        self.nops += 1

    def cc(self, kind, src, dst, groups):
        deps = self._deps([src], [dst])
        w = self._waits("pool", deps)
        k = ("dma", "cc")
        self.dma_cnt[k] = self.dma_cnt.get(k, 0) + 1
        tok = (k, self.dma_cnt[k])
        self.q["pool"].append((lambda en: en.collective_compute(kind, ALU.bypass, replica_groups=groups,
                                                                ins=[src], outs=[dst]), w, (k, 1)))
        self._record(tok, [src], [dst])
        self.nops += 1

    def wait_all(self, e):
        w = []
        for k, v in self.dma_cnt.items():
            if self.waited[e].get(k, 0) < v:
                self.waited[e][k] = v
                w.append((k, v))
        self.q[e].append((None, w, None))

    def emit(self):
        nc = self.nc
        for e in self.eng:
            self.sem(e)
        for k in self.dma_cnt:
            self.sem(k)
        with nc.Block() as block:
            def mk(e):
                def body(en):
                    for fn, w, inc in self.q[e]:
                        for s, v in w:
                            en.wait_ge(self.sems[s], v)
                        if fn is not None:
                            ins = fn(en)
                            ins.then_inc(self.sems[inc[0]], inc[1])
                return body
            block.tensor(mk("pe"))
            block.vector(mk("dve"))
            block.scalar(mk("act"))
            block.gpsimd(mk("pool"))
            block.sync(mk("sp"))
        while self.stack:
            self.stack.pop().__exit__(None, None, None)

    def mm(self, out, lhsT, rhs, start=True, stop=True, **kw):
        self.op("pe", lambda en: en.matmul(out, lhsT, rhs, start=start, stop=stop, **kw),
                [lhsT, rhs], [out])

    def tr(self, out, in_, ident):
        self.op("pe", lambda en: en.transpose(out, in_, ident), [in_, ident], [out])

    def act(self, out, in_, func, bias=None, scale=None, accum_out=None, eng="act"):
        kw = {}
        rd = [in_]
        if bias is not None:
            kw["bias"] = bias
            if not isinstance(bias, (int, float)):
                rd.append(bias)
        if scale is not None:
            kw["scale"] = scale
            if not isinstance(scale, (int, float)):
                rd.append(scale)
        wr = [out]
        if accum_out is not None:
            kw["accum_out"] = accum_out
            wr.append(accum_out)
        self.op("act", lambda en: en.activation(out=out, in_=in_, func=func, **kw), rd, wr)

    def tt(self, out, in0, in1, op, eng="dve"):
        self.op(eng, lambda en: en.tensor_tensor(out=out, in0=in0, in1=in1, op=op), [in0, in1], [out])

    def ts(self, out, in0, s1, s2=None, op0=ALU.mult, op1=None, eng="dve", accum_out=None):
        rd = [in0]
        if not isinstance(s1, (int, float)):
            rd.append(s1)
        if s2 is not None and not isinstance(s2, (int, float)):
            rd.append(s2)
        kw = {}
        wr = [out]
        if op1 is not None:
            kw["op1"] = op1
        if accum_out is not None:
            kw["accum_out"] = accum_out
            wr.append(accum_out)
        self.op(eng, lambda en: en.tensor_scalar(out=out, in0=in0, scalar1=s1, scalar2=s2, op0=op0, **kw), rd, wr)

    def stt(self, out, in0, scalar, in1, op0, op1, accum_out=None):
        rd = [in0, in1]
        if not isinstance(scalar, (int, float)):
            rd.append(scalar)
        kw = {}
        wr = [out]
        if accum_out is not None:
            kw["accum_out"] = accum_out
            wr.append(accum_out)
        self.op("dve", lambda en: en.scalar_tensor_tensor(out=out, in0=in0, scalar=scalar, in1=in1, op0=op0, op1=op1, **kw), rd, wr)

    def copy(self, out, in_, eng="dve"):
        if eng == "act":
            self.op("act", lambda en: en.copy(out=out, in_=in_), [in_], [out])
        else:
            self.op(eng, lambda en: en.tensor_copy(out=out, in_=in_), [in_], [out])

    def memset(self, ap, val, eng="dve"):
        self.op(eng, lambda en: en.memset(ap, val), [], [ap])

    def recip(self, out, in_):
        self.op("dve", lambda en: en.reciprocal(out=out, in_=in_), [in_], [out])

    def reduce(self, out, in_, op=ALU.add, axis=AX.X):
        self.op("dve", lambda en: en.tensor_reduce(out=out, in_=in_, axis=axis, op=op), [in_], [out])


D = 2048
KC = 16
FFN = 5504
FC = 43
ALPHA = (2.0 * 2) ** 0.25
LN_EPS = 1e-5


def make_ident(p):
    ident = p.sb("ident", [128, 128])
    p.memset(ident[:], 1.0, eng="pool")
    p.op("pool", lambda en: en.affine_select(out=ident[:], in_=ident[:], pattern=[[-1, 128]],
                                              compare_op=ALU.is_equal, fill=0.0, base=0, channel_multiplier=1),
         [ident[:]], [ident[:]])
    return ident


def layer_norm_tile(p, out, xin, nb, gt, bt, st, mv, rstd, eps=LN_EPS):
    for c in range(4):
        p.op("dve", lambda en, c=c: en.bn_stats(out=st[0:nb, c, :], in_=xin[0:nb, c * 512:(c + 1) * 512]),
             [xin[0:nb, c * 512:(c + 1) * 512]], [st[0:nb, c, :]])
    p.op("dve", lambda en: en.bn_aggr(out=mv[0:nb, :], in_=st[0:nb].rearrange("p a b -> p (a b)")),
         [st[0:nb]], [mv[0:nb, :]])
    p.act(rstd[0:nb, :], mv[0:nb, 1:2], AF.Sqrt, bias=eps, scale=1.0)
    p.recip(rstd[0:nb, :], rstd[0:nb, :])
    p.ts(out[0:nb, :], xin[0:nb, :], mv[0:nb, 0:1], rstd[0:nb, 0:1], op0=ALU.subtract, op1=ALU.mult)
    p.tt(out[0:nb, :], out[0:nb, :], gt[0:nb, :], ALU.mult)
    p.tt(out[0:nb, :], out[0:nb, :], bt[0:nb, :], ALU.add)


def transpose_to_bf16(p, dstT, src, nb, ident, pst, tok0):
    for c4 in range(4):
        ps = pst[c4 % 2]
        for j in range(4):
            kc = c4 * 4 + j
            p.tr(ps[:, j * 128:j * 128 + nb], src[0:nb, kc * 128:(kc + 1) * 128], ident[0:nb, 0:nb])
        p.copy(dstT[:, c4 * 4:(c4 + 1) * 4, tok0:tok0 + nb],
               ps[:].rearrange("p (a b) -> p a b", a=4)[:, :, 0:nb], eng="act")


def emit_B(p, nc, ident, PS, TC, sfx, io, last, TB=512):
    T = TC

    def din(name, shape, dt=F32):
        return nc.dram_tensor(name + sfx, list(shape), dt, kind="ExternalInput").ap()

    hmask_d = io["hmask"]
    wg_d = din("wg", [D, 3 * D])
    pj_d = [din(n, [D, D]) for n in ("p_ssd", "p_rwkv", "p_ml")]
    wo_d = din("w_out", [D, D])
    ln_d = [din(n, [1, D]) for n in ("ln1_g", "ln1_b", "ln2_g", "ln2_b")]
    wup_d = din("w_up", [D, 2 * FFN])
    cw_d = din("cw", [128, 2 * FC * 3])
    cb_d = din("cb", [128, 2 * FC])
    wdn_d = din("w_down", [FFN, D])
    h_loc, hT_loc, htail_loc = io["h_loc"], io["hT_loc"], io["htail_loc"]
    hT_all, htail_all, y_all = io["hT_all"], io["htail_all"], io["y_all"]
    ho_d = io["out"] if last else h_loc
    NG = 4

    rank_cache = io.setdefault("rank_cache", {})

    def q_of(en):
        k = ("q", id(en))
        if k not in rank_cache:
            rank_cache[k] = en.partition_id() % NG
        return rank_cache[k]

    def rr_of(en):
        k = ("rr", id(en))
        if k not in rank_cache:
            rank_cache[k] = (en.partition_id() + (NG - 1)) % NG
        return rank_cache[k]

    lnt = [p.sb(f"lnt{i}", [128, D]) for i in range(2)]
    cw = p.sb("cw", [128, 2 * FC, 3])
    cb = p.sb("cb", [128, 2 * FC])
    hmask = p.sb("hmask", [128, 1])
    p.dma("sp", cw[:].rearrange("p c j -> p (c j)"), cw_d[:, :], "c4")
    p.dma("sp", cb[:], cb_d[:, :], "c5")
    p.dma("sp", hmask[:], hmask_d[:, :], "c6")
    tails = p.sb("tails", [128, 2 * FC, 2])

    st = p.sb("st", [128, 4, 6])
    mv = p.sb("mv", [128, 2])
    rstd = p.sb("rstd", [128, 1])

    WS = [p.sb(f"ws{i}", [128, KC, 256]) for i in range(2)]
    WB = [p.sb(f"wb{i}", [128, KC, 256], BF16) for i in range(2)]
    hT = p.sb("hTb", [128, KC, TB], BF16)
    h1T = hT
    ybig = p.sb("ybig", [128, 3 * KC * TB], BF16)
    yT = [ybig[:, i * KC * TB:(i + 1) * KC * TB].rearrange("p (k t) -> p k t", k=KC) for i in range(3)]
    actT = ybig[:, 0:FC * TB].rearrange("p (k t) -> p k t", k=FC)
    mT = p.sb("mT", [128, KC, TB], BF16)
    htm = [p.sb(f"htm{i}", [128, D]) for i in range(4)]
    G = p.sb("G", [128, 9 * TB])
    gst = [G[:, i * TB:(i + 1) * TB] for i in range(6)]
    gsb = [G[:, (6 + i) * TB:(7 + i) * TB] for i in range(3)]
    tmp = [G[:, 0:D], G[:, D:2 * D]]
    uext = [p.sb(f"uext{i}", [128, TB + 2]) for i in range(2)]
    wcnt = [0]

    def load_ln(i0):
        for i in range(2):
            p.dma("sp", lnt[i][:], ln_d[i0 + i][0:1, :].partition_broadcast(128), f"ln{i}")

    def load_panel(wd, col0, ncols, krows=None):
        s = wcnt[0] % 2
        wcnt[0] += 1
        p.dma("sp", WS[s][:, :, 0:ncols], wd.rearrange("(k p) n -> p k n", p=128)[:, :, col0:col0 + ncols], f"ws{s}")
        eng = "act" if s == 0 else "dve"
        p.copy(WB[s][:, :, 0:ncols], WS[s][:, :, 0:ncols], eng=eng)
        return WB[s]

    blocks = [(0, 2)] + [(2 + b * TB, min(TB, T - b * TB)) for b in range((T + TB - 1) // TB)]
    for bi, (t0, nb) in enumerate(blocks):
        halo = bi == 0
        ntile = (nb + 127) // 128
        lt0 = t0 - 2
        HC = io["HC"]; NCHL = TC // HC; TT = io["TT"]; NPR = TC // TT
        if halo:
            p.dma_dyn("act", hT[:, :, 0:2],
                      lambda en: hT_all[bass.DynSlice(rr_of(en) * D + (NCHL - 1) * NG * D, D), HC - 2:HC].rearrange("(k p) t -> p k t", p=128),
                      hT_all, "hT")
            for i in range(3):
                p.dma_dyn("sp", yT[i][:, :, 0:2],
                          lambda en, i=i: y_all[i][bass.DynSlice((rr_of(en) * NPR + NPR - 1) * D, D), TT - 2:TT].rearrange("(k p) t -> p k t", p=128),
                          y_all[i], f"yl{i}")
        else:
            for o_ in range(0, nb, HC):
                n_ = min(HC, nb - o_)
                ch_, col_ = divmod(lt0 + o_, HC)
                p.dma("act", hT[:, :, o_:o_ + n_], hT_loc[ch_ * D:(ch_ + 1) * D, col_:col_ + n_].rearrange("(k p) t -> p k t", p=128), "hT")
            tcl, colb = divmod(lt0, TT)
            for i in range(3):
                p.dma_dyn("sp", yT[i][:, :, 0:nb],
                          lambda en, i=i, tcl=tcl, colb=colb, nb=nb: y_all[i][bass.DynSlice((q_of(en) * NPR + tcl) * D, D), colb:colb + nb].rearrange("(k p) t -> p k t", p=128),
                          y_all[i], f"yl{i}")
        for oc2 in range(D // 256):
            for i in range(3):
                wb = load_panel(wg_d, i * D + oc2 * 256, 256)
                for sub in range(2):
                    ps = PS[sub * 3 + i]
                    for kc in range(KC):
                        p.mm(ps[:, 0:nb], wb[:, kc, sub * 128:(sub + 1) * 128], hT[:, kc, 0:nb],
                             start=(kc == 0), stop=(kc == KC - 1))
            for sub in range(2):
                for i in range(3):
                    p.act(gst[sub * 3 + i][:, 0:nb], PS[sub * 3 + i][:, 0:nb], AF.Sigmoid)
            for i in range(3):
                wb = load_panel(pj_d[i], oc2 * 256, 256)
                for sub in range(2):
                    ps = PS[sub * 3 + i]
                    for kc in range(KC):
                        p.mm(ps[:, 0:nb], wb[:, kc, sub * 128:(sub + 1) * 128], yT[i][:, kc, 0:nb],
                             start=(kc == 0), stop=(kc == KC - 1))
            for sub in range(2):
                oc = oc2 * 2 + sub
                a = gsb[0]
                b_ = gsb[1]
                p.tt(a[:, 0:nb], gst[sub * 3 + 0][:, 0:nb], PS[sub * 3 + 0][:, 0:nb], ALU.mult)
                p.tt(b_[:, 0:nb], gst[sub * 3 + 1][:, 0:nb], PS[sub * 3 + 1][:, 0:nb], ALU.mult)
                p.tt(a[:, 0:nb], a[:, 0:nb], b_[:, 0:nb], ALU.add)
                p.tt(b_[:, 0:nb], gst[sub * 3 + 2][:, 0:nb], PS[sub * 3 + 2][:, 0:nb], ALU.mult)
                p.tt(mT[:, oc, 0:nb], a[:, 0:nb], b_[:, 0:nb], ALU.add)
        for ti in range(ntile):
            tn = min(128, nb - ti * 128)
            if halo:
                p.dma_dyn("act", htm[ti][0:tn, :], lambda en: htail_all[bass.DynSlice(rr_of(en) * 2, 2), :], htail_all, f"htm{ti}")
            else:
                p.dma("act", htm[ti][0:tn, :], h_loc[lt0 + ti * 128:lt0 + ti * 128 + tn, :], f"htm{ti}")
        for cg2 in range(D // 256):
            wb = load_panel(wo_d, cg2 * 256, 256)
            for ti in range(ntile):
                tn = min(128, nb - ti * 128)
                ps = PS[ti]
                for kc in range(KC):
                    p.mm(ps[0:tn, 0:256], mT[:, kc, ti * 128:ti * 128 + tn], wb[:, kc, 0:256],
                         start=(kc == 0), stop=(kc == KC - 1))
                sl = slice(cg2 * 256, (cg2 + 1) * 256)
                p.stt(htm[ti][0:tn, sl], htm[ti][0:tn, sl], ALPHA, ps[0:tn, 0:256], ALU.mult, ALU.add)
        load_ln(0)
        for ti in range(ntile):
            tn = min(128, nb - ti * 128)
            layer_norm_tile(p, htm[ti], htm[ti], tn, lnt[0], lnt[1], st, mv, rstd)
            transpose_to_bf16(p, h1T, htm[ti], tn, ident, PS[6:8], ti * 128)
        for c2 in range(0, FC, 2):
            ncol = min(2, FC - c2) * 128
            wbs = [load_panel(wup_d, half * FFN + c2 * 128, ncol) for half in range(2)]
            for sub in range(ncol // 128):
                c = c2 + sub
                accs = []
                for half in range(2):
                    ch = half * FC + c
                    ps = PS[half * 2 + sub]
                    for kc in range(KC):
                        p.mm(ps[:, 0:nb], wbs[half][:, kc, sub * 128:(sub + 1) * 128], h1T[:, kc, 0:nb],
                             start=(kc == 0), stop=(kc == KC - 1))
                    ue = uext[half]
                    if halo:
                        p.ts(tails[:, ch, :], ps[:, 0:2], hmask[:, 0:1], None, op0=ALU.mult)
                        continue
                    p.copy(ue[:, 2:2 + nb], ps[:, 0:nb], eng="act")
                    p.copy(ue[:, 0:2], tails[:, ch, :], eng="act")
                    acc = gsb[half]
                    p.ts(acc[:, 0:nb], ue[:, 2:2 + nb], cw[:, ch, 2:3], cb[:, ch:ch + 1], op0=ALU.mult, op1=ALU.add)
                    p.stt(acc[:, 0:nb], ue[:, 1:1 + nb], cw[:, ch, 1:2], acc[:, 0:nb], ALU.mult, ALU.add)
                    p.stt(acc[:, 0:nb], ue[:, 0:nb], cw[:, ch, 0:1], acc[:, 0:nb], ALU.mult, ALU.add)
                    p.copy(tails[:, ch, :], ue[:, nb:nb + 2], eng="act")
                    accs.append(acc)
                if halo:
                    continue
                p.act(gsb[2][:, 0:nb], accs[0][:, 0:nb], AF.Gelu)
                p.tt(actT[:, c, 0:nb], gsb[2][:, 0:nb], accs[1][:, 0:nb], ALU.mult)
        if halo:
            continue
        for cg2 in range(D // 256):
            sl = slice(cg2 * 256, (cg2 + 1) * 256)
            for k0 in range(0, FC, KC):
                kn = min(KC, FC - k0)
                s = wcnt[0] % 2
                wcnt[0] += 1
                p.dma("sp", WS[s][:, 0:kn, 0:256],
                      wdn_d.rearrange("(k p) n -> p k n", p=128)[:, k0:k0 + kn, cg2 * 256:(cg2 + 1) * 256], f"ws{s}")
                p.copy(WB[s][:, 0:kn, 0:256], WS[s][:, 0:kn, 0:256], eng="act" if s == 0 else "dve")
                for ti in range(ntile):
                    tn = min(128, nb - ti * 128)
                    ps = PS[ti]
                    for k in range(kn):
                        p.mm(ps[0:tn, 0:256], actT[:, k0 + k, ti * 128:ti * 128 + tn], WB[s][:, k, 0:256],
                             start=(k0 + k == 0), stop=(k0 + k == FC - 1))
            for ti in range(ntile):
                tn = min(128, nb - ti * 128)
                p.stt(htm[ti][0:tn, sl], htm[ti][0:tn, sl], ALPHA, PS[ti][0:tn, 0:256], ALU.mult, ALU.add)
        load_ln(2)
        for ti in range(ntile):
            tn = min(128, nb - ti * 128)
            o = tmp[ti % 2]
            layer_norm_tile(p, o, htm[ti], tn, lnt[0], lnt[1], st, mv, rstd)
            tg = t0 - 2 + ti * 128
            p.dma("pool", ho_d[tg:tg + tn, :], o[0:tn, :], f"ho{ti % 2}")
            if (not last) and tg + tn == TC:
                p.dma("pool", htail_loc[0:2, :], o[tn - 2:tn, :], "htl")
            transpose_to_bf16(p, h1T, o, tn, ident, PS[6:8], ti * 128)
        if not last:
            for o_ in range(0, nb, HC):
                n_ = min(HC, nb - o_)
                ch_, col_ = divmod(lt0 + o_, HC)
                p.dma("pool", hT_loc[ch_ * D:(ch_ + 1) * D, col_:col_ + n_].rearrange("(k p) t -> p k t", p=128), h1T[:, :, o_:o_ + n_], "hTo")
                if io["overlap"] and col_ + n_ == HC:
                    io["xchg_h"](ch_)
            if io["overlap"] and lt0 + nb == TC:
                io["xchg_tail"]()


RTB = 256
CH = 32
NEG_E = -0.6065306597126334


def emit_rwkv(p, c, nc, T, ld_, PS, warena, hT_src, sfx, y_d):
    def din(name, shape, dt=F32):
        return nc.dram_tensor(name, list(shape), dt, kind="ExternalInput").ap()

    TB = RTB
    NW = 2048
    w_d = din("rw_w" + sfx, [D, NW])
    mu_d = din("rw_mu" + sfx, [128, 16])
    pc_d = din("rw_pc" + sfx, [128, 20])
    w2_d = din("rw_w2" + sfx, [96, 512]); a2_d = din("rw_a2" + sfx, [96, 512]); g2_d = din("rw_g2" + sfx, [256, 512])
    lnw_d = din("rw_lnw" + sfx, [128, 128]); lnb_d = din("rw_lnb" + sfx, [128, 128])

    w = warena[:, 0:KC * NW].rearrange("p (k n) -> p k n", k=KC)
    ld_.load(w_d, w, NW)
    mu = p.sb("r_mu", [128, 16]); pc = p.sb("r_pc", [128, 5, 4])
    w2 = p.sb("r_w2", [96, 512]); a2 = p.sb("r_a2", [96, 512])
    g2f = p.sb("r_g2f", [128, 2, 512]); g2 = p.sb("r_g2", [128, 2, 512], BF16)
    lnw = p.sb("r_lnw", [128, 2, 64]); lnb = p.sb("r_lnb", [128, 2, 64])
    p.dma("sp", mu[:], mu_d[:, :], "k0")
    p.dma("sp", pc[:].rearrange("p a b -> p (a b)"), pc_d[:, :], "k1")
    p.dma("sp", w2[:], w2_d[:, :], "k2")
    p.dma("sp", a2[:], a2_d[:, :], "k0")
    p.dma("sp", g2f[:], g2_d.rearrange("(k p) n -> p k n", p=128), "k1")
    p.dma("sp", lnw[:].rearrange("p a b -> p (a b)"), lnw_d[:, :], "k2")
    p.dma("sp", lnb[:].rearrange("p a b -> p (a b)"), lnb_d[:, :], "k0")
    p.copy(g2[:], g2f[:])

    def tri_mask(name, base):
        t = p.sb(name, [128, 4, 32])
        p.memset(t[:], 1.0, eng="pool")
        for rb in range(4):
            v = t[32 * rb:32 * rb + 32, :, :]
            p.op("pool", lambda en, v=v: en.affine_select(out=v, in_=v, pattern=[[0, 4], [1, 32]], compare_op=ALU.is_ge,
                                                          fill=0.0, base=base, channel_multiplier=-1), [v], [v])
        return t
    m_su = tri_mask("r_msu", -1)
    m_le = tri_mask("r_mle", 0)
    m_sl = p.sb("r_msl", [128, 4, 32])
    p.memset(m_sl[:], 1.0, eng="pool")
    for rb in range(4):
        v = m_sl[32 * rb:32 * rb + 32, :, :]
        p.op("pool", lambda en, v=v: en.affine_select(out=v, in_=v, pattern=[[0, 4], [-1, 32]], compare_op=ALU.is_ge,
                                                      fill=0.0, base=-1, channel_multiplier=1), [v], [v])
    m4 = p.sb("r_m4", [128, 4, 128])
    for i, m in enumerate((m_su, m_le, m_su, m_le)):
        p.copy(m4[:, i, :], m[:].rearrange("p a b -> p (a b)"), eng="pool")
    E2 = p.sb("r_E2", [128, 64])
    p.copy(E2[0:64, :], c["ident"][0:64, 0:64], eng="pool")
    p.copy(E2[64:128, :], c["ident"][64:128, 64:128], eng="pool")
    BO = p.sb("r_BO", [128, 128])
    p.memset(BO[:], 0.0, eng="pool")
    p.memset(BO[0:64, 0:64], 1.0, eng="pool")
    p.memset(BO[64:128, 64:128], 1.0, eng="pool")
    zb = p.sb("r_zb", [128, 512], BF16)
    p.memset(zb[:], 0.0, eng="pool")
    for i in range(8):
        p.mm(PS[i][:, 0:512], zb[:, 0:128], zb[:, 0:512])
    rmask = p.sb("r_rmask", [128, TB])
    p.memset(rmask[:], 1.0, eng="pool")
    p.memset(rmask[:].rearrange("p (a b) -> p a b", b=CH)[:, :, 0:1], 0.0, eng="pool")

    hT = [p.sb("r_hT0", [128, KC, TB], BF16)] * 2
    pe = [p.sb("r_pe0", [128, TB + 1])] * 2
    ptail = p.sb("r_ptail", [128, 16, 1])
    p.memset(ptail[:], 0.0)
    dtmp = p.sb("r_dtmp", [128, TB])
    tw = p.sb("r_tw", [128, TB]); alo = p.sb("r_alo", [128, TB]); sgb = p.sb("r_sgb", [128, 2, TB], BF16)
    rr = p.sb("r_rr", [128, TB]); kx = p.sb("r_kx", [128, TB]); vv = p.sb("r_vv", [128, TB])
    sg = p.sb("r_sg", [128, TB]); cs = p.sb("r_cs", [128, TB]); ecl = p.sb("r_ecl", [128, 4, TB])
    em = p.sb("r_em", [128, TB]); ecp = p.sb("r_ecp", [128, TB]); kk = p.sb("r_kk", [128, TB])
    t1 = p.sb("r_t1", [128, TB]); asg = p.sb("r_asg", [128, TB]); kh = p.sb("r_kh", [128, TB])
    NCK = TB // CH
    AR = [p.sb(f"r_AR{i}", [128, 2, NCK, 2, CH]) for i in range(4)]
    Bz = [p.sb(f"r_Bz{i}", [128, NCK, 2, CH]) for i in range(4)]
    Kz = [p.sb(f"r_Kz{i}", [128, NCK, 2, CH]) for i in range(4)]
    Vz = [p.sb(f"r_Vz{i}", [128, NCK, 2, CH]) for i in range(4)]
    RKz = [p.sb(f"r_RKz{i}", [128, NCK, 2, CH]) for i in range(4)]
    for lst in (AR, Bz, Kz, Vz, RKz):
        for t in lst:
            p.memset(t[:], 0.0, eng="pool")
    ST = p.sb("r_ST", [128, 4, 64])
    p.memset(ST[:], 0.0)
    gfm = p.sb("r_gfm", [64, 8, TB])
    ystage = [p.sb(f"r_ys{i}", [64, 8, TB], BF16) for i in range(2)]
    N4 = p.sb("r_N4", [128, 4, 128]); NT = p.sb("r_NT", [128, 128])
    Pk = [p.sb(f"r_P{i}", [128, 128]) for i in range(2)]; Qk = [p.sb(f"r_Q{i}", [128, 128]) for i in range(2)]
    Vst = p.sb("r_Vst", [128, 64]); Wk = [p.sb(f"r_W{i}", [128, 64]) for i in range(2)]
    KTx = [p.sb(f"r_KTs{i}", [128, 128]) for i in range(2)]; BTx = [p.sb(f"r_BTs{i}", [128, 128]) for i in range(2)]
    for t_ in KTx + BTx:
        p.memset(t_[:], 0.0, eng="pool")
    stmp = p.sb("r_stmp", [128, 2, 64])
    st6 = p.sb("r_st6", [128, 6]); mv = p.sb("r_mv", [128, 2]); rstd = p.sb("r_rstd", [128, 1])
    Pb16 = p.sb("r_Pb16", [128, 128], BF16); Qb16 = p.sb("r_Qb16", [128, 128], BF16)
    yn = p.sb("r_yn", [128, 64]); Ysb = p.sb("r_Ysb", [128, 64]); bsc = p.sb("r_bsc", [128, 2]); yfin = p.sb("r_yfin", [128, 64])

    w0 = lambda cc: pc[:, 0, cc:cc + 1]
    a0 = lambda cc: pc[:, 1, cc:cc + 1]
    k_k = lambda cc: pc[:, 2, cc:cc + 1]
    k_a = lambda cc: pc[:, 3, cc:cc + 1]
    r_k = lambda cc: pc[:, 4, cc:cc + 1]

    def proj_lerp(hb, ch, out):
        ps = PS[ch % 2]
        for kc in range(KC):
            p.mm(ps[:, 0:TB], w[:, kc, ch * 128:(ch + 1) * 128], hb[:, kc, :], start=(kc == 0), stop=(kc == KC - 1))
        x_ = pe[ch % 2]
        p.copy(x_[:, 1:1 + TB], ps[:, 0:TB], eng="act")
        p.copy(x_[:, 0:1], ptail[:, ch, :], eng="act")
        p.tt(dtmp[:], x_[:, 0:TB], x_[:, 1:1 + TB], ALU.subtract)
        p.stt(out, dtmp[:], mu[:, ch:ch + 1], x_[:, 1:1 + TB], ALU.mult, ALU.add)
        p.copy(ptail[:, ch, :], x_[:, TB:TB + 1], eng="act")

    nblk = T // TB
    H = slice(0, 64), slice(64, 128)
    for b in range(nblk):
        hb = hT[b % 2]
        for off_, ap_ in hT_src(b * TB, TB):
            p.dma("act", hb[:, :, off_:off_ + ap_.shape[2]], ap_, f"r_hT{b % 2}")
        proj_lerp(hb, 12, tw[:])
        p.act(tw[0:96, :], tw[0:96, :], AF.Tanh)
        proj_lerp(hb, 13, alo[:])
        for j in range(2):
            proj_lerp(hb, 14 + j, dtmp[:])
            p.act(sgb[:, j, :], dtmp[:], AF.Sigmoid)
        for hd in range(8):
            ps = PS[6 + hd % 2]
            for j in range(2):
                p.mm(ps[0:64, 0:TB], g2[:, j, hd * 64:(hd + 1) * 64], sgb[:, j, :], start=(j == 0), stop=(j == 1))
            p.copy(gfm[:, hd, :], ps[0:64, 0:TB], eng="act")
        for cc in range(4):
            proj_lerp(hb, cc, rr[:])
            proj_lerp(hb, 4 + cc, kx[:])
            proj_lerp(hb, 8 + cc, vv[:])
            p.mm(PS[6][:, 0:TB], w2[:, cc * 128:(cc + 1) * 128], tw[0:96, :])
            p.act(sg[:], PS[6][:, 0:TB], AF.Sigmoid, bias=w0(cc), scale=1.0)
            p.op("dve", lambda en: en.tensor_tensor_scan(out=cs[:], data0=rmask[:], data1=sg[:], initial=0.0,
                                                         op0=ALU.mult, op1=ALU.add), [rmask[:], sg[:]], [cs[:]])
            p.act(ecl[:, cc, :], cs[:], AF.Exp, scale=NEG_E)
            p.act(em[:], cs[:], AF.Exp, scale=-NEG_E)
            p.tt(t1[:], cs[:], sg[:], ALU.subtract)
            p.act(ecp[:], t1[:], AF.Exp, scale=NEG_E)
            p.mm(PS[7][:, 0:TB], a2[:, cc * 128:(cc + 1) * 128], alo[0:96, :])
            p.act(asg[:], PS[7][:, 0:TB], AF.Sigmoid, bias=a0(cc), scale=1.0)
            p.ts(kk[:], kx[:], k_k(cc), None, op0=ALU.mult)
            p.tt(t1[:], kk[:], kk[:], ALU.mult)
            p.mm(PS[6][:, 0:TB], BO[:], t1[:])
            p.ts(t1[:], PS[6][:, 0:TB], 1e-24, None, op0=ALU.max)
            p.act(t1[:], t1[:], AF.Sqrt)
            p.recip(t1[:], t1[:])
            p.tt(kk[:], kk[:], t1[:], ALU.mult)
            p.ts(t1[:], asg[:], -1.0, k_a(cc), op0=ALU.add, op1=ALU.mult)
            p.stt(kh[:], t1[:], 1.0, kx[:], ALU.add, ALU.mult)
            c3 = lambda ap: ap.rearrange("p (a b) -> p a b", b=CH)
            for h2 in range(2):
                hs = H[h2]
                p.stt(AR[cc][hs, 0, :, h2, :], c3(kk[hs, :]), -1.0, c3(ecp[hs, :]), ALU.mult, ALU.mult)
                p.tt(AR[cc][hs, 1, :, h2, :], c3(rr[hs, :]), c3(ecl[hs, cc, :]), ALU.mult)
                p.tt(t1[hs, :], kk[hs, :], asg[hs, :], ALU.mult)
                p.tt(Bz[cc][hs, :, h2, :], c3(t1[hs, :]), c3(em[hs, :]), ALU.mult)
                p.tt(Kz[cc][hs, :, h2, :], c3(kh[hs, :]), c3(em[hs, :]), ALU.mult)
                p.copy(Vz[cc][hs, :, h2, :], c3(vv[hs, :]), eng="act")
                p.stt(RKz[cc][hs, :, h2, :], c3(rr[hs, :]), r_k(cc)[hs, :], c3(kh[hs, :]), ALU.mult, ALU.mult)
        ysb = ystage[b % 2]
        import os
        STG = int(os.environ.get('RW_STAGE', '9'))
        for ci in range(TB // CH if STG > 0 else 0):
            tk = slice(ci * CH, (ci + 1) * CH)
            for g in range(2):
                cs2 = (2 * g, 2 * g + 1)
                for cl, cc in enumerate(cs2):
                    rhsAR = AR[cc][:, :, ci, :, :].rearrange("p m a b -> p m (a b)")
                    o1 = PS[2][64 * cl:64 * cl + 64, :].rearrange("p (m c x) -> p m c x", m=4, c=2)
                    p.mm(o1[:, 0:2, cl, :], Bz[cc][:, ci, :, :].rearrange("p a b -> p (a b)"), rhsAR)
                    p.mm(o1[:, 2:4, cl, :], Kz[cc][:, ci, :, :].rearrange("p a b -> p (a b)"), rhsAR)
                    p.mm(PS[3][64 * cl:64 * cl + 64, 64 * cl:64 * cl + 64], AR[cc][:, 0, ci, :, :].rearrange("p a b -> p (a b)"), Bz[cc][:, ci, :, :].rearrange("p a b -> p (a b)"))
                    p.mm(PS[4][64 * cl:64 * cl + 64, 0:64], Vz[cc][:, ci, :, :].rearrange("p a b -> p (a b)"), E2[:])
                p.tt(N4[:].rearrange("p a b -> p (a b)"), PS[2][:, 0:512], m4[:].rearrange("p a b -> p (a b)"), ALU.mult)
                p.tt(NT[:], PS[3][:, 0:128], m_sl[:].rearrange("p a b -> p (a b)"), ALU.mult)
                p.copy(Vst[:], PS[4][:, 0:64], eng="act")
                Nab, Nrb, Nak, Nrk = N4[:, 0, :], N4[:, 1, :], N4[:, 2, :], N4[:, 3, :]
                if STG < 2:
                    continue
                SUB = int(os.environ.get('RW_SUB', '9'))
                for cl, cc in enumerate(cs2):
                    hs = slice(64 * cl, 64 * cl + 64)
                    p.mm((PS[7][hs, 256:320] if SUB == 0 else PS[4][hs, 64:128]), AR[cc][:, 0, ci, :, :].rearrange("p a b -> p (a b)"), ST[:, cc, :])
                if SUB >= 2:
                    p.mm(PS[4][:, 320:384], Nak, Vst[:])
                if SUB >= 3:
                    p.copy(Wk[0][:], PS[4][:, 64:128], eng="act")
                if SUB >= 4:
                    p.tt(Wk[0][:], Wk[0][:], PS[4][:, 320:384], ALU.add)
                if STG < 3:
                    continue
                Pc, Qc = Nab, NT[:]
                wi = 0
                QPS = PS[7][:, 384:512] if os.environ.get('RW_QPS', '1') == '1' else PS[3][:, 256:384]
                RK = int(os.environ.get('RW_K', '5')); RQ = int(os.environ.get('RW_Q', '9'))
                for k in range(min(5, RK)):
                    p.mm(PS[5][:, 0:64], Pc, Wk[wi][:])
                    p.tt(Wk[1 - wi][:], Wk[wi][:], PS[5][:, 0:64], ALU.add)
                    wi = 1 - wi
                    if RQ < 1:
                        continue
                    if k < 4:
                        if os.environ.get('RW_BF', '0') == '1':
                            p.copy(Pb16[:], Pc); p.copy(Qb16[:], Qc, eng='pool')
                            Pm, Qm = Pb16[:], Qb16[:]
                        else:
                            Pm, Qm = Pc, Qc
                        p.mm(PS[3][:, 128:256], Qm, Pm)
                        if k < 3 and RQ != 7:
                            p.mm(QPS, Pm, Qm)
                        Pn = Pk[k % 2]
                        p.copy(Pn[:], PS[3][:, 128:256], eng="act")
                        if k < 3 and RQ != 7:
                            Qn = Qk[k % 2]
                            p.copy(Qn[:], QPS, eng=('act' if RQ == 8 else 'dve'))
                            Qc = Qn[:]
                        Pc = Pn[:]
                if STG < 4:
                    continue
                U = Wk[wi]
                for cl, cc in enumerate(cs2):
                    hs = slice(64 * cl, 64 * cl + 64)
                    p.mm(PS[4][hs, 128:192], AR[cc][:, 1, ci, :, :].rearrange("p a b -> p (a b)"), ST[:, cc, :])
                p.mm(PS[4][:, 384:448], Nrk, Vst[:])
                p.mm(PS[4][:, 448:512], Nrb, U[:])
                p.copy(Ysb[:], PS[4][:, 128:192], eng="act")
                p.tt(Ysb[:], Ysb[:], PS[4][:, 384:448], ALU.add)
                p.tt(Ysb[:], Ysb[:], PS[4][:, 448:512], ALU.add)
                for cl, cc in enumerate(cs2):
                    p.mm(PS[4][64 * cl:64 * cl + 64, 192:194], RKz[cc][:, ci, :, :].rearrange("p a b -> p (a b)"), c["ones"][:, 0:2])
                if STG < 5:
                    continue
                for cl, cc in enumerate(cs2):
                    p.mm(PS[5][64 * cl:64 * cl + 64, 128:256], Kz[cc][:, ci, :, :].rearrange("p a b -> p (a b)"), c["ident"][:])
                    p.mm(PS[7][64 * cl:64 * cl + 64, 256:384], Bz[cc][:, ci, :, :].rearrange("p a b -> p (a b)"), c["ident"][:])
                for cl in range(2):
                    hs = slice(64 * cl, 64 * cl + 64)
                    p.copy(KTx[cl][hs, :], PS[5][hs, 128:256], eng="act")
                    p.copy(BTx[cl][hs, :], PS[7][hs, 256:384])
                if STG < 6:
                    continue
                SUk = (PS[6][:, 256:320], PS[6][:, 320:384]); SUb = (PS[6][:, 384:448], PS[6][:, 448:512])
                for cl, cc in enumerate(cs2):
                    p.mm(SUk[cl], KTx[cl][:], Vst[:])
                    p.mm(SUb[cl], BTx[cl][:], U[:])
                if STG < 7:
                    continue
                p.op("dve", lambda en: en.bn_stats(out=st6[:], in_=Ysb[:]), [Ysb[:]], [st6[:]])
                p.op("dve", lambda en: en.bn_aggr(out=mv[:], in_=st6[:]), [st6[:]], [mv[:]])
                p.act(rstd[:], mv[:, 1:2], AF.Sqrt, bias=64e-5, scale=1.0)
                p.recip(rstd[:], rstd[:])
                p.ts(yn[:], Ysb[:], mv[:, 0:1], rstd[:, 0:1], op0=ALU.subtract, op1=ALU.mult)
                p.tt(yn[:], yn[:], lnw[:, g, :], ALU.mult)
                p.tt(yn[:], yn[:], lnb[:, g, :], ALU.add)
                p.copy(bsc[:], PS[4][:, 192:194], eng="act")
                p.stt(yfin[:], Vst[:], bsc[:, 0:1], yn[:], ALU.mult, ALU.add)
                p.tr(PS[5][0:64, 384:512], yfin[:], c["ident"][:])
                p.tt(ysb[:, 4 * g:4 * g + 4, tk], PS[5][0:64, 384:512].rearrange("p (h t) -> p h t", h=4),
                     gfm[:, 4 * g:4 * g + 4, tk], ALU.mult)
                for cl, cc in enumerate(cs2):
                    p.tt(stmp[:, cl, :], ST[:, cc, :], SUk[cl], ALU.add)
                    p.tt(stmp[:, cl, :], stmp[:, cl, :], SUb[cl], ALU.add)
                for cl, cc in enumerate(cs2):
                    te = ci * CH + CH - 1
                    p.ts(ST[:, cc, :], stmp[:, cl, :], ecl[:, cc, te:te + 1], None, op0=ALU.mult)
        p.dma("pool", y_d(b * TB, TB).rearrange("(h i) t -> i h t", i=64), ysb[:], f"r_yo{b % 2}")


TB = 512


def make_consts(p):
    c = {}
    c["ident"] = make_ident(p)
    for name, pat, cm, base in (("le", [[1, 128]], -1, 0), ("gt", [[-1, 128]], 1, -1)):
        t = p.sb("mask_" + name, [128, 128])
        p.memset(t[:], 1.0, eng="pool")
        p.op("pool", lambda en, t=t, pat=pat, cm=cm, base=base: en.affine_select(
            out=t[:], in_=t[:], pattern=pat, compare_op=ALU.is_ge, fill=0.0, base=base, channel_multiplier=cm),
            [t[:]], [t[:]])
        c[name] = t
    ones = p.sb("ones", [128, 128])
    p.memset(ones[:], 1.0, eng="pool")
    c["ones"] = ones
    return c


class Loader:
    def __init__(self, p):
        self.p = p
        self.WS = [p.sb(f"lws{i}", [128, KC, 128]) for i in range(2)]
        self.n = 0

    def load(self, wd, dst, N, kc=KC):
        p = self.p
        for n0 in range(0, N, 128):
            nn = min(128, N - n0)
            s = self.n % 2
            self.n += 1
            p.dma("sp", self.WS[s][:, 0:kc, 0:nn], wd.rearrange("(k p) n -> p k n", p=128)[:, :, n0:n0 + nn], f"lws{s}")
            p.copy(dst[:, :, n0:n0 + nn], self.WS[s][:, 0:kc, 0:nn], eng="act" if s == 0 else "dve")


def gla_chunk(p, c, g, QT, KT, Kt, V, ld, nh, dvh, Yout):
    nhp = max(nh, 2)
    W = nh * dvh
    PS = g["PS"]
    p.mm(PS[0][:, 0:128], KT, QT)
    p.tt(g["sTm"][:], PS[0][:, 0:128], c["le"][:], ALU.mult)
    for h in range(nh):
        p.act(g["rhs_all"][:, h, :], c["le"][:], AF.Identity, scale=ld[:, h:h + 1])
    for h0 in range(0, nh, 4):
        hn = min(4, nh - h0)
        ps = PS[1 + h0 // 4]
        p.mm(ps[:, 0:hn * 128], c["gt"][:], g["rhs_all"][:, h0:h0 + hn, :].rearrange("p h l -> p (h l)"))
        p.act(g["dec"][:, h0:h0 + hn, :].rearrange("p h l -> p (h l)"), ps[:, 0:hn * 128], AF.Exp)
    sm = PS[3]
    p.mm(sm[:, 0:nhp], c["le"][:], ld[:, 0:nhp])
    p.mm(sm[:, 8:8 + nhp], c["gt"][:], ld[:, 0:nhp])
    p.mm(sm[:, 16:16 + nhp], c["ones"][:], ld[:, 0:nhp])
    p.act(g["sme"][:, 0:24], sm[:, 0:24], AF.Exp)
    ea = g["sme"][:, 0:nh]
    dte = g["sme"][:, 8:8 + nh]
    cdb = g["sme"][:, 16:16 + nh]
    p.tt(g["MT"][:, 0:nh, :], g["dec"][:, 0:nh, :], g["sTm"][:].unsqueeze(1).broadcast_to([128, nh, 128]), ALU.mult)
    for h in range(nh):
        p.mm(PS[4][:, h * dvh:(h + 1) * dvh] if W <= 512 else PS[4][:, 0:dvh], g["MT"][:, h, :], V[:, h * dvh:(h + 1) * dvh])
    p.mm(PS[5][:, 0:W], QT, g["Sb"][:, 0:W])
    v3 = lambda ap: ap.rearrange("p (h d) -> p h d", h=nh)
    p.tt(v3(g["tmpY"][:, 0:W]), v3(PS[5][:, 0:W]), ea.unsqueeze(2).broadcast_to([128, nh, dvh]), ALU.mult)
    p.tt(Yout[:, 0:W], g["tmpY"][:, 0:W], PS[4][:, 0:W], ALU.add)
    p.tt(v3(g["Vw"][:, 0:W]), v3(V[:, 0:W]), dte.unsqueeze(2).broadcast_to([128, nh, dvh]), ALU.mult)
    p.mm(PS[6][:, 0:W], Kt, g["Vw"][:, 0:W])
    p.tt(v3(g["S"][:, 0:W]), v3(g["S"][:, 0:W]), cdb.unsqueeze(2).broadcast_to([128, nh, dvh]), ALU.mult)
    p.tt(g["S"][:, 0:W], g["S"][:, 0:W], PS[6][:, 0:W], ALU.add)
    p.copy(g["Sb"][:, 0:W], g["S"][:, 0:W], eng="act")


def gla_alloc(p, tag, nh, dvh, PS):
    W = nh * dvh
    g = {"PS": PS}
    g["sTm"] = p.dual(tag + "sTm", [128, 128])
    g["rhs_all"] = p.dual(tag + "rhs", [128, nh, 128])
    g["dec"] = p.dual(tag + "dec", [128, nh, 128])
    g["sme"] = p.dual(tag + "sme", [128, 24])
    g["MT"] = p.dual(tag + "MT", [128, nh, 128], BF16)
    g["tmpY"] = p.dual(tag + "tmpY", [128, W])
    g["Vw"] = p.dual(tag + "Vw", [128, W], BF16)
    g["S"] = p.sb(tag + "S", [128, W])
    g["Sb"] = p.sb(tag + "Sb", [128, W], BF16)
    p.memset(g["S"][:], 0.0)
    p.memset(g["Sb"][:], 0.0)
    return g


def conv_fm(p, out_acc, xe, nb, cwt, cbt, ch, ntap):
    p.ts(out_acc[:, 0:nb], xe[:, ntap - 1:ntap - 1 + nb], cwt[:, ch, ntap - 1:ntap], cbt[:, ch:ch + 1], op0=ALU.mult, op1=ALU.add)
    for j in range(ntap - 2, -1, -1):
        p.stt(out_acc[:, 0:nb], xe[:, j:j + nb], cwt[:, ch, j:j + 1], out_acc[:, 0:nb], ALU.mult, ALU.add)


def emit_ssd(p, c, nc, T, ld_, PS, warena, hT_src, sfx, y_d):
    def din(name, shape, dt=F32):
        return nc.dram_tensor(name, list(shape), dt, kind="ExternalInput").ap()

    NW = 1288
    w_d = din("ssd_w" + sfx, [D, NW])
    cw_d = din("ssd_cw" + sfx, [128, 6 * 4])
    cb_d = din("ssd_cb" + sfx, [128, 6])
    vec_d = din("ssd_vec" + sfx, [1, 8 + 8 + 512 + 512])

    w = warena[:, 0:KC * NW].rearrange("p (k n) -> p k n", k=KC)
    ld_.load(w_d, w, NW)
    cw = p.sb("s_cw", [128, 6, 4]); cb = p.sb("s_cb", [128, 6])
    p.dma("sp", cw[:].rearrange("p c j -> p (c j)"), cw_d[:, :], "k0")
    p.dma("sp", cb[:], cb_d[:, :], "k1")
    vec = p.sb("s_vec", [128, 1040])
    p.dma("sp", vec[:], vec_d[0:1, :].partition_broadcast(128), "k2")
    dtb = vec[:, 0:8]; Dx = vec[:, 16:528]; nw = vec[:, 528:1040]
    aneg = p.sb("s_aneg", [128, 8])
    p.act(aneg[:], vec[:, 8:16], AF.Exp)
    p.ts(aneg[:], aneg[:], -1.0, None, op0=ALU.mult)

    hT = [p.sb(f"s_hT{i}", [128, KC, TB], BF16) for i in range(2)]
    xe = [p.sb(f"s_xe{i}", [128, TB + 3]) for i in range(2)]
    tails = p.sb("s_tails", [128, 6, 3])
    p.memset(tails[:], 0.0)
    acc = p.sb("s_acc", [128, TB])
    xs = p.sb("s_xs", [128, 4, TB])
    Bf = p.sb("s_Bf", [128, TB]); Bb = p.sb("s_Bb", [128, TB], BF16); Cb = p.sb("s_Cb", [128, TB], BF16)
    zs = p.dual("s_zs", [128, 512]); xtm = p.dual("s_xtm", [128, 512]); V = p.dual("s_V", [128, 512], BF16)
    Kt = p.dual("s_Kt", [128, 128], BF16)
    dt = p.dual("s_dt", [128, 8]); ld = p.dual("s_ld", [128, 8])
    Y = p.dual("s_Y", [128, 512]); yo = [p.sb(f"s_yo{i}", [128, 512]) for i in range(2)]
    ss = p.dual("s_ss", [128, 1]); junk = p.dual("s_junk", [128, 512])
    yob = [p.sb(f"s_yob{i}", [128, 4, 128], BF16) for i in range(2)]
    g = gla_alloc(p, "sg_", 8, 64, PS)

    nblk = T // TB

    def load_hT_(bb):
        for off_, ap_ in hT_src(bb * TB, TB):
            p.dma("sp", hT[bb % 2][:, :, off_:off_ + ap_.shape[2]], ap_, f"s_hT{bb % 2}")

    load_hT_(0)
    for b in range(nblk):
        hb = hT[b % 2]
        if b + 1 < nblk:
            load_hT_(b + 1)
        for ch in range(6):
            ps = PS[ch % 2]
            for kc in range(KC):
                p.mm(ps[:, 0:TB], w[:, kc, ch * 128:(ch + 1) * 128], hb[:, kc, :], start=(kc == 0), stop=(kc == KC - 1))
            x_ = xe[ch % 2]
            p.copy(x_[:, 3:3 + TB], ps[:, 0:TB], eng="act")
            p.copy(x_[:, 0:3], tails[:, ch, :], eng="act")
            conv_fm(p, acc, x_, TB, cw, cb, ch, 4)
            p.copy(tails[:, ch, :], x_[:, TB:TB + 3], eng="act")
            if ch < 4:
                p.act(xs[:, ch, :], acc[:], AF.Silu)
            elif ch == 4:
                p.act(Bf[:], acc[:], AF.Silu)
                p.copy(Bb[:], Bf[:], eng="act")
            else:
                p.act(Cb[:], acc[:], AF.Silu)
        for ci in range(TB // 128):
            tk = slice(ci * 128, (ci + 1) * 128)
            p.set_par(ci % 2)
            for kc in range(KC):
                p.mm(PS[7][:, 0:512], hb[:, kc, tk], w[:, kc, 768:1280], start=(kc == 0), stop=(kc == KC - 1))
            p.act(zs[:], PS[7][:, 0:512], AF.Silu)
            for kc in range(KC):
                p.mm(PS[3][:, 32:40], hb[:, kc, tk], w[:, kc, 1280:1288], start=(kc == 0), stop=(kc == KC - 1))
            p.tt(dt[:], PS[3][:, 32:40], dtb, ALU.add)
            p.act(dt[:], dt[:], AF.Exp)
            p.act(dt[:], dt[:], AF.Ln, bias=1.0, scale=1.0)
            p.tt(ld[:], dt[:], aneg[:], ALU.mult)
            for j in range(4):
                p.tr(PS[7][:, j * 128:(j + 1) * 128], xs[:, j, tk], c["ident"][:])
            p.copy(xtm[:], PS[7][:, 0:512], eng="act")
            p.tt(V[:].rearrange("p (h d) -> p h d", h=8), xtm[:].rearrange("p (h d) -> p h d", h=8),
                 dt[:].unsqueeze(2).broadcast_to([128, 8, 64]), ALU.mult)
            p.tr(PS[0][:, 128:256], Bf[:, tk], c["ident"][:])
            p.copy(Kt[:], PS[0][:, 128:256], eng="act")
            gla_chunk(p, c, g, Cb[:, tk], Bb[:, tk], Kt[:], V, ld, 8, 64, Y)
            p.tt(xtm[:], xtm[:], Dx, ALU.mult)
            p.tt(Y[:], Y[:], xtm[:], ALU.add)
            p.tt(Y[:], Y[:], zs[:], ALU.mult)
            p.act(junk[:], Y[:], AF.Square, accum_out=ss[:])
            p.act(ss[:], ss[:], AF.Sqrt, bias=1e-5, scale=1.0 / 512)
            p.recip(ss[:], ss[:])
            o = yo[ci % 2]
            p.stt(o[:], Y[:], ss[:, 0:1], nw, ALU.mult, ALU.mult)
            t0 = b * TB + ci * 128
            ob = yob[ci % 2]
            for j in range(4):
                p.tr(PS[7][:, j * 128:(j + 1) * 128], o[:, j * 128:(j + 1) * 128], c["ident"][:])
            p.copy(ob[:], PS[7][:, 0:512].rearrange("p (a b) -> p a b", a=4), eng="act")
            p.dma("pool", y_d(t0, 128).rearrange("(c p) t -> p c t", p=128), ob[:], f"s_yo{ci % 2}")


def emit_mlstm(p, c, nc, T, ld_, PS, warena, hT_src, sfx, y_d):
    def din(name, shape, dt=F32):
        return nc.dram_tensor(name, list(shape), dt, kind="ExternalInput").ap()

    NW = 1540
    w_d = din("ml_w" + sfx, [D, NW])
    cw_d = din("ml_cw" + sfx, [128, 4 * 4])
    cb_d = din("ml_cb" + sfx, [128, 4])
    vec_d = din("ml_vec" + sfx, [1, 2 + 2 + 512])
    w = warena[:, 0:KC * NW].rearrange("p (k n) -> p k n", k=KC)
    ld_.load(w_d, w, NW)
    cw = p.sb("m_cw", [128, 4, 4]); cb = p.sb("m_cb", [128, 4])
    p.dma("sp", cw[:].rearrange("p c j -> p (c j)"), cw_d[:, :], "k0")
    p.dma("sp", cb[:], cb_d[:, :], "k1")
    vec = p.sb("m_vec", [128, 516])
    p.dma("sp", vec[:], vec_d[0:1, :].partition_broadcast(128), "k2")
    nw = vec[:, 4:516]

    hT = [p.sb(f"m_hT{i}", [128, KC, TB], BF16) for i in range(2)]
    xe = [p.sb(f"m_xe{i}", [128, TB + 3]) for i in range(2)]
    tails = p.sb("m_tails", [128, 4, 3])
    p.memset(tails[:], 0.0)
    acc = p.sb("m_acc", [128, TB])
    qb = p.sb("m_qb", [128, 2, TB], BF16); kf = p.sb("m_kf", [128, 2, TB]); kb = p.sb("m_kb", [128, 2, TB], BF16)
    vtm = p.dual("m_vtm", [128, 512]); osg = p.dual("m_osg", [128, 512])
    Vp = [p.dual(f"m_Vp{i}", [128, 257], BF16) for i in range(2)]
    Kt = [p.dual(f"m_Kt{i}", [128, 128], BF16) for i in range(2)]
    gi = p.dual("m_gi", [128, 4]); ei = p.dual("m_ei", [128, 2]); lf = p.dual("m_lf", [128, 2, 2])
    Y = [p.dual(f"m_Y{i}", [128, 257]) for i in range(2)]; hh = p.dual("m_h", [128, 512]); yo = [p.sb(f"m_yo{i}", [128, 512]) for i in range(2)]
    yob = [p.sb(f"m_yob{i}", [128, 4, 128], BF16) for i in range(2)]
    den = [p.dual(f"m_den{i}", [128, 1]) for i in range(2)]; ss = [p.dual(f"m_ss{i}", [128, 1]) for i in range(2)]; junk = [p.dual(f"m_junk{i}", [128, 256]) for i in range(2)]
    gs = [gla_alloc(p, f"mg{i}_", 1, 257, PS) for i in range(2)]

    nblk = T // TB

    def load_hT_(bb):
        for off_, ap_ in hT_src(bb * TB, TB):
            p.dma("sp", hT[bb % 2][:, :, off_:off_ + ap_.shape[2]], ap_, f"m_hT{bb % 2}")

    load_hT_(0)
    for b in range(nblk):
        hb = hT[b % 2]
        if b + 1 < nblk:
            load_hT_(b + 1)
        for ch in range(4):
            ps = PS[ch % 2]
            for kc in range(KC):
                p.mm(ps[:, 0:TB], w[:, kc, ch * 128:(ch + 1) * 128], hb[:, kc, :], start=(kc == 0), stop=(kc == KC - 1))
            x_ = xe[ch % 2]
            p.copy(x_[:, 3:3 + TB], ps[:, 0:TB], eng="act")
            p.copy(x_[:, 0:3], tails[:, ch, :], eng="act")
            conv_fm(p, acc, x_, TB, cw, cb, ch, 4)
            p.copy(tails[:, ch, :], x_[:, TB:TB + 3], eng="act")
            if ch < 2:
                p.act(acc[:], acc[:], AF.Silu)
                p.ts(qb[:, ch, :], acc[:], 128.0 ** -0.5, None, op0=ALU.mult)
            else:
                p.act(kf[:, ch - 2, :], acc[:], AF.Silu)
                p.copy(kb[:, ch - 2, :], kf[:, ch - 2, :], eng="act")
        for ci in range(TB // 128):
            tk = slice(ci * 128, (ci + 1) * 128)
            p.set_par(ci % 2)
            for kc in range(KC):
                p.mm(PS[7][:, 0:512], hb[:, kc, tk], w[:, kc, 512:1024], start=(kc == 0), stop=(kc == KC - 1))
            p.copy(vtm[:], PS[7][:, 0:512], eng="act")
            for kc in range(KC):
                p.mm(PS[7][:, 0:512], hb[:, kc, tk], w[:, kc, 1024:1536], start=(kc == 0), stop=(kc == KC - 1))
            p.act(osg[:], PS[7][:, 0:512], AF.Sigmoid)
            for kc in range(KC):
                p.mm(PS[3][:, 32:36], hb[:, kc, tk], w[:, kc, 1536:1540], start=(kc == 0), stop=(kc == KC - 1))
            p.tt(gi[:], PS[3][:, 32:36], vec[:, 0:4], ALU.add)
            p.act(ei[:], gi[:, 0:2], AF.Exp)
            p.act(gi[:, 2:4], gi[:, 2:4], AF.Exp, scale=-1.0)
            p.act(gi[:, 2:4], gi[:, 2:4], AF.Ln, bias=1.0, scale=1.0)
            for hd in range(2):
                p.ts(lf[:, hd, :], gi[:, 2 + hd:3 + hd].to_broadcast([128, 2]), -1.0, None, op0=ALU.mult)
            t0 = b * TB + ci * 128
            for hd in range(2):
                vp = Vp[hd]; Kt_ = Kt[hd]; Y_ = Y[hd]; den_ = den[hd]; ss_ = ss[hd]; junk_ = junk[hd]
                p.ts(vp[:, 0:256], vtm[:, hd * 256:(hd + 1) * 256], ei[:, hd:hd + 1], None, op0=ALU.mult)
                p.copy(vp[:, 256:257], ei[:, hd:hd + 1], eng="act")
                p.tr(PS[0][:, 128:256], kf[:, hd, tk], c["ident"][:])
                p.copy(Kt_[:], PS[0][:, 128:256], eng="act")
                gla_chunk(p, c, gs[hd], qb[:, hd, tk], kb[:, hd, tk], Kt_[:], vp, lf[:, hd, :], 1, 257, Y_)
                p.ts(den_[:], Y_[:, 256:257], -1.0, 1.0, op0=ALU.mult, op1=ALU.max)
                p.ts(junk_[:, 0:1], Y_[:, 256:257], 1.0, None, op0=ALU.max)
                p.tt(den_[:], den_[:], junk_[:, 0:1], ALU.max)
                p.recip(den_[:], den_[:])
                hv = hh[:, hd * 256:(hd + 1) * 256]
                p.ts(hv, Y_[:, 0:256], den_[:, 0:1], None, op0=ALU.mult)
                p.act(junk_[:], hv, AF.Square, accum_out=ss_[:])
                p.act(ss_[:], ss_[:], AF.Sqrt, bias=1e-6, scale=1.0 / 256)
                p.recip(ss_[:], ss_[:])
                p.ts(hv, hv, ss_[:, 0:1], None, op0=ALU.mult)
            o = yo[ci % 2]
            p.tt(o[:], hh[:], nw, ALU.mult)
            p.tt(o[:], o[:], osg[:], ALU.mult)
            ob = yob[ci % 2]
            for j in range(4):
                p.tr(PS[7][:, j * 128:(j + 1) * 128], o[:, j * 128:(j + 1) * 128], c["ident"][:])
            p.copy(ob[:], PS[7][:, 0:512].rearrange("p (a b) -> p a b", a=4), eng="act")
            p.dma("pool", y_d(t0, 128).rearrange("(c p) t -> p c t", p=128), ob[:], f"m_yo{ci % 2}")


D = 2048
FFN = 5504
FC = 43
O_Z = 0
O_XBC = 2048
O_DT = 5120
O_RW = 5152
O_QK = 11744
O_V = 13792
O_O = 15840
O_I = 17888
O_F = 17896
O_G = 17904


def fm(vec, nch):
    return np.ascontiguousarray(np.asarray(vec).reshape(nch, 128).T)


def fm_taps(w, nch):
    ntap = w.shape[0]
    return np.ascontiguousarray(np.asarray(w).reshape(ntap, nch, 128).transpose(2, 1, 0).reshape(128, nch * ntap))


def prep_A(I, l, q):
    w_in = I["w_in"][l]
    r = {}
    sl = lambda o, n: np.arange(o, o + n)
    cols = np.concatenate([sl(O_XBC + q * 512, 512), sl(O_XBC + 2048 + q * 128, 128), sl(O_XBC + 2560 + q * 128, 128),
                           sl(O_Z + q * 512, 512), sl(O_DT + q * 8, 8)])
    r["ssd_w"] = np.ascontiguousarray(w_in[:, cols])
    cch = np.concatenate([sl(q * 512, 512), sl(2048 + q * 128, 128), sl(2560 + q * 128, 128)])
    r["ssd_cw"] = fm_taps(I["ssd_conv_w"][l][:, cch], 6)
    r["ssd_cb"] = fm(I["ssd_conv_b"][l][cch], 6)
    hs = slice(q * 8, q * 8 + 8)
    r["ssd_vec"] = np.concatenate([I["ssd_dt_bias"][l][hs], I["ssd_a_log"][l][hs], np.repeat(I["ssd_d"][l][hs], 64),
                                   I["ssd_norm_w"][l][q * 512:(q + 1) * 512]]).astype(np.float32)[None, :]
    cols = np.concatenate([sl(O_QK + q * 256, 256), sl(O_QK + 1024 + q * 256, 256), sl(O_V + q * 512, 512),
                           sl(O_O + q * 512, 512), sl(O_I + 2 * q, 2), sl(O_F + 2 * q, 2)])
    r["ml_w"] = np.ascontiguousarray(w_in[:, cols])
    cch = np.concatenate([sl(q * 256, 256), sl(1024 + q * 256, 256)])
    r["ml_cw"] = fm_taps(I["mlstm_conv_w"][l][:, cch], 4)
    r["ml_cb"] = fm(I["mlstm_conv_b"][l][cch], 4)
    r["ml_vec"] = np.concatenate([I["mlstm_i_bias"][l][2 * q:2 * q + 2], I["mlstm_f_bias"][l][2 * q:2 * q + 2],
                                  I["mlstm_norm_w"][l][q * 512:(q + 1) * 512]]).astype(np.float32)[None, :]
    cs = slice(q * 512, (q + 1) * 512)
    z32 = np.zeros((D, 32), np.float32)
    rw = w_in[:, O_RW:O_RW + 6592]
    r["rw_w"] = np.ascontiguousarray(np.concatenate([rw[:, q * 512:(q + 1) * 512], rw[:, 2048 + q * 512:2048 + (q + 1) * 512],
                                                      rw[:, 4096 + q * 512:4096 + (q + 1) * 512], rw[:, 6144:6240], z32,
                                                      rw[:, 6240:6336], z32, rw[:, 6336:6592]], axis=1))
    mu = I["rwkv_mu"][l]
    zz = np.zeros(32, np.float32)
    mu_c = np.concatenate([mu[q * 512:(q + 1) * 512], mu[2048 + q * 512:2048 + (q + 1) * 512],
                           mu[4096 + q * 512:4096 + (q + 1) * 512], mu[6144:6240], zz, mu[6240:6336], zz, mu[6336:6592]])
    r["rw_mu"] = fm(mu_c, 16)
    pc = np.stack([fm(I["rwkv_w0"][l][cs], 4), fm(I["rwkv_a0"][l][cs], 4), fm(I["rwkv_k_k"][l][cs], 4),
                   fm(I["rwkv_k_a"][l][cs], 4), fm(I["rwkv_r_k"][l].reshape(-1)[cs], 4)], axis=1)
    r["rw_pc"] = np.ascontiguousarray(pc.reshape(128, 20))
    r["rw_w2"] = np.ascontiguousarray(I["rwkv_w2"][l][:, cs])
    r["rw_a2"] = np.ascontiguousarray(I["rwkv_a2"][l][:, cs])
    r["rw_g2"] = np.ascontiguousarray(I["rwkv_g2"][l][:, cs])
    lw = I["rwkv_ln_w"][l][cs].reshape(2, 4, 1, 64)
    lb = I["rwkv_ln_b"][l][cs].reshape(2, 4, 1, 64)
    r["rw_lnw"] = np.ascontiguousarray(np.broadcast_to(lw, (2, 4, 32, 64)).reshape(2, 128, 64).transpose(1, 0, 2).reshape(128, 128))
    r["rw_lnb"] = np.ascontiguousarray(np.broadcast_to(lb, (2, 4, 32, 64)).reshape(2, 128, 64).transpose(1, 0, 2).reshape(128, 128))
    return r


def prep_B(I, l):
    r = {}
    r["wg"] = np.ascontiguousarray(I["w_in"][l][:, O_G:O_G + 3 * D])
    r["p_ssd"] = I["proj_ssd"][l]; r["p_rwkv"] = I["proj_rwkv"][l]; r["p_ml"] = I["proj_mlstm"][l]
    r["w_out"] = I["w_out"][l]
    r["ln1_g"] = I["ln1_g"][l][None, :]; r["ln1_b"] = I["ln1_b"][l][None, :]
    r["ln2_g"] = I["ln2_g"][l][None, :]; r["ln2_b"] = I["ln2_b"][l][None, :]
    r["w_up"] = I["ffn_w_up"][l]
    r["cw"] = fm_taps(I["ffn_conv_w"][l], 2 * FC)
    r["cb"] = fm(I["ffn_conv_b"][l], 2 * FC)
    r["w_down"] = I["ffn_w_down"][l]
    return r


NG = 4
GROUPS = [[0, 1, 2, 3], [4, 5, 6, 7]]
OVERLAP_CC = False


def emit_L(p, nc, ident, PS, TC, io):
    x = nc.dram_tensor("x", [TC, D], F32, kind="ExternalInput").ap()
    g = nc.dram_tensor("ln_in_g", [1, D], F32, kind="ExternalInput").ap()
    b = nc.dram_tensor("ln_in_b", [1, D], F32, kind="ExternalInput").ap()
    h_loc, hT_loc, htail_loc = io["h_loc"], io["hT_loc"], io["htail_loc"]
    gt = p.sb("gt", [128, D]); bt = p.sb("bt", [128, D])
    p.dma("sp", gt[:], g[0:1, :].partition_broadcast(128), "c0")
    p.dma("sp", bt[:], b[0:1, :].partition_broadcast(128), "c1")
    xt = [p.sb(f"xt{i}", [128, D]) for i in range(2)]
    ht = [p.sb(f"ht{i}", [128, D]) for i in range(2)]
    hTt = [p.sb(f"hTt{i}", [128, KC, 128], BF16) for i in range(2)]
    st = p.sb("st", [128, 4, 6]); mv = p.sb("mv", [128, 2]); rstd = p.sb("rstd", [128, 1])
    nt = TC // 128
    for i in range(nt):
        s = i % 2
        p.dma("sp", xt[s][:], x[i * 128:(i + 1) * 128, :], f"x{s}")
        layer_norm_tile(p, ht[s], xt[s], 128, gt, bt, st, mv, rstd)
        p.dma("pool", h_loc[i * 128:(i + 1) * 128, :], ht[s][:], f"ho{s}")
        if i == nt - 1:
            p.dma("pool", htail_loc[0:2, :], ht[s][126:128, :], "htl")
        transpose_to_bf16(p, hTt[s], ht[s], 128, ident, PS[6:8], 0)
        ch_, col_ = divmod(i * 128, io["HC"])
        p.dma("pool", hT_loc[ch_ * D:(ch_ + 1) * D, col_:col_ + 128].rearrange("(k p) t -> p k t", p=128), hTt[s][:], f"hTo{s}")
        if io["overlap"] and col_ + 128 == io["HC"]:
            io["xchg_h"](ch_)
        if io["overlap"] and i == nt - 1:
            io["xchg_tail"]()


def build_fused(T, TC, depth=2):
    nc = bass.Bass("TRN2", target_bir_lowering=False)

    def internal(name, shape, dt, local=False):
        if local:
            return nc.dram_tensor(name, list(shape), dt, kind="Internal", addr_space="Local").ap()
        return nc.dram_tensor(name, list(shape), dt, kind="Internal").ap()

    io = {}
    io["h_loc"] = internal("h_loc", [TC, D], F32)
    HC = 256
    TT = min(1024, TC)
    NCHL = TC // HC
    NTC = T // TT
    io["HC"] = HC; io["TT"] = TT
    io["hT_loc"] = internal("hT_loc", [NCHL * D, HC], BF16)
    io["htail_loc"] = internal("htail_loc", [2, D], F32)
    io["hT_all"] = internal("hT_all", [NCHL * NG * D, HC], BF16, True)
    io["htail_all"] = internal("htail_all", [NG * 2, D], F32, True)
    y_loc = [internal(f"y_loc{i}", [NTC * 512, TT], BF16) for i in range(3)]
    io["y_all"] = [internal(f"y_all{i}", [NTC * NG * 512, TT], BF16, True) for i in range(3)]
    io["hmask"] = nc.dram_tensor("hmask", [128, 1], F32, kind="ExternalInput").ap()
    io["out"] = nc.dram_tensor("out", [TC, D], F32, kind="ExternalOutput").ap()
    hT_all = io["hT_all"]

    def hT_src(t0, n):
        r, lt = divmod(t0, TC)
        pieces = []
        for o_ in range(0, n, HC):
            n_ = min(HC, n - o_)
            ch_, col_ = divmod(lt + o_, HC)
            row0 = (ch_ * NG + r) * D
            pieces.append((o_, hT_all[row0:row0 + D, col_:col_ + n_].rearrange("(k p) t -> p k t", p=128)))
        return pieces

    def y_ap(i):
        def f_(t0, n):
            tc_, col_ = divmod(t0, TT)
            return y_loc[i][tc_ * 512:(tc_ + 1) * 512, col_:col_ + n]
        return f_

    def xchg_h(ch_):
        p.cc("AllGather", io["hT_loc"][ch_ * D:(ch_ + 1) * D, :], hT_all[ch_ * NG * D:(ch_ + 1) * NG * D, :], GROUPS)

    def xchg_tail():
        p.cc("AllGather", io["htail_loc"], io["htail_all"], GROUPS)

    io["xchg_h"] = xchg_h
    io["overlap"] = OVERLAP_CC
    io["xchg_tail"] = xchg_tail

    def exchange_y(i):
        for tc_ in range(NTC):
            p.cc("AllGather", y_loc[i][tc_ * 512:(tc_ + 1) * 512, :], io["y_all"][i][tc_ * NG * 512:(tc_ + 1) * NG * 512, :], GROUPS)

    p = Prog(nc)
    c = make_consts(p)
    PS = [p.ps(f"ps{i}", [128, 512]) for i in range(8)]
    m_ = p.mark()
    p.prefix = "L_"
    emit_L(p, nc, c["ident"], PS, TC, io)
    p.release(m_)
    for l in range(depth):
        sfx = f"_{l}"
        if not OVERLAP_CC:
            for ch_ in range(NCHL):
                xchg_h(ch_)
            xchg_tail()
        m_ = p.mark()
        p.prefix = f"A{l}_"
        warena = p.sb("warena", [128, KC * 2048], BF16)
        ld_ = Loader(p)
        m2 = p.mark()
        emit_ssd(p, c, nc, T, ld_, PS, warena, hT_src, sfx, y_ap(0))
        p.release(m2)
        if OVERLAP_CC:
            exchange_y(0)
        m2 = p.mark()
        emit_mlstm(p, c, nc, T, ld_, PS, warena, hT_src, sfx, y_ap(2))
        p.release(m2)
        if OVERLAP_CC:
            exchange_y(2)
        m2 = p.mark()
        emit_rwkv(p, c, nc, T, ld_, PS, warena, hT_src, sfx, y_ap(1))
        p.release(m_)
        for i_ in ((1,) if OVERLAP_CC else (0, 2, 1)):
            exchange_y(i_)
        m_ = p.mark()
        p.prefix = f"B{l}_"
        emit_B(p, nc, c["ident"], PS, TC, sfx, io, last=(l == depth - 1))
        p.release(m_)
    p.wait_all("pool")
    p.emit()
    return nc


def fused_inputs(I, NB, T, TC):
    QN = T // TC
    x = I["x"].astype(np.float32)
    maps = []
    for cidx in range(NB * QN):
        b, q = divmod(cidx, QN)
        m = {"x": np.ascontiguousarray(x[b, q * TC:(q + 1) * TC, :]),
             "ln_in_g": np.ascontiguousarray(I["ln_in_g"][None, :]), "ln_in_b": np.ascontiguousarray(I["ln_in_b"][None, :]),
             "hmask": np.full((128, 1), 0.0 if q == 0 else 1.0, np.float32)}
        for l in range(2):
            for k, v in prep_A(I, l, q).items():
                m[f"{k}_{l}"] = v
            for k, v in prep_B(I, l).items():
                m[f"{k}_{l}"] = np.ascontiguousarray(v)
        maps.append(m)
    return maps


def kernel(**inputs):
    I = {k: np.asarray(v) for k, v in inputs.items()}
    NB, T, _ = I["x"].shape
    QN = 4
    TC = T // QN
    nc = build_fused(T, TC)
    maps = fused_inputs(I, NB, T, TC)
    res = run_bass_kernel_spmd(nc, maps, core_ids=list(range(NB * QN))).results
    out = np.stack([np.asarray(r["out"]) for r in res]).reshape(NB, T, D)
    return np.ascontiguousarray(out.astype(np.float32))
```
